# Optimizing a Trainium2 kernel written in Bass

```python
import math, functools
import jax, jax.numpy as jnp
from jax import lax
import numpy as np

D_MODEL = 2048
BATCH = 4
SEQ = 2048
DEPTH = 1
DEC_BATCH = 128
DEC_SEQ = 1
PAST_LEN = 16384
PAGE_SIZE = 128

N_META = 16
D_SSD = D_MODEL
SSD_HEADDIM = 64
SSD_HEADS = D_SSD // SSD_HEADDIM
SSD_GROUPS = 4
SSD_STATE = 128
SSD_CONV = 4
D_SC = D_MODEL
SC_CONV = 3
CHUNK = 128
EPS = 1e-5
D_GN = SSD_GROUPS * SSD_STATE
D_XBC = D_SSD + 2 * D_GN
D_MIX = D_SSD + D_SC
SPLITS = (D_SSD, D_SSD + D_XBC, D_SSD + D_XBC + SSD_HEADS,
          D_SSD + D_XBC + SSD_HEADS + D_SC, D_SSD + D_XBC + SSD_HEADS + 2 * D_SC,
          D_SSD + D_XBC + SSD_HEADS + 3 * D_SC)
D_IN_PROJ = D_SSD + D_XBC + SSD_HEADS + 4 * D_SC

kernel_name = "hymba_ssd_shortconv_step"


def rmsnorm(x, g):
    xf = x.astype(jnp.float32)
    y = xf * lax.rsqrt(jnp.mean(xf * xf, axis=-1, keepdims=True) + EPS)
    return (y * g.astype(jnp.float32)).astype(x.dtype)


def causal_dwconv(x_full, w):
    k = w.shape[0]
    t = x_full.shape[1] - k + 1
    out = x_full[:, 0:t] * w[0]
    for i in range(1, k):
        out = out + x_full[:, i:i + t] * w[i]
    return out


def ssd_chunked(x, dt, a, bm, cm):
    b, L = x.shape[0], x.shape[1]
    nc = L // CHUNK
    r = SSD_HEADS // SSD_GROUPS
    xf = x.astype(jnp.float32).reshape(b, nc, CHUNK, SSD_GROUPS, r, SSD_HEADDIM)
    dtf = dt.reshape(b, nc, CHUNK, SSD_GROUPS, r)
    bf = bm.astype(jnp.float32).reshape(b, nc, CHUNK, SSD_GROUPS, SSD_STATE)
    cf = cm.astype(jnp.float32).reshape(b, nc, CHUNK, SSD_GROUPS, SSD_STATE)
    a_cs = jnp.cumsum(dtf * a.reshape(SSD_GROUPS, r), axis=2)
    xdt = xf * dtf[..., None]
    seg = a_cs[:, :, :, None] - a_cs[:, :, None]
    causal = jnp.tril(jnp.ones((CHUNK, CHUNK), dtype=bool))[:, :, None, None]
    decay = jnp.exp(jnp.where(causal, seg, -jnp.inf))
    cb = jnp.einsum('bclgn,bcsgn->bclsg', cf, bf)
    m = cb[..., None] * decay
    y_diag = jnp.einsum('bclsgr,bcsgrp->bclgrp', m, xdt)
    decay_end = jnp.exp(a_cs[:, :, -1:] - a_cs)
    states = jnp.einsum('bclgn,bclgr,bclgrp->bcgrpn', bf, decay_end, xdt)
    chunk_decay = jnp.exp(a_cs[:, :, -1])

    def step(s, inp):
        st, dc = inp
        return s * dc[..., None, None] + st, s

    s0 = jnp.zeros((b, SSD_GROUPS, r, SSD_HEADDIM, SSD_STATE), jnp.float32)
    s_fin, s_prev = lax.scan(step, s0, (jnp.moveaxis(states, 1, 0), jnp.moveaxis(chunk_decay, 1, 0)))
    s_prev = jnp.moveaxis(s_prev, 0, 1)
    y_off = jnp.einsum('bclgn,bcgrpn,bclgr->bclgrp', cf, s_prev, jnp.exp(a_cs))
    y = (y_diag + y_off).reshape(b, L, SSD_HEADS, SSD_HEADDIM)
    return y, s_fin.reshape(b, SSD_HEADS, SSD_HEADDIM, SSD_STATE)


def ssd_recurrent(x, dt, a, bm, cm, s0):
    b = x.shape[0]
    r = SSD_HEADS // SSD_GROUPS
    ar = a.reshape(SSD_GROUPS, r)

    def step(s, inp):
        xt, dtt, bt, ct = inp
        xt = xt.astype(jnp.float32).reshape(b, SSD_GROUPS, r, SSD_HEADDIM)
        dtt = dtt.reshape(b, SSD_GROUPS, r)
        s = s * jnp.exp(dtt * ar)[..., None, None] + jnp.einsum('bgrp,bgn->bgrpn', xt * dtt[..., None], bt.astype(jnp.float32))
        yt = jnp.einsum('bgrpn,bgn->bgrp', s, ct.astype(jnp.float32))
        return s, yt.reshape(b, SSD_HEADS, SSD_HEADDIM)

    s_init = s0.astype(jnp.float32).reshape(b, SSD_GROUPS, r, SSD_HEADDIM, SSD_STATE)
    xs_t = (jnp.moveaxis(x, 1, 0), jnp.moveaxis(dt, 1, 0), jnp.moveaxis(bm, 1, 0), jnp.moveaxis(cm, 1, 0))
    s_fin, ys = lax.scan(step, s_init, xs_t)
    return jnp.moveaxis(ys, 0, 1), s_fin.reshape(b, SSD_HEADS, SSD_HEADDIM, SSD_STATE)


def _mixer_layer(u, ssd_buf, sc_buf, ssm_state, norm_w, w_in, conv_ssd_w, conv_ssd_b, dt_bias, a_log,
                 d_skip, ssd_norm_w, conv_sc_w, sc_norm_w, w_out):
    b, t = u.shape[0], u.shape[1]
    hn = rmsnorm(u, norm_w)
    z_ssd, xbc, dt_raw, z_sc, b_sc, c_sc, h_sc = jnp.split(hn @ w_in, SPLITS, axis=-1)
    xbc_full = jnp.concatenate([ssd_buf.astype(xbc.dtype), xbc], axis=1)
    xbc_c = jax.nn.silu(causal_dwconv(xbc_full, conv_ssd_w) + conv_ssd_b)
    new_ssd_buf = xbc_full[:, -(SSD_CONV - 1):]
    xs = xbc_c[..., :D_SSD].reshape(b, t, SSD_HEADS, SSD_HEADDIM)
    bm = xbc_c[..., D_SSD:D_SSD + D_GN].reshape(b, t, SSD_GROUPS, SSD_STATE)
    cm = xbc_c[..., D_SSD + D_GN:].reshape(b, t, SSD_GROUPS, SSD_STATE)
    dt = jax.nn.softplus(dt_raw.astype(jnp.float32) + dt_bias.astype(jnp.float32))
    a = -jnp.exp(a_log.astype(jnp.float32))
    if ssm_state is None:
        pad = CHUNK - N_META
        padf = lambda v: jnp.pad(v, [(0, 0), (pad, 0)] + [(0, 0)] * (v.ndim - 2))
        y, new_state = ssd_chunked(padf(xs), padf(dt), a, padf(bm), padf(cm))
        y = y[:, pad:]
    else:
        y, new_state = ssd_recurrent(xs, dt, a, bm, cm, ssm_state)
    y = y + d_skip.astype(jnp.float32)[:, None] * xs.astype(jnp.float32)
    y = y.reshape(b, t, D_SSD) * jax.nn.silu(z_ssd.astype(jnp.float32))
    y_ssd = rmsnorm(y, ssd_norm_w).astype(u.dtype)
    v = c_sc * h_sc
    v_full = jnp.concatenate([sc_buf.astype(v.dtype), v], axis=1)
    new_sc_buf = v_full[:, -(SC_CONV - 1):]
    y_sc = b_sc * causal_dwconv(v_full, conv_sc_w)
    y_sc = rmsnorm(y_sc * jax.nn.silu(z_sc), sc_norm_w)
    out = jnp.concatenate([y_ssd, y_sc], axis=-1) @ w_out
    return u + out, new_state, new_ssd_buf, new_sc_buf


def setup_inputs(seed: int = 0) -> dict:
    key = jax.random.key(seed)
    ks = jax.random.split(key, 20)
    f32 = jnp.float32
    x_prompt = jax.random.normal(ks[0], (BATCH, SEQ, D_MODEL), f32)
    x_sample = jax.random.normal(ks[1], (DEC_BATCH, DEC_SEQ, D_MODEL), f32)
    state_ssm = 0.1 * jax.random.normal(ks[2], (DEPTH, DEC_BATCH, SSD_HEADS, SSD_HEADDIM, SSD_STATE), f32)
    state_ssd_conv = jax.random.normal(ks[3], (DEPTH, DEC_BATCH, SSD_CONV - 1, D_XBC), f32)
    state_short_conv = jax.random.normal(ks[4], (DEPTH, DEC_BATCH, SC_CONV - 1, D_SC), f32)
    meta_tokens = jax.random.normal(ks[5], (N_META, D_MODEL), f32)
    norm_w = 1.0 + 0.02 * jax.random.normal(ks[6], (DEPTH, D_MODEL), f32)
    w_in = jax.random.normal(ks[7], (DEPTH, D_MODEL, D_IN_PROJ), f32) * D_MODEL ** -0.5
    conv_ssd_w = jax.random.normal(ks[8], (DEPTH, SSD_CONV, D_XBC), f32) * SSD_CONV ** -0.5
    conv_ssd_b = 0.02 * jax.random.normal(ks[9], (DEPTH, D_XBC), f32)
    dt0 = jnp.exp(jax.random.uniform(ks[10], (DEPTH, SSD_HEADS), f32) * (math.log(0.1) - math.log(0.001)) + math.log(0.001))
    dt_bias = dt0 + jnp.log(-jnp.expm1(-dt0))
    a_log = jnp.log(jax.random.uniform(ks[11], (DEPTH, SSD_HEADS), f32, 1.0, 16.0))
    d_skip = 1.0 + 0.1 * jax.random.normal(ks[12], (DEPTH, SSD_HEADS), f32)
    ssd_norm_w = 1.0 + 0.02 * jax.random.normal(ks[13], (DEPTH, D_SSD), f32)
    conv_sc_w = jax.random.normal(ks[14], (DEPTH, SC_CONV, D_SC), f32) * SC_CONV ** -0.5
    sc_norm_w = 1.0 + 0.02 * jax.random.normal(ks[15], (DEPTH, D_SC), f32)
    w_out = jax.random.normal(ks[16], (DEPTH, D_MIX, D_MODEL), f32) * D_MIX ** -0.5
    final_norm_w = 1.0 + 0.02 * jax.random.normal(ks[17], (D_MODEL,), f32)
    return {"x_prompt": x_prompt, "x_sample": x_sample, "state_ssm": state_ssm,
            "state_ssd_conv": state_ssd_conv, "state_short_conv": state_short_conv,
            "meta_tokens": meta_tokens, "norm_w": norm_w, "w_in": w_in, "conv_ssd_w": conv_ssd_w,
            "conv_ssd_b": conv_ssd_b, "dt_bias": dt_bias, "a_log": a_log, "d_skip": d_skip,
            "ssd_norm_w": ssd_norm_w, "conv_sc_w": conv_sc_w, "sc_norm_w": sc_norm_w,
            "w_out": w_out, "final_norm_w": final_norm_w}


def reference(x_prompt, x_sample, state_ssm, state_ssd_conv, state_short_conv, meta_tokens, norm_w, w_in,
              conv_ssd_w, conv_ssd_b, dt_bias, a_log, d_skip, ssd_norm_w, conv_sc_w, sc_norm_w, w_out,
              final_norm_w):
    b_p = x_prompt.shape[0]
    meta = jnp.broadcast_to(meta_tokens.astype(x_prompt.dtype)[None], (b_p, N_META, D_MODEL))
    u_p = jnp.concatenate([meta, x_prompt], axis=1)
    u_s = x_sample
    ssm_p, cssd_p, csc_p, ssm_s, cssd_s, csc_s = [], [], [], [], [], []
    for l in range(DEPTH):
        lw = (norm_w[l], w_in[l], conv_ssd_w[l], conv_ssd_b[l], dt_bias[l], a_log[l], d_skip[l],
              ssd_norm_w[l], conv_sc_w[l], sc_norm_w[l], w_out[l])
        u_p, s1, c1, k1 = _mixer_layer(u_p, jnp.zeros((b_p, SSD_CONV - 1, D_XBC), u_p.dtype),
                                       jnp.zeros((b_p, SC_CONV - 1, D_SC), u_p.dtype), None, *lw)
        u_s, s2, c2, k2 = _mixer_layer(u_s, state_ssd_conv[l], state_short_conv[l], state_ssm[l], *lw)
        ssm_p.append(s1); cssd_p.append(c1); csc_p.append(k1)
        ssm_s.append(s2); cssd_s.append(c2); csc_s.append(k2)
    y_prompt = rmsnorm(u_p, final_norm_w)[:, N_META:]
    y_sample = rmsnorm(u_s, final_norm_w)
    return (y_prompt, y_sample, jnp.stack(ssm_p), jnp.stack(cssd_p), jnp.stack(csc_p),
            jnp.stack(ssm_s), jnp.stack(cssd_s), jnp.stack(csc_s))
```

```python
import numpy as np
from contextlib import ExitStack
import concourse.bass as bass
import concourse.mybir as mybir
from concourse.bass_utils import run_bass_kernel_spmd

F32 = mybir.dt.float32
BF16 = mybir.dt.bfloat16
AF = mybir.ActivationFunctionType
ALU = mybir.AluOpType
AX = mybir.AxisListType

D = 2048
NE = 1056
OWN0 = 32
HALO0 = 16
EPS = 1e-5
DIN = 13344
C_Z, C_XBC, C_DT, C_ZSC, C_BSC, C_CSC, C_HSC = 0, 2048, 5120, 5152, 7200, 9248, 11296
BLKS = [(0, 352), (352, 704), (704, 1056)]
ENGS = ("pe", "act", "dve", "pool", "sp")
N_DMA_SEMS = 24
NWS = 8


class _Op:
    __slots__ = ("eng", "emit", "deps", "dma", "signal", "ticket", "sem", "idx")

    def __init__(self, eng, emit, deps, dma, idx):
        self.eng, self.emit, self.deps, self.dma, self.idx = eng, emit, deps, dma, idx
        self.signal = False
        self.ticket = None
        self.sem = None


class Sched:
    def __init__(self, nc, sems):
        self.nc, self.sems = nc, sems
        self.ops = []
        self.last_writer = {}
        self.readers = {}
        self.dma_last = [None] * N_DMA_SEMS
        self.dma_rr = {"sp": 0, "pool": 0}
        self.bar = None
        self.since_bar = []

    def add(self, eng, emit, reads=(), writes=(), dma=False):
        idx = len(self.ops)
        deps = set()
        for k in reads:
            w = self.last_writer.get(k)
            if w is not None:
                deps.add(w)
        for k in writes:
            w = self.last_writer.get(k)
            if w is not None:
                deps.add(w)
            for r in self.readers.get(k, ()):
                deps.add(r)
        for k in writes:
            self.last_writer[k] = idx
            self.readers[k] = []
        for k in reads:
            self.readers.setdefault(k, []).append(idx)
        if self.bar is not None:
            deps.add(self.bar)
        deps.discard(idx)
        op = _Op(eng, emit, deps, dma, idx)
        if dma:
            r = self.dma_rr[eng]
            self.dma_rr[eng] = r + 1
            s = (r % 16) if eng == "sp" else 16 + (r % (N_DMA_SEMS - 16))
            op.sem = s
            if self.dma_last[s] is not None:
                op.deps.add(self.dma_last[s])
            self.dma_last[s] = idx
        self.ops.append(op)
        self.since_bar.append(idx)
        return idx

    def dma(self, eng, out, in_, reads=(), writes=(), **kw):
        return self.add(eng, lambda e: e.dma_start(out=out, in_=in_, **kw), reads, writes, dma=True)

    def barrier(self):
        idx = len(self.ops)
        last = {}
        for i in self.since_bar:
            o = self.ops[i]
            last[("dma", o.sem) if o.dma else ("eng", o.eng)] = i
        deps = set(last.values())
        if self.bar is not None:
            deps.add(self.bar)
        op = _Op("sp", lambda e: e.nop(), deps, False, idx)
        self.ops.append(op)
        self.bar = idx
        self.since_bar = []
        return idx

    def flush(self, final_keys=()):
        nc, ops = self.nc, self.ops
        for op in ops:
            if op.eng == "pe" and not op.dma:
                op.deps = {d for d in op.deps if not (ops[d].eng == "pe" and not ops[d].dma)}
        fin = set()
        for k in final_keys:
            w = self.last_writer.get(k)
            if w is not None:
                fin.add(w)
        for op in ops:
            for d in op.deps:
                ops[d].signal = True
        for d in fin:
            ops[d].signal = True
        eng_count = {e: 0 for e in ENGS}
        dma_count = [0] * N_DMA_SEMS
        for op in ops:
            if op.dma:
                dma_count[op.sem] += 16
                op.ticket = dma_count[op.sem]
            elif op.signal:
                eng_count[op.eng] += 1
                op.ticket = eng_count[op.eng]
        per_eng = {e: [] for e in ENGS}
        for op in ops:
            per_eng[op.eng].append(op)
        sems = self.sems

        def semkey(d):
            return ("dma", d.sem) if d.dma else ("eng", d.eng)

        def getsem(k):
            return sems["dma%d" % k[1]] if k[0] == "dma" else sems[k[1]]

        def run(engname, eng):
            seen = {}
            for op in per_eng[engname]:
                need = {}
                for di in op.deps:
                    d = ops[di]
                    k = semkey(d)
                    if d.ticket > need.get(k, 0):
                        need[k] = d.ticket
                for k, t in need.items():
                    if seen.get(k, 0) >= t:
                        continue
                    seen[k] = t
                    eng.wait_ge(getsem(k), t)
                ins = op.emit(eng)
                if op.dma:
                    ins.then_inc(sems["dma%d" % op.sem], 16)
                elif op.signal:
                    ins.then_inc(sems[op.eng], 1)
            if engname == "sp":
                need = {}
                for di in fin:
                    d = ops[di]
                    k = semkey(d)
                    need[k] = max(need.get(k, 0), d.ticket)
                for k, t in need.items():
                    eng.wait_ge(getsem(k), t)

        with nc.Block() as block:
            @block.tensor
            def _(e):
                run("pe", e)

            @block.scalar
            def _(e):
                run("act", e)

            @block.vector
            def _(e):
                run("dve", e)

            @block.gpsimd
            def _(e):
                run("pool", e)

            @block.sync
            def _(e):
                run("sp", e)


def ck(name, idx, c0, c1, gran=128):
    return [(name, idx, q) for q in range(c0 // gran, (c1 - 1) // gran + 1)]


def build(debug=False):
    nc = bass.Bass("TRN2", target_bir_lowering=False)

    def din(name, shape):
        return nc.dram_tensor(name, shape, F32, kind="ExternalInput").ap()

    def dout(name, shape):
        return nc.dram_tensor(name, shape, F32, kind="ExternalOutput").ap()

    xs = din("xs", [2064, D])
    xsm = din("xsm", [16, D])
    sssm = din("sssm", [16, 2048, 128])
    scv = din("scv", [16, 3, 3072])
    ssc = din("ssc", [16, 2, 2048])
    w_in = din("w_in", [D, DIN])
    w_out = din("w_out", [2 * D, D])
    normw_bc = din("normw_bc", [128, D])
    fnormw_bc = din("fnormw_bc", [128, D])
    cvec_d = din("cvec", [128, 216])
    hvec_d = din("hvec", [128, 128])
    cmat_d = din("cmat", [128, 512])

    y_own = dout("y_own", [1024, D])
    y_smp = dout("y_smp", [16, D])
    ssm_fin = dout("ssm_fin", [2048, 128])
    cssd_fin = dout("cssd_fin", [3, 3072])
    csc_fin = dout("csc_fin", [2, 2048])
    ssm_smp = dout("ssm_smp", [16, 2048, 128])
    cssd_smp = dout("cssd_smp", [16, 3, 3072])
    csc_smp = dout("csc_smp", [16, 2, 2048])
    scr = nc.dram_tensor("scr", [16, 1024], F32).ap()
    out_keys = []
    dbg_outs = {}

    with ExitStack() as es:
        E = es.enter_context
        sems = {n: E(nc.semaphore(n)) for n in ("pe", "act", "dve", "pool", "sp")}
        for i in range(N_DMA_SEMS):
            sems["dma%d" % i] = E(nc.semaphore("dma%d" % i))

        def sb(name, shape, dt):
            return E(nc.sbuf_tensor("s_" + name, shape, dt))

        big = sb("big", [128, 25408], BF16)
        hnT = big[:, 0:16896].rearrange("p (k t) -> p k t", k=16)
        stage_f = big[:, 16896:25408].bitcast(F32)
        raw = [stage_f[:, 0:1064], stage_f[:, 1064:2128]]
        acc = [stage_f[:, 2128:3192], stage_f[:, 3192:4256]]
        out_acc = big[:, 0:20480].bitcast(F32).rearrange("p (t d) -> p t d", t=5)
        Wp = sb("Wp", [128, 16384], BF16)
        Wslot = [Wp[:, 2048 * s:2048 * (s + 1)].rearrange("p (k c) -> p k c", k=16) for s in range(NWS)]
        Wout = [Wp[:, 8192 * s:8192 * (s + 1)].rearrange("p (k c) -> p k c", k=32) for s in range(2)]
        post_t = sb("post", [128, 24, NE], BF16)
        post = [post_t[:, ct, :] for ct in range(24)]
        gi = sb("gi", [128, 17152], BF16)
        gbc = gi[:, 0:4096].bitcast(F32)
        xin = [gi[:, 4096 + 4096 * b:8192 + 4096 * b].bitcast(F32) for b in range(2)]
        hnb = [gi[:, 12288 + 2048 * b:14336 + 2048 * b] for b in range(2)]
        A_ = [[gi[:, 2048 * b + 1024 * hl:2048 * b + 1024 * (hl + 1)].rearrange("p (r l) -> p r l", r=8) for hl in range(2)] for b in range(2)]
        decT = [gi[:, 4096 + 1024 * b:5120 + 1024 * b].rearrange("p (r l) -> p r l", r=8) for b in range(2)]
        MT = [gi[:, 6144 + 1024 * b:7168 + 1024 * b].rearrange("p (r l) -> p r l", r=8) for b in range(2)]
        CTe = [gi[:, 512 * b:512 * (b + 1)].rearrange("p (r l) -> p r l", r=4) for b in range(4)]
        CBm = gi[:, 9216:9728].rearrange("p (g l) -> p g l", g=4)
        xdt = gi[:, 9728:11776]
        xdd = gi[:, 11776:13824]
        Btok = gi[:, 13824:14336]
        stg_f = gi[:, 0:4096].bitcast(F32)
        ysc_t = gi[:, 0:16896].rearrange("p (f t) -> p f t", f=16)
        ysc = [ysc_t[:, f, :] for f in range(16)]
        Sf = sb("Sf", [128, 2048], F32)
        Sb = sb("Sb", [128, 2048], BF16)
        Bsb = [Sf[:, 512 * b:512 * (b + 1)] for b in range(2)]
        Csb = [Sf[:, 1024 + 512 * b:1536 + 512 * b] for b in range(2)]
        St = [post_t[:, 16 + 4 * k:20 + 4 * k, :].rearrange("p a b -> p (a b)").bitcast(F32)[:, 0:2048]
              .rearrange("p (j n) -> p j n", j=16) for k in range(2)]
        cm = sb("cm", [128, 512], F32)
        identf, triL, triU, onesf = cm[:, 0:128], cm[:, 128:256], cm[:, 256:384], cm[:, 384:512]
        identb = sb("identb", [128, 128], BF16)
        onesb = sb("onesb", [128, 128], BF16)
        triLb = sb("triLb", [128, 128], BF16)
        triUb = sb("triUb", [128, 128], BF16)
        dth_all = sb("dth_all", [128, 8, 2, 32], BF16)
        dtf = sb("dtf", [128, 32], F32)
        penb = sb("penb", [128, 128], BF16)
        negacs = sb("negacs", [128, 32], F32)
        sq16 = sb("sq16", [128, 16, 16], F32)
        cvec = sb("cvec", [128, 216], F32)
        hvec = sb("hvec", [128, 128], F32)
        dtb_bc, alog_bc = hvec[:, 0:32], hvec[:, 32:64]
        a_bc = sb("a_bc", [128, 32], F32)
        wdt = sb("wdt", [128, 16, 32], BF16)
        dtP = sb("dtP", [128, 8, 32], F32)
        dtaP = sb("dtaP", [128, 8, 32], F32)
        dtO, dtaO = dtP, dtaP
        dtH = sb("dtH", [16, 2, 32], F32)
        dtS = sb("dtS", [16, 2, 32], F32)
        sm = sb("sm", [128, 8, 32], F32)
        eacs = sb("eacs", [128, 32], BF16)
        eacsT = sb("eacsT", [32, 128], BF16)
        ss = sb("ss", [128, 8], F32)
        sraw = gi[:, 12288:15360].bitcast(F32).rearrange("p (a b c) -> p a b c", a=24, b=16)
        sacc = sb("sacc", [128, 2, 64], F32)
        sxp = sb("sxp", [128, 24, 16], F32)
        svraw = sb("svraw", [128, 16, 16, 3], F32)
        svacc = sb("svacc", [128, 48], F32)
        tsb = sb("tsb", [128, 16, 16], F32)
        xlastP = sb("xlastP", [128, 24, 3], F32)
        xlastE = sb("xlastE", [128, 24, 3], F32)
        vlast = sb("vlast", [128, 16, 2], F32)
        szs = sb("szs", [128, 16, 16], F32)
        sqb = [sb("sqb%d" % b, [128, 512], BF16) for b in range(2)]
        dAT = sb("dAT", [32, 2, 16], F32)
        decx = sb("decx", [128, 16, 16], F32)
        dtx = sb("dtx", [128, 16, 16], F32)
        xdts = sb("xdts", [128, 16, 16], F32)
        ysT = sb("ysT", [128, 16, 16], F32)
        yg_s = sb("yg_s", [128, 2, 16, 16], F32)
        ysn = sb("ysn", [128, 32, 16], BF16)
        junk = sb("junk", [128, 128], F32)
        rs16 = sb("rs16", [128, 2, 16], F32)
        ps = E(nc.psum_tensor("ps", [128, 4096], F32))

        def bank(i):
            return ps[:, 512 * i:512 * (i + 1)]

        def bank_bf(i):
            return ps[:, 512 * i:512 * (i + 1)].bitcast(BF16)

        S = Sched(nc, sems)
        KPS = lambda i: ("ps", i)

        def dbg(name, ap, shape, reads):
            if not debug:
                return
            d = dout("dbg_" + name, list(shape))
            dbg_outs[name] = shape
            S.dma("sp", d, ap, reads=reads, writes=[("dbg", name)])
            out_keys.append(("dbg", name))

        def rsqrt_ip(ap, keys):
            S.add("act", lambda e: e.activation(out=ap, in_=ap, func=AF.Sqrt), reads=keys, writes=keys)
            S.add("dve", lambda e: e.reciprocal(out=ap, in_=ap), reads=keys, writes=keys)

        S.dma("sp", cm[:], cmat_d, writes=["cm"])
        S.dma("sp", cvec[:], cvec_d, writes=["cvec"])
        S.dma("sp", hvec[:], hvec_d, writes=["hvec"])
        S.dma("sp", gbc, normw_bc, writes=["gbc"])
        S.add("act", lambda e: e.copy(out=identb[:], in_=identf), reads=["cm"], writes=["identb"])
        S.add("act", lambda e: e.copy(out=onesb[:], in_=onesf), reads=["cm"], writes=["onesb"])
        S.add("act", lambda e: e.copy(out=triLb[:], in_=triL), reads=["cm"], writes=["triLb"])
        S.add("act", lambda e: e.copy(out=triUb[:], in_=triU), reads=["cm"], writes=["triUb"])
        S.add("act", lambda e: e.activation(out=penb[:], in_=triU, func=AF.Identity, scale=-30000.0), reads=["cm"], writes=["penb"])
        S.add("act", lambda e: e.activation(out=a_bc[:], in_=alog_bc, func=AF.Exp), reads=["hvec"], writes=["a_bc0"])
        S.add("dve", lambda e: e.tensor_scalar(out=a_bc[:], in0=a_bc[:], scalar1=-1.0, scalar2=None, op0=ALU.mult),
              reads=["a_bc0"], writes=["a_bc"])
        S.dma("pool", wdt[:], w_in.rearrange("(k p) c -> p k c", p=128)[:, :, C_DT:C_DT + 32], writes=["wdt"])
        for b in range(2):
            S.add("dve", lambda e, b=b: e.memset(raw[b][:, 0:1064], 0.0), writes=ck("raw", b, 0, 1064))
        S.add("dve", lambda e: e.memset(ss[:], 0.0), writes=[("ss", k) for k in range(8)])
        w_in_v = w_in.rearrange("(k p) c -> p k c", p=128)
        w_out_v = w_out.rearrange("(k p) c -> p k c", p=128)

        tile_ctr = [0]

        def hn_tile(srcs, R, col0):
            t = tile_ctr[0]
            tile_ctr[0] += 1
            b = t % 2
            kx, kh = ("xin", b), ("hnb", b)
            if len(srcs) > 1 or srcs[0][1] != R:
                S.add("dve", lambda e: e.memset(xin[b][0:R, :], 0.0), writes=[kx])
            for (r0, n, ap) in srcs:
                S.dma("sp", xin[b][r0:r0 + n, :], ap, writes=[kx])
            sscol = ss[0:R, b:b + 1]
            S.add("act", lambda e: e.activation(out=hnb[b][0:R, :], in_=xin[b][0:R, :], func=AF.Square, accum_out=sscol),
                  reads=[kx], writes=[kh, ("ss", b)])
            S.add("dve", lambda e: e.tensor_scalar(out=sscol, in0=sscol, scalar1=1.0 / D, scalar2=EPS, op0=ALU.mult, op1=ALU.add),
                  reads=[("ss", b)], writes=[("ss", b)])
            rsqrt_ip(sscol, [("ss", b)])
            S.add("dve", lambda e: e.scalar_tensor_tensor(out=hnb[b][0:R, :], in0=xin[b][0:R, :], scalar=sscol, in1=gbc[0:R, :],
                                                          op0=ALU.mult, op1=ALU.mult),
                  reads=[kx, ("ss", b), "gbc"], writes=[kh])
            S.add("dve", lambda e: e.memset(sscol, 0.0), writes=[("ss", b)])

            def stage_b():
                pb = 2 * b
                ptr = ps[:, 512 * pb:512 * (pb + 2)].bitcast(BF16).rearrange("p (k t) -> p k t", k=16)
                for kt in range(16):
                    S.add("pe", lambda e, kt=kt: e.transpose(out=ptr[:, kt, 0:R], in_=hnb[b][0:R, kt * 128:(kt + 1) * 128],
                                                             identity=identb[0:R, 0:R]),
                          reads=[kh, "identb"], writes=[KPS(pb + kt // 8)])
                S.add("act", lambda e: e.copy(out=hnT[:, 0:8, col0:col0 + R], in_=ptr[:, 0:8, 0:R]),
                      reads=[KPS(pb)], writes=ck("hnTa", 0, col0, col0 + R))
                S.add("dve", lambda e: e.tensor_copy(out=hnT[:, 8:16, col0:col0 + R], in_=ptr[:, 8:16, 0:R]),
                      reads=[KPS(pb + 1)], writes=ck("hnT", 0, col0, col0 + R))
            return stage_b

        def hn_tiles(specs):
            pend = None
            for sp in specs:
                nb = hn_tile(*sp)
                if pend is not None:
                    pend()
                pend = nb
            pend()

        wctr = [0]
        bank_pool = [list(range(7))]
        bctr = [0]

        pe_deferred = []
        bg = [None, 0]

        def run_jobs(jobs, PF=5):
            slots = {}

            def issue(j):
                s = wctr[0] % NWS
                wctr[0] += 1
                slots[j] = s
                c = jobs[j][0]
                S.dma("pool", Wslot[s][:], w_in_v[:, :, c:c + 128], writes=[("W", s)])
            for j in range(min(PF, len(jobs))):
                issue(j)
            for j, (wc, blocks, epi, fin) in enumerate(jobs):
                if j + PF < len(jobs):
                    issue(j + PF)
                s = slots[j]
                for (c0, c1) in blocks:
                    bp = bank_pool[0]
                    bi = bp[bctr[0] % len(bp)]
                    bctr[0] += 1
                    pap = bank(bi)[:, 0:c1 - c0]
                    for kt in range(16):
                        S.add("pe", lambda e, kt=kt, s=s, c0=c0, c1=c1, pap=pap:
                              e.matmul(pap, lhsT=Wslot[s][:, kt, :], rhs=hnT[:, kt, c0:c1], start=(kt == 0), stop=(kt == 15)),
                              reads=[("W", s)] + ck("hnT", 0, c0, c1) + ck("hnTa", 0, c0, c1), writes=[KPS(bi)])
                    while pe_deferred:
                        pe_deferred.pop(0)()
                    epi(c0, c1, pap, KPS(bi))
                    if bg[0] is not None:
                        for _ in range(bg[1]):
                            if next(bg[0], "done") == "done":
                                bg[0] = None
                                break
                if fin is not None:
                    fin()
                if bg[0] is not None and next(bg[0], "done") == "done":
                    bg[0] = None
            while pe_deferred:
                pe_deferred.pop(0)()

        def dt_tile(c0, R, dst_dt, dst_dta, mask_ap, kd, hl=None):
            pap = bank(7)[0:R, 0:32]
            for kt in range(16):
                S.add("pe", lambda e, kt=kt: e.matmul(pap, lhsT=hnT[:, kt, c0:c0 + R], rhs=wdt[:, kt, :], start=(kt == 0), stop=(kt == 15)),
                      reads=["wdt"] + ck("hnT", 0, c0, c0 + R) + ck("hnTa", 0, c0, c0 + R), writes=[KPS(7)])
            t1 = sm[0:R, 0, :]
            S.add("dve", lambda e: e.tensor_tensor(out=t1, in0=pap, in1=dtb_bc[0:R, :], op=ALU.add),
                  reads=[KPS(7), "hvec"], writes=[("sm", 0)])
            S.add("act", lambda e: e.activation(out=t1, in_=t1, func=AF.Exp), reads=[("sm", 0)], writes=[("sm", 0)])
            S.add("act", lambda e: e.activation(out=dst_dt, in_=t1, func=AF.Ln, bias=1.0), reads=[("sm", 0)], writes=[kd])
            if mask_ap is not None:
                S.add("dve", lambda e: e.tensor_scalar(out=dst_dt, in0=dst_dt, scalar1=mask_ap, scalar2=None, op0=ALU.mult),
                      reads=[kd, "hvec"], writes=[kd])
            S.add("dve", lambda e: e.tensor_tensor(out=dst_dta, in0=dst_dt, in1=a_bc[0:R, :], op=ALU.mult),
                  reads=[kd, "a_bc"], writes=[kd])
            if hl is not None:
                S.add("dve", lambda e: e.tensor_copy(out=hl[:, 0, :], in_=dst_dta), reads=[kd], writes=[kd])
                S.add("dve", lambda e: e.tensor_copy(out=dtf[:], in_=hl[:, 0, :]), reads=[kd], writes=["dtf"])
                S.add("dve", lambda e: e.tensor_tensor(out=hl[:, 1, :], in0=dst_dta, in1=dtf[:], op=ALU.subtract), reads=[kd, "dtf"], writes=[kd])

        def cw(i, ct):
            return cvec[:, 24 * i + ct:24 * i + ct + 1]

        def cb_(ct):
            return cvec[:, 96 + ct:97 + ct]

        def conv4(rb, ct, lo, hi, dst):
            n = hi - lo
            ka = ck("acc", rb, lo, hi)
            S.add("act", lambda e: e.activation(out=acc[rb][:, lo:hi], in_=raw[rb][:, 3 + lo:3 + hi], func=AF.Identity,
                                                bias=cb_(ct), scale=cw(3, ct)),
                  reads=ck("raw", rb, 3 + lo, 3 + hi) + ["cvec"], writes=ka)
            for i in range(3):
                S.add("dve",
                      lambda e, i=i: e.scalar_tensor_tensor(out=acc[rb][:, lo:hi], in0=raw[rb][:, i + lo:i + hi], scalar=cw(i, ct),
                                                            in1=acc[rb][:, lo:hi], op0=ALU.mult, op1=ALU.add),
                      reads=ck("raw", rb, i + lo, i + hi) + ["cvec"] + ka, writes=ka)
            S.add("act", lambda e: e.activation(out=dst[:, lo:hi], in_=acc[rb][:, lo:hi], func=AF.Silu),
                  reads=ka, writes=ck("post", ct, lo, hi))

        hn_tiles([([(0, 128, xs[128 * i:128 * (i + 1), :])], 128, 128 * i) for i in range(8)])
        for i in range(8):
            dt_tile(128 * i, 128, dtP[:, i, :], dtaP[:, i, :], hvec[:, 64 + i:65 + i], ("dtP", i))
        jobs = []
        for ct in range(20):
            rb = ct % 2

            def epi(c0, c1, pap, pk, rb=rb):
                S.add("act", lambda e: e.copy(out=raw[rb][:, 3 + c0:3 + c1], in_=pap), reads=[pk], writes=ck("raw", rb, 3 + c0, 3 + c1))

            def fin(rb=rb, ct=ct):
                S.add("dve", lambda e: e.tensor_copy(out=xlastP[:, ct, :], in_=raw[rb][:, 1024:1027]),
                      reads=ck("raw", rb, 1024, 1027), writes=[("xlastP", ct)])
                conv4(rb, ct, 0, 1024, post[ct])
            jobs.append((C_XBC + 128 * ct, [(0, 512), (512, 1024)], epi, fin))
        run_jobs(jobs)
        S.add("dve", lambda e: e.memset(xlastP[:, 20:24, :], 0.0), writes=[("xlastP", c) for c in range(20, 24)])

        hn_tiles([([(0, 16, xsm), (16, 16, xs[1024:1040, :])], 32, 0)] +
                 [([(0, 128, xs[1040 + 128 * i:1040 + 128 * (i + 1), :])], 128, OWN0 + 128 * i) for i in range(8)])

        S.add("dve", lambda e: e.memset(Sf[:], 0.0), writes=["Sf"])
        S.add("dve", lambda e: e.memset(Sb[:], 0.0), writes=["Sb"])
        b1bf = bank_bf(1)
        xtok_ps = ps[:, 1024:2048].bitcast(BF16).rearrange("p (c t) -> p c t", c=16)
        btok_ps = bank_bf(0)[:, 0:512].rearrange("p (g t) -> p g t", g=4)
        sm_ps = bank(1)[:, 128:224]
        gab = [0]

        def ssd_chunk(c0, L, dt_ap, dta_ap, kd, with_y):
            S.add("pe", lambda e: e.matmul(sm_ps[0:L, 0:32], lhsT=triU[0:L, 0:L], rhs=dta_ap, start=True, stop=True),
                  reads=["cm", kd], writes=[KPS(1)])
            S.add("pe", lambda e: e.matmul(sm_ps[:, 32:64], lhsT=onesf[0:L, :], rhs=dta_ap, start=True, stop=True),
                  reads=["cm", kd], writes=[KPS(1)])
            if with_y:
                S.add("pe", lambda e: e.matmul(sm_ps[0:L, 64:96], lhsT=triL[0:L, 0:L], rhs=dta_ap, start=True, stop=True),
                      reads=["cm", kd], writes=[KPS(1)])
                for g in range(4):
                    S.add("pe", lambda e, g=g: e.matmul(bank(0)[:, 128 * g:128 * (g + 1)], lhsT=post[16 + g][:, c0:c0 + 128],
                                                        rhs=post[20 + g][:, c0:c0 + 128], start=True, stop=True),
                          reads=ck("post", 16 + g, c0, c0 + 128) + ck("post", 20 + g, c0, c0 + 128), writes=[KPS(0)])
            for ct in range(16):
                S.add("pe", lambda e, ct=ct: e.transpose(out=xtok_ps[0:L, ct, :], in_=post[ct][:, c0:c0 + L], identity=identb[:]),
                      reads=ck("post", ct, c0, c0 + L) + ["identb"], writes=[KPS(2), KPS(3)])
            dend, cd, w2 = sm[0:L, 1, :], sm[:, 2, :], sm[0:L, 3, :]
            S.add("act", lambda e: e.activation(out=dend, in_=sm_ps[0:L, 0:32], func=AF.Exp), reads=[KPS(1)], writes=[("sm", 1)])
            S.add("act", lambda e: e.activation(out=cd, in_=sm_ps[:, 32:64], func=AF.Exp), reads=[KPS(1)], writes=[("sm", 2)])
            if with_y:
                S.add("act", lambda e: e.activation(out=eacs[:], in_=sm_ps[:, 64:96], func=AF.Exp), reads=[KPS(1)], writes=["eacs"])
                S.add("act", lambda e: e.activation(out=negacs[:], in_=sm_ps[:, 64:96], func=AF.Identity, scale=-1.0), reads=[KPS(1)], writes=["negacs"])
                S.add("dve", lambda e: e.tensor_tensor(out=CBm, in0=bank(0).rearrange("p (g l) -> p g l", g=4),
                                                       in1=triL.unsqueeze(1).to_broadcast([128, 4, 128]), op=ALU.mult),
                      reads=[KPS(0), "cm"], writes=["CBm"])
            S.add("dve", lambda e: e.tensor_tensor(out=w2, in0=dt_ap, in1=dend, op=ALU.mult), reads=[kd, ("sm", 1)], writes=[("sm", 3)])
            if with_y:
                S.add("pe", lambda e: e.transpose(out=b1bf[0:32, 0:128], in_=eacs[:], identity=identb[:]),
                      reads=["eacs", "identb"], writes=[KPS(1)])
                S.add("act", lambda e: e.copy(out=eacsT[:], in_=b1bf[0:32, 0:128]), reads=[KPS(1)], writes=["eacsT"])
            for g in range(4):
                S.add("pe", lambda e, g=g: e.transpose(out=btok_ps[0:L, g, :], in_=post[16 + g][:, c0:c0 + L], identity=identb[:]),
                      reads=ck("post", 16 + g, c0, c0 + L) + ["identb"], writes=[KPS(0)])
            xt3 = xtok_ps[0:L].rearrange("p c t -> p (c t)").rearrange("p (h q) -> p h q", h=32)
            S.add("dve", lambda e: e.tensor_tensor(out=xdd[0:L, :].rearrange("p (h q) -> p h q", h=32), in0=xt3,
                                                   in1=w2.unsqueeze(2).to_broadcast([L, 32, 64]), op=ALU.mult),
                  reads=[KPS(2), KPS(3), ("sm", 3)], writes=["xdd"])
            S.add("act", lambda e: e.copy(out=Btok[0:L, :], in_=btok_ps[0:L].rearrange("p g t -> p (g t)")), reads=[KPS(0)], writes=["Btok"])
            if with_y:
                S.add("dve", lambda e: e.tensor_tensor(out=xdt[0:L, :].rearrange("p (h q) -> p h q", h=32), in0=xt3,
                                                       in1=dt_ap.unsqueeze(2).to_broadcast([L, 32, 64]), op=ALU.mult),
                      reads=[KPS(2), KPS(3), kd], writes=["xdt"])
                dth = dth_all[:, (c0 - OWN0) // 128, :, :]
                def stage1(g):
                    ab = g % 2
                    for r in range(8):
                        h = 8 * g + r
                        outp = bank(4 + r // 4)[:, 128 * (r % 4):128 * (r % 4 + 1)]
                        for hl in range(2):
                            S.add("pe", lambda e, h=h, hl=hl, outp=outp: e.matmul(outp, lhsT=dth[:, hl, h:h + 1].to_broadcast([128, 128]), rhs=triLb[:],
                                                                                 start=(hl == 0), stop=False),
                                  reads=[kd, "triLb"], writes=[KPS(4 + r // 4)])
                        S.add("pe", lambda e, outp=outp: e.matmul(outp, lhsT=identb[:], rhs=penb[:], start=False, stop=True),
                              reads=["identb", "penb"], writes=[KPS(4 + r // 4)])
                    for r in range(8):
                        h = 8 * g + r
                        S.add("act", lambda e, r=r, h=h, ab=ab: e.activation(out=decT[ab][:, r, :], in_=bank(4 + r // 4)[:, 128 * (r % 4):128 * (r % 4 + 1)],
                                                                             func=AF.Exp, bias=negacs[:, h:h + 1]),
                              reads=[KPS(4 + r // 4), "negacs"], writes=[("decT", ab, r // 4)])

                def stage2a(g):
                    ab = g % 2
                    S.add("dve", lambda e: e.tensor_tensor(out=MT[ab], in0=decT[ab],
                                                           in1=CBm[:, g, :].unsqueeze(1).to_broadcast([128, 8, 128]), op=ALU.mult),
                          reads=[("decT", ab, 0), ("decT", ab, 1), "CBm"], writes=[("MT", ab)])
                    for hh in range(2):
                        qq = 2 * g + hh
                        cbi = qq % 4
                        eb = 6 if qq % 2 == 0 else 2
                        for r4 in range(4):
                            h = 8 * g + 4 * hh + r4
                            S.add("pe", lambda e, r4=r4, h=h, eb=eb: e.matmul(bank(eb)[:, 128 * r4:128 * (r4 + 1)],
                                                                              lhsT=identb[0:32, h:h + 1].to_broadcast([32, 128]),
                                                                              rhs=eacsT[:], start=True, stop=True),
                                  reads=["identb", "eacsT"], writes=[KPS(eb)])
                        S.add("dve", lambda e, cbi=cbi, eb=eb: e.tensor_tensor(out=CTe[cbi], in0=bank(eb).rearrange("p (r l) -> p r l", r=4),
                                                                               in1=post[20 + g][:, c0:c0 + 128].unsqueeze(1).to_broadcast([128, 4, 128]),
                                                                               op=ALU.mult),
                              reads=[KPS(eb)] + ck("post", 20 + g, c0, c0 + 128), writes=[("CTe", cbi), "stg"])

                def stage2b(g):
                    ab = g % 2
                    for hh in range(2):
                        qq = 2 * g + hh
                        cbi = qq % 4
                        yb = 7 if qq % 2 == 0 else 3
                        for t2 in range(2):
                            tt = 2 * hh + t2
                            yps = bank(yb)[:, 128 * t2:128 * (t2 + 1)]
                            for hp in range(2):
                                h = 8 * g + 2 * tt + hp
                                r = 2 * tt + hp
                                r4 = r - 4 * hh
                                S.add("pe", lambda e, h=h, r=r, hp=hp, yps=yps: e.matmul(yps[64 * hp:64 * hp + 64, :], lhsT=xdt[:, 64 * h:64 * h + 64],
                                                                                        rhs=MT[ab][:, r, :], start=True, stop=False),
                                      reads=["xdt", ("MT", ab)], writes=[KPS(yb)])
                                S.add("pe", lambda e, h=h, r4=r4, hp=hp, yps=yps, cbi=cbi: e.matmul(yps[64 * hp:64 * hp + 64, :], lhsT=Sb[:, 64 * h:64 * h + 64],
                                                                                                   rhs=CTe[cbi][:, r4, :], start=False, stop=True),
                                      reads=["Sb", ("CTe", cbi)], writes=[KPS(yb)])
                    for hh in range(2):
                        qq = 2 * g + hh
                        yb = 7 if qq % 2 == 0 else 3
                        for t2 in range(2):
                            tt = 2 * hh + t2
                            ct = 4 * g + tt
                            yps = bank(yb)[:, 128 * t2:128 * (t2 + 1)]
                            pk = ck("post", ct, c0, c0 + 128)
                            S.add("dve", lambda e, ct=ct, yps=yps: e.scalar_tensor_tensor(out=post[ct][:, c0:c0 + 128], in0=post[ct][:, c0:c0 + 128],
                                                                                         scalar=cvec[:, 200 + ct:201 + ct], in1=yps,
                                                                                         op0=ALU.mult, op1=ALU.add),
                                  reads=pk + [KPS(yb), "cvec"], writes=pk)
                stage1(0)
                stage1(1)
                stage2a(0)
                stage1(2)
                stage2a(1)
                stage2b(0)
                stage1(3)
                stage2a(2)
                stage2b(1)
                stage2a(3)
                stage2b(2)
                stage2b(3)
            sb0 = 2 if with_y else 4
            for g in range(4):
                S.add("pe", lambda e, g=g: e.matmul(bank(sb0 + g), lhsT=Btok[0:L, 128 * g:128 * (g + 1)], rhs=xdd[0:L, 512 * g:512 * (g + 1)],
                                                    start=True, stop=True),
                      reads=["Btok", "xdd"], writes=[KPS(sb0 + g)])
            S.add("pool", lambda e: e.tensor_tensor(out=Sf[:].rearrange("p (h q) -> p h q", h=32), in0=Sf[:].rearrange("p (h q) -> p h q", h=32),
                                                    in1=cd.unsqueeze(2).to_broadcast([128, 32, 64]), op=ALU.mult),
                  reads=["Sf", ("sm", 2)], writes=["Sf"])
            for g in range(4):
                S.add("dve", lambda e, g=g: e.tensor_tensor(out=Sf[:, 512 * g:512 * (g + 1)], in0=Sf[:, 512 * g:512 * (g + 1)], in1=bank(sb0 + g), op=ALU.add),
                      reads=["Sf", KPS(sb0 + g)], writes=["Sf"])
            S.add("act", lambda e: e.copy(out=Sb[:], in_=Sf[:]), reads=["Sf"], writes=["Sb"])

        S.barrier()
        dkeys = [("dtP", i) for i in range(8)]
        for c in range(8):
            psm = bank(1)[:, 32 * c:32 * (c + 1)]
            S.add("pe", lambda e, c=c, psm=psm: e.matmul(psm, lhsT=triU, rhs=dtaP[:, c, :], start=True, stop=(c == 7)),
                  reads=["cm"] + dkeys, writes=[KPS(1)])
            for c2 in range(c + 1, 8):
                S.add("pe", lambda e, c2=c2, psm=psm: e.matmul(psm, lhsT=onesf, rhs=dtaP[:, c2, :], start=False, stop=(c2 == 7)),
                      reads=["cm"] + dkeys, writes=[KPS(1)])
        smf = sm[:].rearrange("p c h -> p (c h)")
        smk = [("sm", i) for i in range(8)]
        S.add("act", lambda e: e.activation(out=smf, in_=bank(1)[:, 0:256], func=AF.Exp), reads=[KPS(1)], writes=smk)
        S.add("dve", lambda e: e.tensor_tensor(out=smf, in0=smf, in1=dtP[:].rearrange("p c h -> p (c h)"), op=ALU.mult),
              reads=smk + dkeys, writes=smk)
        xddP = [xdd, xdt]
        BtokP = [Btok, gi[:, 9216:9728]]
        for c in range(8):
            b = c % 2
            c0 = 128 * c
            for ct in range(16):
                S.add("pe", lambda e, ct=ct, c0=c0: e.transpose(out=xtok_ps[:, ct, :], in_=post[ct][:, c0:c0 + 128], identity=identb[:]),
                      reads=ck("post", ct, c0, c0 + 128) + ["identb"], writes=[KPS(2), KPS(3)])
            for g in range(4):
                S.add("pe", lambda e, g=g, c0=c0: e.transpose(out=btok_ps[:, g, :], in_=post[16 + g][:, c0:c0 + 128], identity=identb[:]),
                      reads=ck("post", 16 + g, c0, c0 + 128) + ["identb"], writes=[KPS(0)])
            xt3 = xtok_ps.rearrange("p c t -> p (c t)").rearrange("p (h q) -> p h q", h=32)
            S.add("dve", lambda e, b=b, c=c, xt3=xt3: e.tensor_tensor(out=xddP[b].rearrange("p (h q) -> p h q", h=32), in0=xt3,
                                                                     in1=sm[:, c, :].unsqueeze(2).to_broadcast([128, 32, 64]), op=ALU.mult),
                  reads=[KPS(2), KPS(3), ("sm", c)], writes=[("xddP", b)])
            S.add("act", lambda e, b=b: e.copy(out=BtokP[b], in_=btok_ps.rearrange("p g t -> p (g t)")), reads=[KPS(0)], writes=[("BtokP", b)])
            for g in range(4):
                S.add("pe", lambda e, g=g, b=b, c=c: e.matmul(bank(4 + g), lhsT=BtokP[b][:, 128 * g:128 * (g + 1)], rhs=xddP[b][:, 512 * g:512 * (g + 1)],
                                                              start=(c == 0), stop=(c == 7)),
                      reads=[("BtokP", b), ("xddP", b)], writes=[KPS(4 + g)])
        for g in range(4):
            S.add("dve" if g % 2 == 0 else "act", (lambda e, g=g: e.tensor_copy(out=Sf[:, 512 * g:512 * (g + 1)], in_=bank(4 + g))) if g % 2 == 0 else
                  (lambda e, g=g: e.copy(out=Sf[:, 512 * g:512 * (g + 1)], in_=bank(4 + g))),
                  reads=[KPS(4 + g)], writes=[("Sfq", g)])
        S.add("act", lambda e: e.copy(out=Sb[:], in_=Sf[:]), reads=[("Sfq", g) for g in range(4)] + ["Sf"], writes=["Sb", "Sf"])
        S.barrier()

        SRK = lambda ct, i: ("sraw", ct, i)

        S.add("dve", lambda e: e.memset(sraw.rearrange("p a b c -> p (a b c)"), 0.0),
              writes=[SRK(ct, i) for ct in range(24) for i in range(4)] + [("hnb", 0), ("hnb", 1)])
        S.add("dve", lambda e: e.memset(svraw[:].rearrange("p a b c -> p (a b c)"), 0.0), writes=["svraw"])

        def p3_side():
            rounds = [("x", i, half) for i in range(3) for half in range(2)] + [("v", i, 0) for i in range(2)]

            def load(r):
                kind, i, half = rounds[r]
                b = r % 2
                if kind == "x":
                    S.dma("sp", xin[b][0:16, 0:1536], scv[:, i, 1536 * half:1536 * (half + 1)], writes=[("xin", b)])
                else:
                    S.dma("sp", xin[b][0:16, :], ssc[:, i, :], writes=[("xin", b)])
            load(0)
            yield 1
            for r, (kind, i, half) in enumerate(rounds):
                b = r % 2
                if r + 1 < len(rounds):
                    load(r + 1)
                n = 12 if kind == "x" else 16
                pv = bank(7)[:, 0:16 * n].rearrange("p (j t) -> p j t", j=n)
                for j in range(n):
                    S.add("pe", lambda e, j=j, pv=pv, b=b: e.transpose(out=pv[:, j, :], in_=xin[b][0:16, 128 * j:128 * (j + 1)],
                                                                      identity=identf[0:16, 0:16]),
                          reads=[("xin", b), "cm"], writes=[KPS(7)])
                if kind == "x":
                    S.add("act", lambda e, pv=pv, half=half, i=i: e.copy(out=sraw[:, 12 * half:12 * (half + 1), :, i], in_=pv),
                          reads=[KPS(7)], writes=[SRK(ct, i) for ct in range(12 * half, 12 * half + 12)])
                else:
                    S.add("act", lambda e, pv=pv, i=i: e.copy(out=svraw[:, :, :, i], in_=pv), reads=[KPS(7)], writes=["svraw"])
                yield 1
            S.dma("sp", cssd_smp[:, 0:2, :], scv[:, 1:3, :], writes=[("o", "cssd_smp01")])
            S.dma("sp", csc_smp[:, 0:1, :], ssc[:, 1:2, :], writes=[("o", "csc_smp0")])
            out_keys.extend([("o", "cssd_smp01"), ("o", "csc_smp0")])
            dt_tile(0, 16, dtS[:, 0, :], dtS[:, 1, :], None, "dtS")
            yield 1
            dt_tile(HALO0, 16, dtH[:, 0, :], dtH[:, 1, :], hvec[0:16, 72:73], "dtH")
            yield 1
            for i in range(8):
                dt_tile(OWN0 + 128 * i, 128, dtO[:, i, :], dtaO[:, i, :], None, ("dtO", i), hl=dth_all[:, i, :, :])
                yield 1

        def sconv(ct):
            rb = ct % 2
            sr = sraw[:, ct, :, :].rearrange("p b c -> p (b c)")
            sa = sacc[:, rb, :]
            srk = [SRK(ct, i) for i in range(4)]
            S.add("act", lambda e: e.activation(out=sa[:, 3:64], in_=sr[:, 3:64], func=AF.Identity, bias=cb_(ct), scale=cw(3, ct)),
                  reads=srk + ["cvec"], writes=[("sacc", rb)])
            for i in range(3):
                S.add("dve", lambda e, i=i: e.scalar_tensor_tensor(out=sa[:, 3:64], in0=sr[:, i:i + 61], scalar=cw(i, ct), in1=sa[:, 3:64],
                                                                   op0=ALU.mult, op1=ALU.add),
                      reads=srk + ["cvec", ("sacc", rb)], writes=[("sacc", rb)])
            S.add("act", lambda e: e.activation(out=sxp[:, ct, :], in_=sa.rearrange("p (b c) -> p b c", c=4)[:, :, 3], func=AF.Silu),
                  reads=[("sacc", rb)], writes=[("sxp", ct)])

        jobs = []
        for ct in range(24):
            rb = ct % 2

            def epi(c0, c1, pap, pk, rb=rb, ct=ct):
                lo = max(c0, HALO0)
                S.add("act", lambda e: e.copy(out=raw[rb][:, 3 + lo:3 + c1], in_=pap[:, lo - c0:c1 - c0]), reads=[pk], writes=ck("raw", rb, 3 + lo, 3 + c1))
                if c0 == 0:
                    S.add("act", lambda e: e.copy(out=sraw[:, ct, :, 3], in_=pap[:, 0:16]), reads=[pk], writes=[SRK(ct, 3)])
                    S.add("dve", lambda e: e.tensor_copy(out=raw[rb][:, HALO0:HALO0 + 3], in_=xlastP[:, ct, :]), reads=[("xlastP", ct)],
                          writes=ck("raw", rb, HALO0, HALO0 + 3))

            def fin(rb=rb, ct=ct):
                S.add("dve", lambda e: e.tensor_copy(out=xlastE[:, ct, :], in_=raw[rb][:, NE:NE + 3]),
                      reads=ck("raw", rb, NE, NE + 3), writes=[("xlastE", ct)])
                conv4(rb, ct, HALO0, NE, post[ct])
                if ct >= 4:
                    sconv(ct - 4)
            jobs.append((C_XBC + 128 * ct, BLKS, epi, fin))
        bg[0], bg[1] = p3_side(), 1
        run_jobs(jobs)
        while bg[0] is not None:
            if next(bg[0], "done") == "done":
                bg[0] = None
        for ct in range(20, 24):
            sconv(ct)

        trc = [0]

        def tok_rows(src_fn, ntile, R, dst_dram, okey, half_tiles=8):
            for h0 in range(0, ntile, half_tiles):
                n = min(half_tiles, ntile - h0)
                ab = trc[0] % 2
                trc[0] += 1
                b0 = 4 if ab == 0 else 0
                stg = acc[ab]
                sk = ck("acc", ab, 0, 1024)
                for j in range(n):
                    bi = b0 + j // 4
                    S.add("pe", lambda e, j=j, bi=bi, h0=h0: e.transpose(out=bank(bi)[0:R, 128 * (j % 4):128 * (j % 4 + 1)], in_=src_fn(h0 + j),
                                                                          identity=identf),
                          reads=["cm", okey + "_src"], writes=[KPS(bi)])
                nb = (n + 3) // 4
                for q in range(nb):
                    w = min(4, n - 4 * q) * 128
                    S.add("act", lambda e, q=q, w=w, stg=stg, b0=b0: e.copy(out=stg[0:R, 512 * q:512 * q + w], in_=bank(b0 + q)[0:R, 0:w]),
                          reads=[KPS(b0 + q)], writes=sk)
                S.dma("sp", dst_dram[:, 128 * h0:128 * (h0 + n)], stg[0:R, 0:128 * n], reads=sk, writes=[("o", okey, h0)])
                out_keys.append(("o", okey, h0))

        S.add("dve", lambda e: e.tensor_copy(out=junk[:, 0:1], in_=xlastE[:, 0, 0:1]), reads=[("xlastE", c) for c in range(24)], writes=["cssd_fin_src"])
        tok_rows(lambda ct: xlastE[:, ct, :], 24, 3, cssd_fin, "cssd_fin")
        S.add("dve", lambda e: e.tensor_copy(out=junk[:, 1:2], in_=sraw[:, 0, 0, 0:1]), reads=[SRK(ct, 3) for ct in range(24)], writes=["cssd_smp_src"])
        tok_rows(lambda ct: sraw[:, ct, :, 3], 24, 16, cssd_smp[:, 2, :], "cssd_smp")
        S.barrier()

        ssd_chunk(HALO0, 16, dtH[:, 0, :], dtH[:, 1, :], "dtH", False)
        for i in range(8):
            ssd_chunk(OWN0 + 128 * i, 128, dtO[:, i, :], dtaO[:, i, :], ("dtO", i), True)
        stg3 = stg_f.rearrange("p (j n) -> p j n", j=16)
        for j in range(16):
            bi = 4 + (j // 4) % 2
            S.add("pe", lambda e, j=j, bi=bi: e.transpose(out=bank(bi)[:, 128 * (j % 4):128 * (j % 4 + 1)], in_=Sf[:, 128 * j:128 * (j + 1)], identity=identf),
                  reads=["Sf", "cm"], writes=[KPS(bi)])
            if j % 4 == 3:
                q = j // 4
                S.add("act", lambda e, q=q, bi=bi: e.copy(out=stg_f[:, 512 * q:512 * (q + 1)], in_=bank(bi)), reads=[KPS(bi)],
                      writes=["stg"] + [("CTe", i) for i in range(4)])
        S.dma("sp", ssm_fin.rearrange("(j p) n -> p j n", p=128), stg3, reads=["stg"], writes=[("o", "ssm_fin")])
        out_keys.append(("o", "ssm_fin"))

        S.barrier()
        S.add("act", lambda e: e.activation(out=dtS[:, 1, :], in_=dtS[:, 1, :], func=AF.Exp), reads=["dtS"], writes=["dtS"])
        for q in range(2):
            src = dtS[:, 1, :] if q == 0 else dtS[:, 0, :]
            S.add("pe", lambda e, q=q, src=src: e.transpose(out=bank(7)[0:32, 16 * q:16 * (q + 1)], in_=src, identity=identf[0:16, 0:16]),
                  reads=["dtS", "cm"], writes=[KPS(7)])
        S.add("act", lambda e: e.copy(out=dAT[:].rearrange("h q b -> h (q b)"), in_=bank(7)[0:32, 0:32]), reads=[KPS(7)], writes=["dAT"])
        for q in range(2):
            pv = bank(4 + q)[:, 0:256].rearrange("p (j b) -> p j b", j=16)
            for j in range(16):
                for hp in range(2):
                    h = 2 * j + hp
                    S.add("pe", lambda e, q=q, j=j, hp=hp, h=h, pv=pv: e.matmul(pv[64 * hp:64 * hp + 64, j, :], lhsT=identf[0:32, h:h + 1].to_broadcast([32, 64]),
                                                                                rhs=dAT[:, q, :], start=True, stop=True),
                          reads=["cm", "dAT"], writes=[KPS(4 + q)])
            dst = decx if q == 0 else dtx
            S.add("act", lambda e, dst=dst, pv=pv: e.copy(out=dst[:], in_=pv), reads=[KPS(4 + q)], writes=["decx" if q == 0 else "dtx"])
        S.add("dve", lambda e: e.tensor_tensor(out=xdts[:], in0=sxp[:, 0:16, :], in1=dtx[:], op=ALU.mult),
              reads=["dtx"] + [("sxp", c) for c in range(16)], writes=["xdts"])
        S.add("dve", lambda e: e.memset(ysT[:].rearrange("p a b -> p (a b)"), 0.0), writes=["ysT"])
        bctok = Sb[:].bitcast(F32)
        for q in range(8):
            S.add("pe", lambda e, q=q: e.transpose(out=bank(6 + q // 4)[0:16, 128 * (q % 4):128 * (q % 4 + 1)], in_=sxp[:, 16 + q, :], identity=identf),
                  reads=[("sxp", 16 + q), "cm"], writes=[KPS(6 + q // 4)])
        for q in range(2):
            S.add("act", lambda e, q=q: e.copy(out=bctok[0:16, 512 * q:512 * (q + 1)], in_=bank(6 + q)[0:16, :]), reads=[KPS(6 + q)], writes=["Sb"])
        S.dma("sp", scr, bctok[0:16, :], reads=["Sb"], writes=["scr"])
        sssm_v = sssm.rearrange("b (j p) n -> b p j n", p=128)
        ssm_smp_v = ssm_smp.rearrange("b (j p) n -> b p j n", p=128)
        Bc = [Sf[:, 1024 * k:1024 * (k + 1)] for k in range(2)]

        def stkeys(k):
            return [kk for ct in range(16 + 4 * k, 20 + 4 * k) for kk in ck("post", ct, 0, NE)]

        def smp_load(b):
            k = b % 2
            S.dma("sp", St[k], sssm_v[b], writes=stkeys(k))
            S.dma("sp", Bc[k], scr[b:b + 1, :].to_broadcast([128, 1024]), reads=["scr"], writes=[("Bc", k), "Sf"])

        def sample_gen():
            smp_load(0)
            for b in range(16):
                k = b % 2
                stk = stkeys(k)
                if b + 1 < 16:
                    smp_load(b + 1)
                for j in range(16):
                    g = j // 4
                    S.add("act", lambda e, j=j, b=b, k=k: e.activation(out=St[k][:, j, :], in_=St[k][:, j, :], func=AF.Identity, scale=decx[:, j, b:b + 1]),
                          reads=stk + ["decx"], writes=stk)
                    S.add("dve", lambda e, j=j, b=b, k=k, g=g: e.scalar_tensor_tensor(out=St[k][:, j, :], in0=Bc[k][:, 128 * g:128 * (g + 1)],
                                                                                     scalar=xdts[:, j, b:b + 1], in1=St[k][:, j, :],
                                                                                     op0=ALU.mult, op1=ALU.add),
                          reads=stk + [("Bc", k), "xdts"], writes=stk)
                    S.add("dve", lambda e, j=j, b=b, k=k, g=g: e.scalar_tensor_tensor(out=junk[:], in0=St[k][:, j, :], scalar=1.0,
                                                                                     in1=Bc[k][:, 512 + 128 * g:512 + 128 * (g + 1)], op0=ALU.mult, op1=ALU.mult,
                                                                                     accum_out=ysT[:, j, b:b + 1]),
                          reads=stk + [("Bc", k)], writes=["junk", "ysT"])
                    yield 1
                S.dma("sp", ssm_smp_v[b], St[k], reads=stk, writes=[("o", "ssm_smp", b)])
                out_keys.append(("o", "ssm_smp", b))
        bg[0] = sample_gen()
        bg[1] = 1

        def sample_finish():
            while bg[0] is not None:
                if next(bg[0], "done") == "done":
                    bg[0] = None
            S.add("dve", lambda e: e.tensor_tensor(out=xdts[:], in0=sxp[:, 0:16, :], in1=cvec[:, 200:216].unsqueeze(2).to_broadcast([128, 16, 16]), op=ALU.mult),
                  reads=["xdts", "cvec", "ysT"] + [("sxp", c) for c in range(16)], writes=["xdts"])
            S.add("dve", lambda e: e.tensor_tensor(out=ysT[:], in0=ysT[:], in1=xdts[:], op=ALU.add), reads=["ysT", "xdts"], writes=["ysT"])

        def sample_norm(q, gcol0, kin):
            src = yg_s[:, q, :, :]
            sq = sq16[:]
            S.add("dve", lambda e: e.tensor_tensor(out=sq, in0=src, in1=src, op=ALU.mult), reads=[kin], writes=["sq16"])
            S.add("pe", lambda e: e.matmul(bank(7)[:, 0:256], lhsT=onesf, rhs=sq.rearrange("p a b -> p (a b)"), start=True, stop=True),
                  reads=["sq16", "cm"], writes=[KPS(7)])
            r = rs16[:, q, :]
            S.add("dve", lambda e: e.tensor_reduce(out=r, in_=bank(7)[:, 0:256].rearrange("p (j b) -> p b j", j=16), axis=AX.X, op=ALU.add),
                  reads=[KPS(7)], writes=[("rs16", q)])
            S.add("dve", lambda e: e.tensor_scalar(out=r, in0=r, scalar1=1.0 / D, scalar2=EPS, op0=ALU.mult, op1=ALU.add),
                  reads=[("rs16", q)], writes=[("rs16", q)])
            rsqrt_ip(r, [("rs16", q)])
            S.add("dve", lambda e: e.tensor_tensor(out=sq, in0=src, in1=cvec[:, gcol0:gcol0 + 16].unsqueeze(2).to_broadcast([128, 16, 16]), op=ALU.mult),
                  reads=[kin, "cvec", "sq16"], writes=["sq16"])
            S.add("dve", lambda e: e.tensor_tensor(out=ysn[:, 16 * q:16 * q + 16, :], in0=sq, in1=r.unsqueeze(1).to_broadcast([128, 16, 16]), op=ALU.mult),
                  reads=["sq16", ("rs16", q)], writes=[("ysn", q)])

        bank_pool[0] = [0, 1, 2, 3, 4]
        sqc = [0]

        def sumsq_block(src_ap, bidx, ncols, first, last, rkeys):
            bi = 5 + bidx
            q = sqc[0] % 2
            sqc[0] += 1
            S.add("act", lambda e: e.activation(out=sqb[q][:, 0:ncols], in_=src_ap, func=AF.Square), reads=rkeys, writes=[("sqb", q)])
            pe_deferred.append(lambda: S.add("pe", lambda e: e.matmul(bank(bi)[:, 0:ncols], lhsT=onesb[:], rhs=sqb[q][:, 0:ncols], start=first, stop=last),
                                             reads=[("sqb", q), "onesb"], writes=[KPS(bi)]))

        def finish_norm(dst_list, gcol0):
            rst = acc[1]
            for bidx, (c0, c1) in enumerate(BLKS):
                lo = max(c0, OWN0)
                S.add("dve", lambda e, bidx=bidx, lo=lo, c1=c1: e.tensor_scalar(out=rst[:, lo:c1], in0=bank(5 + bidx)[:, 0:c1 - lo], scalar1=1.0 / D, scalar2=EPS,
                                                                                op0=ALU.mult, op1=ALU.add),
                      reads=[KPS(5 + bidx)], writes=ck("acc", 1, lo, c1))
                rsqrt_ip(rst[:, lo:c1], ck("acc", 1, lo, c1))
            for j, (dst, dkeys) in enumerate(dst_list):
                S.add("dve", lambda e, j=j, dst=dst: e.scalar_tensor_tensor(out=dst[:, OWN0:NE], in0=dst[:, OWN0:NE], scalar=cvec[:, gcol0 + j:gcol0 + j + 1],
                                                                            in1=rst[:, OWN0:NE], op0=ALU.mult, op1=ALU.mult),
                      reads=dkeys + ck("acc", 1, OWN0, NE) + ["cvec"], writes=dkeys)

        jobs = []
        for j in range(16):
            def epi(c0, c1, pap, pk, j=j):
                if c0 == 0:
                    S.add("act", lambda e: e.activation(out=szs[:, j, :], in_=pap[:, 0:16], func=AF.Silu), reads=[pk], writes=[("szs", j)])
                lo = max(c0, OWN0)
                bidx = [b[0] for b in BLKS].index(c0)
                ka = ck("acc", 0, lo, c1)
                S.add("act", lambda e: e.activation(out=acc[0][:, lo:c1], in_=pap[:, lo - c0:c1 - c0], func=AF.Silu), reads=[pk], writes=ka)
                pk2 = ck("post", j, lo, c1)
                S.add("dve", lambda e: e.tensor_tensor(out=post[j][:, lo:c1], in0=post[j][:, lo:c1], in1=acc[0][:, lo:c1], op=ALU.mult),
                      reads=pk2 + ka, writes=pk2)
                sumsq_block(post[j][:, lo:c1], bidx, c1 - lo, j == 0, j == 15, pk2)
            jobs.append((C_Z + 128 * j, BLKS, epi, None))
        run_jobs(jobs)
        finish_norm([(post[j], ck("post", j, OWN0, NE)) for j in range(16)], 120)

        jobs = []
        for f in range(16):
            def cwsc(i, f=f):
                return cvec[:, 136 + 16 * i + f:137 + 16 * i + f]

            def epi_c(c0, c1, pap, pk):
                S.add("act", lambda e: e.copy(out=raw[0][:, 3 + c0:3 + c1], in_=pap), reads=[pk], writes=ck("raw", 0, 3 + c0, 3 + c1))

            def epi_h(c0, c1, pap, pk, f=f):
                S.add("dve", lambda e: e.tensor_tensor(out=raw[1][:, 3 + c0:3 + c1], in0=pap, in1=raw[0][:, 3 + c0:3 + c1], op=ALU.mult),
                      reads=[pk] + ck("raw", 0, 3 + c0, 3 + c1), writes=ck("raw", 1, 3 + c0, 3 + c1))
                if c0 == 0:
                    S.add("act", lambda e: e.copy(out=svraw[:, f, :, 2], in_=raw[1][:, 3:19]), reads=ck("raw", 1, 3, 19), writes=["svraw"])

            def fin_h(f=f, cwsc=cwsc):
                lo, hi = HALO0, NE
                ka = ck("acc", 0, lo, hi)
                S.add("act", lambda e: e.activation(out=acc[0][:, lo:hi], in_=raw[1][:, 3 + lo:3 + hi], func=AF.Identity, scale=cwsc(2)),
                      reads=ck("raw", 1, 3 + lo, 3 + hi) + ["cvec"], writes=ka)
                for i in (1, 0):
                    S.add("dve", lambda e, i=i: e.scalar_tensor_tensor(out=acc[0][:, lo:hi], in0=raw[1][:, 1 + i + lo:1 + i + hi], scalar=cwsc(i),
                                                                       in1=acc[0][:, lo:hi], op0=ALU.mult, op1=ALU.add),
                          reads=ck("raw", 1, 1 + i + lo, 1 + i + hi) + ["cvec"] + ka, writes=ka)
                S.add("dve", lambda e: e.tensor_copy(out=vlast[:, f, :], in_=raw[1][:, NE + 1:NE + 3]), reads=ck("raw", 1, NE + 1, NE + 3),
                      writes=[("vlast", f)])
                sv = svraw[:, f, :, :].rearrange("p b c -> p (b c)")
                S.add("act", lambda e: e.activation(out=svacc[:, 2:48], in_=sv[:, 2:48], func=AF.Identity, scale=cwsc(2)),
                      reads=["svraw", "cvec"], writes=["svacc"])
                for i in (1, 0):
                    S.add("dve", lambda e, i=i: e.scalar_tensor_tensor(out=svacc[:, 2:48], in0=sv[:, i:i + 46], scalar=cwsc(i), in1=svacc[:, 2:48],
                                                                       op0=ALU.mult, op1=ALU.add),
                          reads=["svraw", "cvec", "svacc"], writes=["svacc"])

            def epi_b(c0, c1, pap, pk, f=f):
                ka = ck("acc", 0, c0, c1)
                S.add("dve", lambda e: e.tensor_tensor(out=acc[0][:, c0:c1], in0=pap, in1=acc[0][:, c0:c1], op=ALU.mult), reads=[pk] + ka, writes=ka)
                if c0 == 0:
                    S.add("dve", lambda e: e.tensor_tensor(out=tsb[:, f, :], in0=pap[:, 0:16], in1=svacc[:].rearrange("p (b c) -> p b c", c=3)[:, :, 2],
                                                           op=ALU.mult),
                          reads=[pk, "svacc"], writes=[("tsbf", f)])

            def epi_z(c0, c1, pap, pk, f=f):
                ka1 = ck("acc", 1, c0, c1)
                S.add("act", lambda e: e.activation(out=acc[1][:, c0:c1], in_=pap, func=AF.Silu), reads=[pk], writes=ka1)
                yk = ck("ysc", f, c0, c1)
                S.add("dve", lambda e: e.tensor_tensor(out=ysc[f][:, c0:c1], in0=acc[0][:, c0:c1], in1=acc[1][:, c0:c1], op=ALU.mult),
                      reads=ck("acc", 0, c0, c1) + ka1, writes=yk)
                if c0 == 0:
                    S.add("dve", lambda e: e.tensor_tensor(out=yg_s[:, 1, f, :], in0=tsb[:, f, :], in1=acc[1][:, 0:16], op=ALU.mult),
                          reads=[("tsbf", f)] + ka1, writes=[("yg1", f)])
                lo = max(c0, OWN0)
                bidx = [b[0] for b in BLKS].index(c0)
                sumsq_block(ysc[f][:, lo:c1], bidx, c1 - lo, f == 0, f == 15, yk)
            blks = BLKS
            jobs.append((C_CSC + 128 * f, blks, epi_c, None))
            jobs.append((C_HSC + 128 * f, blks, epi_h, fin_h))
            jobs.append((C_BSC + 128 * f, blks, epi_b, None))
            jobs.append((C_ZSC + 128 * f, blks, epi_z, None))
        run_jobs(jobs)
        wo_pre = {}
        for dc in range(2):
            wk = [("W", 4 * dc + q) for q in range(4)]
            wo_pre[dc] = S.dma("pool", Wout[dc][:], w_out_v[:, :, 256 * dc:256 * (dc + 1)], writes=wk)
        while bg[0] is not None:
            if next(bg[0], "done") == "done":
                bg[0] = None
        def oacc_alias(ti):
            k = ck("hnT", 0, 0, NE) + ck("hnTa", 0, 0, NE)
            if ti == 4:
                k = k + ck("raw", 0, 0, 1064) + ck("raw", 1, 0, 1064)
            return k
        tiles0 = [("o", 128, xs[1040 + 128 * i:1040 + 128 * (i + 1), :], y_own[128 * i:128 * (i + 1), :], i) for i in range(4)] + \
                 [("s", 16, xsm, y_smp, None)]
        for ti, (kind, R, usrc, ydst, i) in enumerate(tiles0):
            S.dma("sp", out_acc[0:R, ti, :], usrc, writes=[("oacc", ti)] + oacc_alias(ti))
        finish_norm([(ysc[f], ck("ysc", f, OWN0, NE)) for f in range(16)], 184)
        def late_p7():
            S.add("dve", lambda e: e.tensor_copy(out=junk[:, 2:3], in_=yg_s[:, 1, 0, 0:1]), reads=[("yg1", f) for f in range(16)], writes=["yg1"])
            sample_norm(1, 184, "yg1")
            sample_finish()
            S.add("dve", lambda e: e.tensor_tensor(out=yg_s[:, 0, :, :], in0=ysT[:], in1=szs[:], op=ALU.mult),
                  reads=["ysT"] + [("szs", j) for j in range(16)], writes=["yg0"])
            sample_norm(0, 120, "yg0")
            S.add("dve", lambda e: e.tensor_copy(out=junk[:, 3:4], in_=vlast[:, 0, 0:1]), reads=[("vlast", f) for f in range(16)], writes=["csc_fin_src"])
            S.add("dve", lambda e: e.tensor_copy(out=junk[:, 4:5], in_=svraw[:, 0, 0, 0:1]), reads=["svraw"], writes=["csc_smp_src"])

            stg2 = Sb[:].bitcast(F32)
            for (src_fn, R, dst, okey) in ((lambda f: vlast[:, f, :], 2, csc_fin, "csc_fin"), (lambda f: svraw[:, f, :, 2], 16, csc_smp[:, 1, :], "csc_smp")):
                for h0 in (0, 8):
                    for j in range(8):
                        bi = 6 + j // 4
                        S.add("pe", lambda e, j=j, bi=bi, h0=h0, src_fn=src_fn, R=R: e.transpose(out=bank(bi)[0:R, 128 * (j % 4):128 * (j % 4 + 1)],
                                                                                                 in_=src_fn(h0 + j), identity=identf),
                              reads=["cm", okey + "_src"], writes=[KPS(bi)])
                    for q in range(2):
                        S.add("act", lambda e, q=q, R=R: e.copy(out=stg2[0:R, 512 * q:512 * (q + 1)], in_=bank(6 + q)[0:R, :]), reads=[KPS(6 + q)], writes=["stg2", "Sb"])
                    S.dma("sp", dst[:, 128 * h0:128 * (h0 + 8)], stg2[0:R, 0:1024], reads=["stg2"], writes=[("o", okey, h0)])
                    out_keys.append(("o", okey, h0))

        gfbc = Sf
        S.dma("sp", gfbc[:], fnormw_bc, writes=["Sf", ("Bc", 0), ("Bc", 1)])
        jrow = St[0].rearrange("p j n -> p (j n)")

        def junk_row(R):
            return jrow[0:R, :]
        wo_ctr = [0]
        for hp in range(2):
            tiles = [("o", 128, xs[1040 + 128 * i:1040 + 128 * (i + 1), :], y_own[128 * i:128 * (i + 1), :], i) for i in range(4 * hp, 4 * hp + 4)] + \
                    ([("s", 16, xsm, y_smp, None)] if hp == 0 else [])
            for ti, (kind, R, usrc, ydst, i) in enumerate(tiles):
                if hp == 1:
                    S.dma("sp", out_acc[0:R, ti, :], usrc, writes=[("oacc", ti)] + oacc_alias(ti))
            for dc in range(8):
                s = wo_ctr[0] % 2
                wo_ctr[0] += 1
                wk = [("W", 4 * s + q) for q in range(4)]
                if not (hp == 0 and dc in wo_pre):
                    S.dma("pool", Wout[s][:], w_out_v[:, :, 256 * dc:256 * (dc + 1)], writes=wk)
                for ti, (kind, R, usrc, ydst, i) in enumerate(tiles):
                    if kind == "s" and dc == 0:
                        late_p7()
                    bi = bctr[0] % 6
                    bctr[0] += 1
                    pap = bank(bi)[0:R, 0:256]
                    for kt in range(32):
                        if kind == "s":
                            lhsT = ysn[:, kt, :]
                            rk = [("ysn", kt // 16)]
                        else:
                            src = post[kt] if kt < 16 else ysc[kt - 16]
                            c0 = OWN0 + 128 * i
                            lhsT = src[:, c0:c0 + 128]
                            rk = ck("post", kt, c0, c0 + 128) if kt < 16 else ck("ysc", kt - 16, c0, c0 + 128)
                        S.add("pe", lambda e, kt=kt, lhsT=lhsT, pap=pap, s=s: e.matmul(pap, lhsT=lhsT, rhs=Wout[s][:, kt, :], start=(kt == 0), stop=(kt == 31)),
                              reads=wk + rk, writes=[KPS(bi)])
                    oa = out_acc[0:R, ti, 256 * dc:256 * (dc + 1)]
                    S.add("dve", lambda e, oa=oa, pap=pap: e.tensor_tensor(out=oa, in0=oa, in1=pap, op=ALU.add), reads=[KPS(bi), ("oacc", ti)], writes=[("oacc", ti)])
            for ti, (kind, R, usrc, ydst, i) in enumerate(tiles):
                b = ti % 2
                sscol = ss[0:R, 2 + b:3 + b]
                S.add("dve", lambda e, sscol=sscol: e.memset(sscol, 0.0), writes=[("ss", 2 + b)])
                S.add("act", lambda e, ti=ti, R=R, sscol=sscol: e.activation(out=junk_row(R), in_=out_acc[0:R, ti, :], func=AF.Square, accum_out=sscol),
                      reads=[("oacc", ti)], writes=[("ss", 2 + b), "jrow"] + stkeys(0))
                S.add("dve", lambda e, sscol=sscol: e.tensor_scalar(out=sscol, in0=sscol, scalar1=1.0 / D, scalar2=EPS, op0=ALU.mult, op1=ALU.add),
                      reads=[("ss", 2 + b)], writes=[("ss", 2 + b)])
                rsqrt_ip(sscol, [("ss", 2 + b)])
                S.add("dve", lambda e, ti=ti, R=R, sscol=sscol: e.scalar_tensor_tensor(out=out_acc[0:R, ti, :], in0=out_acc[0:R, ti, :], scalar=sscol,
                                                                                        in1=gfbc[0:R, :], op0=ALU.mult, op1=ALU.mult),
                      reads=[("oacc", ti), ("ss", 2 + b), "Sf"], writes=[("oacc", ti)])
                ok = ("o", "y", hp, ti)
                S.dma("sp", ydst, out_acc[0:R, ti, :], reads=[("oacc", ti)], writes=[ok])
                out_keys.append(ok)
        S.flush(final_keys=out_keys)
    return nc, dbg_outs


_CACHE = {}


def _consts():
    ident = np.eye(128, dtype=np.float32)
    triL = np.triu(np.ones((128, 128), np.float32))
    triU = np.tril(np.ones((128, 128), np.float32), -1)
    ones = np.ones((128, 128), np.float32)
    return np.concatenate([ident, triL, triU, ones], axis=1)


def kernel(x_prompt, x_sample, state_ssm, state_ssd_conv, state_short_conv, meta_tokens, norm_w, w_in,
           conv_ssd_w, conv_ssd_b, dt_bias, a_log, d_skip, ssd_norm_w, conv_sc_w, sc_norm_w, w_out, final_norm_w,
           _debug=False):
    f = lambda a: np.ascontiguousarray(np.asarray(a, dtype=np.float32))
    x_prompt, x_sample, state_ssm, state_ssd_conv, state_short_conv = map(f, (x_prompt, x_sample, state_ssm, state_ssd_conv, state_short_conv))
    meta_tokens, norm_w, w_in, conv_ssd_w, conv_ssd_b = map(f, (meta_tokens, norm_w, w_in, conv_ssd_w, conv_ssd_b))
    dt_bias, a_log, d_skip, ssd_norm_w, conv_sc_w, sc_norm_w, w_out, final_norm_w = map(
        f, (dt_bias, a_log, d_skip, ssd_norm_w, conv_sc_w, sc_norm_w, w_out, final_norm_w))
    key = ("nc", bool(_debug))
    if key not in _CACHE:
        _CACHE[key] = build(debug=_debug)
    nc, dbg_outs = _CACHE[key]

    def colmajor(v):
        return v.reshape(-1, 128).T

    cvec = np.zeros((128, 216), np.float32)
    for i in range(4):
        cvec[:, 24 * i:24 * (i + 1)] = colmajor(conv_ssd_w[0, i])
    cvec[:, 96:120] = colmajor(conv_ssd_b[0])
    cvec[:, 120:136] = colmajor(ssd_norm_w[0])
    for i in range(3):
        cvec[:, 136 + 16 * i:152 + 16 * i] = colmajor(conv_sc_w[0, i])
    cvec[:, 184:200] = colmajor(sc_norm_w[0])
    cvec[:, 200:216] = colmajor(np.repeat(d_skip[0], 64))
    cmat = _consts()
    normw_bc = np.ascontiguousarray(np.broadcast_to(norm_w[0][None, :], (128, D)))
    fnormw_bc = np.ascontiguousarray(np.broadcast_to(final_norm_w[None, :], (128, D)))
    w_in0, w_out0 = w_in[0], w_out[0]
    in_maps = []
    for c in range(8):
        b, half = c // 2, c % 2
        if half == 0:
            stream = np.concatenate([np.zeros((1024, D), np.float32), meta_tokens, x_prompt[b, 0:1024]], axis=0)
        else:
            stream = np.concatenate([meta_tokens, x_prompt[b]], axis=0)
        hvec = np.zeros((128, 128), np.float32)
        hvec[:, 0:32] = dt_bias[0][None, :]
        hvec[:, 32:64] = a_log[0][None, :]
        hvec[:, 64:72] = float(half)
        hvec[:, 72] = 1.0
        sl = slice(16 * c, 16 * (c + 1))
        in_maps.append(dict(
            xs=np.ascontiguousarray(stream), xsm=np.ascontiguousarray(x_sample[sl, 0, :]),
            sssm=np.ascontiguousarray(state_ssm[0, sl].reshape(16, 2048, 128)),
            scv=np.ascontiguousarray(state_ssd_conv[0, sl]), ssc=np.ascontiguousarray(state_short_conv[0, sl]),
            w_in=w_in0, w_out=w_out0, normw_bc=normw_bc, fnormw_bc=fnormw_bc, cvec=cvec, hvec=hvec, cmat=cmat))
    res = run_bass_kernel_spmd(nc, in_maps, core_ids=list(range(8)))
    R = res.results
    y_prompt = np.empty((4, 2048, D), np.float32)
    y_sample = np.empty((128, 1, D), np.float32)
    ssm_p = np.empty((1, 4, 32, 64, 128), np.float32)
    cssd_p = np.empty((1, 4, 3, 3072), np.float32)
    csc_p = np.empty((1, 4, 2, 2048), np.float32)
    ssm_s = np.empty((1, 128, 32, 64, 128), np.float32)
    cssd_s = np.empty((1, 128, 3, 3072), np.float32)
    csc_s = np.empty((1, 128, 2, 2048), np.float32)
    for c in range(8):
        b, half = c // 2, c % 2
        r = R[c]
        y_prompt[b, 1024 * half:1024 * (half + 1)] = r["y_own"]
        sl = slice(16 * c, 16 * (c + 1))
        y_sample[sl, 0, :] = r["y_smp"]
        ssm_s[0, sl] = r["ssm_smp"].reshape(16, 32, 64, 128)
        cssd_s[0, sl] = r["cssd_smp"]
        csc_s[0, sl] = r["csc_smp"]
        if half == 1:
            ssm_p[0, b] = r["ssm_fin"].reshape(32, 64, 128)
            cssd_p[0, b] = r["cssd_fin"]
            csc_p[0, b] = r["csc_fin"]
    if _debug:
        kernel.last_debug = [{k: R[c]["dbg_" + k] for k in dbg_outs} for c in range(8)]
    return (y_prompt, y_sample, ssm_p, cssd_p, csc_p, ssm_s, cssd_s, csc_s)
```

```python
import numpy as np
from contextlib import ExitStack
import concourse.bass as bass
import concourse.mybir as mybir
from concourse.bass_utils import run_bass_kernel_spmd

F32 = mybir.dt.float32
BF16 = mybir.dt.bfloat16
AF = mybir.ActivationFunctionType
ALU = mybir.AluOpType
AX = mybir.AxisListType

D = 2048
NE = 1056
OWN0 = 32
HALO0 = 16
EPS = 1e-5
DIN = 13344
C_Z, C_XBC, C_DT, C_ZSC, C_BSC, C_CSC, C_HSC = 0, 2048, 5120, 5152, 7200, 9248, 11296
BLKS = [(0, 352), (352, 704), (704, 1056)]
ENGS = ("pe", "act", "dve", "pool", "sp")
N_DMA_SEMS = 24
NWS = 8


class _Op:
    __slots__ = ("eng", "emit", "deps", "dma", "signal", "ticket", "sem", "idx")

    def __init__(self, eng, emit, deps, dma, idx):
        self.eng, self.emit, self.deps, self.dma, self.idx = eng, emit, deps, dma, idx
        self.signal = False
        self.ticket = None
        self.sem = None


class Sched:
    def __init__(self, nc, sems):
        self.nc, self.sems = nc, sems
        self.ops = []
        self.last_writer = {}
        self.readers = {}
        self.dma_last = [None] * N_DMA_SEMS
        self.dma_rr = {"sp": 0, "pool": 0}
        self.bar = None
        self.since_bar = []

    def add(self, eng, emit, reads=(), writes=(), dma=False):
        idx = len(self.ops)
        deps = set()
        for k in reads:
            w = self.last_writer.get(k)
            if w is not None:
                deps.add(w)
        for k in writes:
            w = self.last_writer.get(k)
            if w is not None:
                deps.add(w)
            for r in self.readers.get(k, ()):
                deps.add(r)
        for k in writes:
            self.last_writer[k] = idx
            self.readers[k] = []
        for k in reads:
            self.readers.setdefault(k, []).append(idx)
        if self.bar is not None:
            deps.add(self.bar)
        deps.discard(idx)
        op = _Op(eng, emit, deps, dma, idx)
        if dma:
            r = self.dma_rr[eng]
            self.dma_rr[eng] = r + 1
            s = (r % 16) if eng == "sp" else 16 + (r % (N_DMA_SEMS - 16))
            op.sem = s
            if self.dma_last[s] is not None:
                op.deps.add(self.dma_last[s])
            self.dma_last[s] = idx
        self.ops.append(op)
        self.since_bar.append(idx)
        return idx

    def dma(self, eng, out, in_, reads=(), writes=(), **kw):
        return self.add(eng, lambda e: e.dma_start(out=out, in_=in_, **kw), reads, writes, dma=True)

    def barrier(self):
        idx = len(self.ops)
        last = {}
        for i in self.since_bar:
            o = self.ops[i]
            last[("dma", o.sem) if o.dma else ("eng", o.eng)] = i
        deps = set(last.values())
        if self.bar is not None:
            deps.add(self.bar)
        op = _Op("sp", lambda e: e.nop(), deps, False, idx)
        self.ops.append(op)
        self.bar = idx
        self.since_bar = []
        return idx

    def flush(self, final_keys=()):
        nc, ops = self.nc, self.ops
        for op in ops:
            if op.eng == "pe" and not op.dma:
                op.deps = {d for d in op.deps if not (ops[d].eng == "pe" and not ops[d].dma)}
        fin = set()
        for k in final_keys:
            w = self.last_writer.get(k)
            if w is not None:
                fin.add(w)
        for op in ops:
            for d in op.deps:
                ops[d].signal = True
        for d in fin:
            ops[d].signal = True
        eng_count = {e: 0 for e in ENGS}
        dma_count = [0] * N_DMA_SEMS
        for op in ops:
            if op.dma:
                dma_count[op.sem] += 16
                op.ticket = dma_count[op.sem]
            elif op.signal:
                eng_count[op.eng] += 1
                op.ticket = eng_count[op.eng]
        per_eng = {e: [] for e in ENGS}
        for op in ops:
            per_eng[op.eng].append(op)
        sems = self.sems

        def semkey(d):
            return ("dma", d.sem) if d.dma else ("eng", d.eng)

        def getsem(k):
            return sems["dma%d" % k[1]] if k[0] == "dma" else sems[k[1]]

        def run(engname, eng):
            seen = {}
            for op in per_eng[engname]:
                need = {}
                for di in op.deps:
                    d = ops[di]
                    k = semkey(d)
                    if d.ticket > need.get(k, 0):
                        need[k] = d.ticket
                for k, t in need.items():
                    if seen.get(k, 0) >= t:
                        continue
                    seen[k] = t
                    eng.wait_ge(getsem(k), t)
                ins = op.emit(eng)
                if op.dma:
                    ins.then_inc(sems["dma%d" % op.sem], 16)
                elif op.signal:
                    ins.then_inc(sems[op.eng], 1)
            if engname == "sp":
                need = {}
                for di in fin:
                    d = ops[di]
                    k = semkey(d)
                    need[k] = max(need.get(k, 0), d.ticket)
                for k, t in need.items():
                    eng.wait_ge(getsem(k), t)

        with nc.Block() as block:
            @block.tensor
            def _(e):
                run("pe", e)

            @block.scalar
            def _(e):
                run("act", e)

            @block.vector
            def _(e):
                run("dve", e)

            @block.gpsimd
            def _(e):
                run("pool", e)

            @block.sync
            def _(e):
                run("sp", e)


def ck(name, idx, c0, c1, gran=128):
    return [(name, idx, q) for q in range(c0 // gran, (c1 - 1) // gran + 1)]


def build(debug=False):
    nc = bass.Bass("TRN2", target_bir_lowering=False)

    def din(name, shape):
        return nc.dram_tensor(name, shape, F32, kind="ExternalInput").ap()

    def dout(name, shape):
        return nc.dram_tensor(name, shape, F32, kind="ExternalOutput").ap()

    xs = din("xs", [2064, D])
    xsm = din("xsm", [16, D])
    sssm = din("sssm", [16, 2048, 128])
    scv = din("scv", [16, 3, 3072])
    ssc = din("ssc", [16, 2, 2048])
    w_in = din("w_in", [D, DIN])
    w_out = din("w_out", [2 * D, D])
    normw_bc = din("normw_bc", [128, D])
    fnormw_bc = din("fnormw_bc", [128, D])
    cvec_d = din("cvec", [128, 216])
    hvec_d = din("hvec", [128, 128])
    cmat_d = din("cmat", [128, 512])

    y_own = dout("y_own", [1024, D])
    y_smp = dout("y_smp", [16, D])
    ssm_fin = dout("ssm_fin", [2048, 128])
    cssd_fin = dout("cssd_fin", [3, 3072])
    csc_fin = dout("csc_fin", [2, 2048])
    ssm_smp = dout("ssm_smp", [16, 2048, 128])
    cssd_smp = dout("cssd_smp", [16, 3, 3072])
    csc_smp = dout("csc_smp", [16, 2, 2048])
    scr = nc.dram_tensor("scr", [16, 1024], F32).ap()
    out_keys = []
    dbg_outs = {}

    with ExitStack() as es:
        E = es.enter_context
        sems = {n: E(nc.semaphore(n)) for n in ("pe", "act", "dve", "pool", "sp")}
        for i in range(N_DMA_SEMS):
            sems["dma%d" % i] = E(nc.semaphore("dma%d" % i))

        def sb(name, shape, dt):
            return E(nc.sbuf_tensor("s_" + name, shape, dt))

        big = sb("big", [128, 25408], BF16)
        hnT = big[:, 0:16896].rearrange("p (k t) -> p k t", k=16)
        stage_f = big[:, 16896:25408].bitcast(F32)
        raw = [stage_f[:, 0:1064], stage_f[:, 1064:2128]]
        acc = [stage_f[:, 2128:3192], stage_f[:, 3192:4256]]
        out_acc = big[:, 0:20480].bitcast(F32).rearrange("p (t d) -> p t d", t=5)
        Wp = sb("Wp", [128, 16384], BF16)
        Wslot = [Wp[:, 2048 * s:2048 * (s + 1)].rearrange("p (k c) -> p k c", k=16) for s in range(NWS)]
        Wout = [Wp[:, 8192 * s:8192 * (s + 1)].rearrange("p (k c) -> p k c", k=32) for s in range(2)]
        post_t = sb("post", [128, 24, NE], BF16)
        post = [post_t[:, ct, :] for ct in range(24)]
        gi = sb("gi", [128, 17152], BF16)
        gbc = gi[:, 0:4096].bitcast(F32)
        xin = [gi[:, 4096 + 4096 * b:8192 + 4096 * b].bitcast(F32) for b in range(2)]
        hnb = [gi[:, 12288 + 2048 * b:14336 + 2048 * b] for b in range(2)]
        A_ = [[gi[:, 2048 * b + 1024 * hl:2048 * b + 1024 * (hl + 1)].rearrange("p (r l) -> p r l", r=8) for hl in range(2)] for b in range(2)]
        decT = [gi[:, 4096 + 1024 * b:5120 + 1024 * b].rearrange("p (r l) -> p r l", r=8) for b in range(2)]
        MT = [gi[:, 6144 + 1024 * b:7168 + 1024 * b].rearrange("p (r l) -> p r l", r=8) for b in range(2)]
        CTe = [gi[:, 512 * b:512 * (b + 1)].rearrange("p (r l) -> p r l", r=4) for b in range(4)]
        CBm = gi[:, 9216:9728].rearrange("p (g l) -> p g l", g=4)
        xdt = gi[:, 9728:11776]
        xdd = gi[:, 11776:13824]
        Btok = gi[:, 13824:14336]
        stg_f = gi[:, 0:4096].bitcast(F32)
        ysc_t = gi[:, 0:16896].rearrange("p (f t) -> p f t", f=16)
        ysc = [ysc_t[:, f, :] for f in range(16)]
        Sf = sb("Sf", [128, 2048], F32)
        Sb = sb("Sb", [128, 2048], BF16)
        Bsb = [Sf[:, 512 * b:512 * (b + 1)] for b in range(2)]
        Csb = [Sf[:, 1024 + 512 * b:1536 + 512 * b] for b in range(2)]
        St = [post_t[:, 16 + 4 * k:20 + 4 * k, :].rearrange("p a b -> p (a b)").bitcast(F32)[:, 0:2048]
              .rearrange("p (j n) -> p j n", j=16) for k in range(2)]
        cm = sb("cm", [128, 512], F32)
        identf, triL, triU, onesf = cm[:, 0:128], cm[:, 128:256], cm[:, 256:384], cm[:, 384:512]
        identb = sb("identb", [128, 128], BF16)
        onesb = sb("onesb", [128, 128], BF16)
        triLb = sb("triLb", [128, 128], BF16)
        triUb = sb("triUb", [128, 128], BF16)
        dth_all = sb("dth_all", [128, 8, 2, 32], BF16)
        dtf = sb("dtf", [128, 32], F32)
        penb = sb("penb", [128, 128], BF16)
        negacs = sb("negacs", [128, 32], F32)
        sq16 = sb("sq16", [128, 16, 16], F32)
        cvec = sb("cvec", [128, 216], F32)
        hvec = sb("hvec", [128, 128], F32)
        dtb_bc, alog_bc = hvec[:, 0:32], hvec[:, 32:64]
        a_bc = sb("a_bc", [128, 32], F32)
        wdt = sb("wdt", [128, 16, 32], BF16)
        dtP = sb("dtP", [128, 8, 32], F32)
        dtaP = sb("dtaP", [128, 8, 32], F32)
        dtO, dtaO = dtP, dtaP
        dtH = sb("dtH", [16, 2, 32], F32)
        dtS = sb("dtS", [16, 2, 32], F32)
        sm = sb("sm", [128, 8, 32], F32)
        eacs = sb("eacs", [128, 32], BF16)
        eacsT = sb("eacsT", [32, 128], BF16)
        ss = sb("ss", [128, 8], F32)
        sraw = gi[:, 12288:15360].bitcast(F32).rearrange("p (a b c) -> p a b c", a=24, b=16)
        sacc = sb("sacc", [128, 2, 64], F32)
        sxp = sb("sxp", [128, 24, 16], F32)
        svraw = sb("svraw", [128, 16, 16, 3], F32)
        svacc = sb("svacc", [128, 48], F32)
        tsb = sb("tsb", [128, 16, 16], F32)
        xlastP = sb("xlastP", [128, 24, 3], F32)
        xlastE = sb("xlastE", [128, 24, 3], F32)
        vlast = sb("vlast", [128, 16, 2], F32)
        szs = sb("szs", [128, 16, 16], F32)
        sqb = [sb("sqb%d" % b, [128, 512], BF16) for b in range(2)]
        dAT = sb("dAT", [32, 2, 16], F32)
        decx = sb("decx", [128, 16, 16], F32)
        dtx = sb("dtx", [128, 16, 16], F32)
        xdts = sb("xdts", [128, 16, 16], F32)
        ysT = sb("ysT", [128, 16, 16], F32)
        yg_s = sb("yg_s", [128, 2, 16, 16], F32)
        ysn = sb("ysn", [128, 32, 16], BF16)
        junk = sb("junk", [128, 128], F32)
        rs16 = sb("rs16", [128, 2, 16], F32)
        ps = E(nc.psum_tensor("ps", [128, 4096], F32))

        def bank(i):
            return ps[:, 512 * i:512 * (i + 1)]

        def bank_bf(i):
            return ps[:, 512 * i:512 * (i + 1)].bitcast(BF16)

        S = Sched(nc, sems)
        KPS = lambda i: ("ps", i)

        def dbg(name, ap, shape, reads):
            if not debug:
                return
            d = dout("dbg_" + name, list(shape))
            dbg_outs[name] = shape
            S.dma("sp", d, ap, reads=reads, writes=[("dbg", name)])
            out_keys.append(("dbg", name))

        def rsqrt_ip(ap, keys):
            S.add("act", lambda e: e.activation(out=ap, in_=ap, func=AF.Sqrt), reads=keys, writes=keys)
            S.add("dve", lambda e: e.reciprocal(out=ap, in_=ap), reads=keys, writes=keys)

        S.dma("sp", cm[:], cmat_d, writes=["cm"])
        S.dma("sp", cvec[:], cvec_d, writes=["cvec"])
        S.dma("sp", hvec[:], hvec_d, writes=["hvec"])
        S.dma("sp", gbc, normw_bc, writes=["gbc"])
        S.add("act", lambda e: e.copy(out=identb[:], in_=identf), reads=["cm"], writes=["identb"])
        S.add("act", lambda e: e.copy(out=onesb[:], in_=onesf), reads=["cm"], writes=["onesb"])
        S.add("act", lambda e: e.copy(out=triLb[:], in_=triL), reads=["cm"], writes=["triLb"])
        S.add("act", lambda e: e.copy(out=triUb[:], in_=triU), reads=["cm"], writes=["triUb"])
        S.add("act", lambda e: e.activation(out=penb[:], in_=triU, func=AF.Identity, scale=-30000.0), reads=["cm"], writes=["penb"])
        S.add("act", lambda e: e.activation(out=a_bc[:], in_=alog_bc, func=AF.Exp), reads=["hvec"], writes=["a_bc0"])
        S.add("dve", lambda e: e.tensor_scalar(out=a_bc[:], in0=a_bc[:], scalar1=-1.0, scalar2=None, op0=ALU.mult),
              reads=["a_bc0"], writes=["a_bc"])
        S.dma("pool", wdt[:], w_in.rearrange("(k p) c -> p k c", p=128)[:, :, C_DT:C_DT + 32], writes=["wdt"])
        for b in range(2):
            S.add("dve", lambda e, b=b: e.memset(raw[b][:, 0:1064], 0.0), writes=ck("raw", b, 0, 1064))
        S.add("dve", lambda e: e.memset(ss[:], 0.0), writes=[("ss", k) for k in range(8)])
        w_in_v = w_in.rearrange("(k p) c -> p k c", p=128)
        w_out_v = w_out.rearrange("(k p) c -> p k c", p=128)

        tile_ctr = [0]

        def hn_tile(srcs, R, col0):
            t = tile_ctr[0]
            tile_ctr[0] += 1
            b = t % 2
            kx, kh = ("xin", b), ("hnb", b)
            if len(srcs) > 1 or srcs[0][1] != R:
                S.add("dve", lambda e: e.memset(xin[b][0:R, :], 0.0), writes=[kx])
            for (r0, n, ap) in srcs:
                S.dma("sp", xin[b][r0:r0 + n, :], ap, writes=[kx])
            sscol = ss[0:R, b:b + 1]
            S.add("act", lambda e: e.activation(out=hnb[b][0:R, :], in_=xin[b][0:R, :], func=AF.Square, accum_out=sscol),
                  reads=[kx], writes=[kh, ("ss", b)])
            S.add("dve", lambda e: e.tensor_scalar(out=sscol, in0=sscol, scalar1=1.0 / D, scalar2=EPS, op0=ALU.mult, op1=ALU.add),
                  reads=[("ss", b)], writes=[("ss", b)])
            rsqrt_ip(sscol, [("ss", b)])
            S.add("dve", lambda e: e.scalar_tensor_tensor(out=hnb[b][0:R, :], in0=xin[b][0:R, :], scalar=sscol, in1=gbc[0:R, :],
                                                          op0=ALU.mult, op1=ALU.mult),
                  reads=[kx, ("ss", b), "gbc"], writes=[kh])
            S.add("dve", lambda e: e.memset(sscol, 0.0), writes=[("ss", b)])

            def stage_b():
                pb = 2 * b
                ptr = ps[:, 512 * pb:512 * (pb + 2)].bitcast(BF16).rearrange("p (k t) -> p k t", k=16)
                for kt in range(16):
                    S.add("pe", lambda e, kt=kt: e.transpose(out=ptr[:, kt, 0:R], in_=hnb[b][0:R, kt * 128:(kt + 1) * 128],
                                                             identity=identb[0:R, 0:R]),
                          reads=[kh, "identb"], writes=[KPS(pb + kt // 8)])
                S.add("act", lambda e: e.copy(out=hnT[:, 0:8, col0:col0 + R], in_=ptr[:, 0:8, 0:R]),
                      reads=[KPS(pb)], writes=ck("hnTa", 0, col0, col0 + R))
                S.add("dve", lambda e: e.tensor_copy(out=hnT[:, 8:16, col0:col0 + R], in_=ptr[:, 8:16, 0:R]),
                      reads=[KPS(pb + 1)], writes=ck("hnT", 0, col0, col0 + R))
            return stage_b

        def hn_tiles(specs):
            pend = None
            for sp in specs:
                nb = hn_tile(*sp)
                if pend is not None:
                    pend()
                pend = nb
            pend()

        wctr = [0]
        bank_pool = [list(range(7))]
        bctr = [0]

        pe_deferred = []
        bg = [None, 0]

        def run_jobs(jobs, PF=5):
            slots = {}

            def issue(j):
                s = wctr[0] % NWS
                wctr[0] += 1
                slots[j] = s
                c = jobs[j][0]
                S.dma("pool", Wslot[s][:], w_in_v[:, :, c:c + 128], writes=[("W", s)])
            for j in range(min(PF, len(jobs))):
                issue(j)
            for j, (wc, blocks, epi, fin) in enumerate(jobs):
                if j + PF < len(jobs):
                    issue(j + PF)
                s = slots[j]
                for (c0, c1) in blocks:
                    bp = bank_pool[0]
                    bi = bp[bctr[0] % len(bp)]
                    bctr[0] += 1
                    pap = bank(bi)[:, 0:c1 - c0]
                    for kt in range(16):
                        S.add("pe", lambda e, kt=kt, s=s, c0=c0, c1=c1, pap=pap:
                              e.matmul(pap, lhsT=Wslot[s][:, kt, :], rhs=hnT[:, kt, c0:c1], start=(kt == 0), stop=(kt == 15)),
                              reads=[("W", s)] + ck("hnT", 0, c0, c1) + ck("hnTa", 0, c0, c1), writes=[KPS(bi)])
                    while pe_deferred:
                        pe_deferred.pop(0)()
                    epi(c0, c1, pap, KPS(bi))
                    if bg[0] is not None:
                        for _ in range(bg[1]):
                            if next(bg[0], "done") == "done":
                                bg[0] = None
                                break
                if fin is not None:
                    fin()
                if bg[0] is not None and next(bg[0], "done") == "done":
                    bg[0] = None
            while pe_deferred:
                pe_deferred.pop(0)()

        def dt_tile(c0, R, dst_dt, dst_dta, mask_ap, kd, hl=None):
            pap = bank(7)[0:R, 0:32]
            for kt in range(16):
                S.add("pe", lambda e, kt=kt: e.matmul(pap, lhsT=hnT[:, kt, c0:c0 + R], rhs=wdt[:, kt, :], start=(kt == 0), stop=(kt == 15)),
                      reads=["wdt"] + ck("hnT", 0, c0, c0 + R) + ck("hnTa", 0, c0, c0 + R), writes=[KPS(7)])
            t1 = sm[0:R, 0, :]
            S.add("dve", lambda e: e.tensor_tensor(out=t1, in0=pap, in1=dtb_bc[0:R, :], op=ALU.add),
                  reads=[KPS(7), "hvec"], writes=[("sm", 0)])
            S.add("act", lambda e: e.activation(out=t1, in_=t1, func=AF.Exp), reads=[("sm", 0)], writes=[("sm", 0)])
            S.add("act", lambda e: e.activation(out=dst_dt, in_=t1, func=AF.Ln, bias=1.0), reads=[("sm", 0)], writes=[kd])
            if mask_ap is not None:
                S.add("dve", lambda e: e.tensor_scalar(out=dst_dt, in0=dst_dt, scalar1=mask_ap, scalar2=None, op0=ALU.mult),
                      reads=[kd, "hvec"], writes=[kd])
            S.add("dve", lambda e: e.tensor_tensor(out=dst_dta, in0=dst_dt, in1=a_bc[0:R, :], op=ALU.mult),
                  reads=[kd, "a_bc"], writes=[kd])
            if hl is not None:
                S.add("dve", lambda e: e.tensor_copy(out=hl[:, 0, :], in_=dst_dta), reads=[kd], writes=[kd])
                S.add("dve", lambda e: e.tensor_copy(out=dtf[:], in_=hl[:, 0, :]), reads=[kd], writes=["dtf"])
                S.add("dve", lambda e: e.tensor_tensor(out=hl[:, 1, :], in0=dst_dta, in1=dtf[:], op=ALU.subtract), reads=[kd, "dtf"], writes=[kd])

        def cw(i, ct):
            return cvec[:, 24 * i + ct:24 * i + ct + 1]

        def cb_(ct):
            return cvec[:, 96 + ct:97 + ct]

        def conv4(rb, ct, lo, hi, dst):
            n = hi - lo
            ka = ck("acc", rb, lo, hi)
            S.add("act", lambda e: e.activation(out=acc[rb][:, lo:hi], in_=raw[rb][:, 3 + lo:3 + hi], func=AF.Identity,
                                                bias=cb_(ct), scale=cw(3, ct)),
                  reads=ck("raw", rb, 3 + lo, 3 + hi) + ["cvec"], writes=ka)
            for i in range(3):
                S.add("dve",
                      lambda e, i=i: e.scalar_tensor_tensor(out=acc[rb][:, lo:hi], in0=raw[rb][:, i + lo:i + hi], scalar=cw(i, ct),
                                                            in1=acc[rb][:, lo:hi], op0=ALU.mult, op1=ALU.add),
                      reads=ck("raw", rb, i + lo, i + hi) + ["cvec"] + ka, writes=ka)
            S.add("act", lambda e: e.activation(out=dst[:, lo:hi], in_=acc[rb][:, lo:hi], func=AF.Silu),
                  reads=ka, writes=ck("post", ct, lo, hi))

        hn_tiles([([(0, 128, xs[128 * i:128 * (i + 1), :])], 128, 128 * i) for i in range(8)])
        for i in range(8):
            dt_tile(128 * i, 128, dtP[:, i, :], dtaP[:, i, :], hvec[:, 64 + i:65 + i], ("dtP", i))
        jobs = []
        for ct in range(20):
            rb = ct % 2

            def epi(c0, c1, pap, pk, rb=rb):
                S.add("act", lambda e: e.copy(out=raw[rb][:, 3 + c0:3 + c1], in_=pap), reads=[pk], writes=ck("raw", rb, 3 + c0, 3 + c1))

            def fin(rb=rb, ct=ct):
                S.add("dve", lambda e: e.tensor_copy(out=xlastP[:, ct, :], in_=raw[rb][:, 1024:1027]),
                      reads=ck("raw", rb, 1024, 1027), writes=[("xlastP", ct)])
                conv4(rb, ct, 0, 1024, post[ct])
            jobs.append((C_XBC + 128 * ct, [(0, 512), (512, 1024)], epi, fin))
        run_jobs(jobs)
        S.add("dve", lambda e: e.memset(xlastP[:, 20:24, :], 0.0), writes=[("xlastP", c) for c in range(20, 24)])

        hn_tiles([([(0, 16, xsm), (16, 16, xs[1024:1040, :])], 32, 0)] +
                 [([(0, 128, xs[1040 + 128 * i:1040 + 128 * (i + 1), :])], 128, OWN0 + 128 * i) for i in range(8)])

        S.add("dve", lambda e: e.memset(Sf[:], 0.0), writes=["Sf"])
        S.add("dve", lambda e: e.memset(Sb[:], 0.0), writes=["Sb"])
        b1bf = bank_bf(1)
        xtok_ps = ps[:, 1024:2048].bitcast(BF16).rearrange("p (c t) -> p c t", c=16)
        btok_ps = bank_bf(0)[:, 0:512].rearrange("p (g t) -> p g t", g=4)
        sm_ps = bank(1)[:, 128:224]
        gab = [0]

        def ssd_chunk(c0, L, dt_ap, dta_ap, kd, with_y):
            S.add("pe", lambda e: e.matmul(sm_ps[0:L, 0:32], lhsT=triU[0:L, 0:L], rhs=dta_ap, start=True, stop=True),
                  reads=["cm", kd], writes=[KPS(1)])
            S.add("pe", lambda e: e.matmul(sm_ps[:, 32:64], lhsT=onesf[0:L, :], rhs=dta_ap, start=True, stop=True),
                  reads=["cm", kd], writes=[KPS(1)])
            if with_y:
                S.add("pe", lambda e: e.matmul(sm_ps[0:L, 64:96], lhsT=triL[0:L, 0:L], rhs=dta_ap, start=True, stop=True),
                      reads=["cm", kd], writes=[KPS(1)])
                for g in range(4):
                    S.add("pe", lambda e, g=g: e.matmul(bank(0)[:, 128 * g:128 * (g + 1)], lhsT=post[16 + g][:, c0:c0 + 128],
                                                        rhs=post[20 + g][:, c0:c0 + 128], start=True, stop=True),
                          reads=ck("post", 16 + g, c0, c0 + 128) + ck("post", 20 + g, c0, c0 + 128), writes=[KPS(0)])
            for ct in range(16):
                S.add("pe", lambda e, ct=ct: e.transpose(out=xtok_ps[0:L, ct, :], in_=post[ct][:, c0:c0 + L], identity=identb[:]),
                      reads=ck("post", ct, c0, c0 + L) + ["identb"], writes=[KPS(2), KPS(3)])
            dend, cd, w2 = sm[0:L, 1, :], sm[:, 2, :], sm[0:L, 3, :]
            S.add("act", lambda e: e.activation(out=dend, in_=sm_ps[0:L, 0:32], func=AF.Exp), reads=[KPS(1)], writes=[("sm", 1)])
            S.add("act", lambda e: e.activation(out=cd, in_=sm_ps[:, 32:64], func=AF.Exp), reads=[KPS(1)], writes=[("sm", 2)])
            if with_y:
                S.add("act", lambda e: e.activation(out=eacs[:], in_=sm_ps[:, 64:96], func=AF.Exp), reads=[KPS(1)], writes=["eacs"])
                S.add("act", lambda e: e.activation(out=negacs[:], in_=sm_ps[:, 64:96], func=AF.Identity, scale=-1.0), reads=[KPS(1)], writes=["negacs"])
                S.add("dve", lambda e: e.tensor_tensor(out=CBm, in0=bank(0).rearrange("p (g l) -> p g l", g=4),
                                                       in1=triL.unsqueeze(1).to_broadcast([128, 4, 128]), op=ALU.mult),
                      reads=[KPS(0), "cm"], writes=["CBm"])
            S.add("dve", lambda e: e.tensor_tensor(out=w2, in0=dt_ap, in1=dend, op=ALU.mult), reads=[kd, ("sm", 1)], writes=[("sm", 3)])
            if with_y:
                S.add("pe", lambda e: e.transpose(out=b1bf[0:32, 0:128], in_=eacs[:], identity=identb[:]),
                      reads=["eacs", "identb"], writes=[KPS(1)])
                S.add("act", lambda e: e.copy(out=eacsT[:], in_=b1bf[0:32, 0:128]), reads=[KPS(1)], writes=["eacsT"])
            for g in range(4):
                S.add("pe", lambda e, g=g: e.transpose(out=btok_ps[0:L, g, :], in_=post[16 + g][:, c0:c0 + L], identity=identb[:]),
                      reads=ck("post", 16 + g, c0, c0 + L) + ["identb"], writes=[KPS(0)])
            xt3 = xtok_ps[0:L].rearrange("p c t -> p (c t)").rearrange("p (h q) -> p h q", h=32)
            S.add("dve", lambda e: e.tensor_tensor(out=xdd[0:L, :].rearrange("p (h q) -> p h q", h=32), in0=xt3,
                                                   in1=w2.unsqueeze(2).to_broadcast([L, 32, 64]), op=ALU.mult),
                  reads=[KPS(2), KPS(3), ("sm", 3)], writes=["xdd"])
            S.add("act", lambda e: e.copy(out=Btok[0:L, :], in_=btok_ps[0:L].rearrange("p g t -> p (g t)")), reads=[KPS(0)], writes=["Btok"])
            if with_y:
                S.add("dve", lambda e: e.tensor_tensor(out=xdt[0:L, :].rearrange("p (h q) -> p h q", h=32), in0=xt3,
                                                       in1=dt_ap.unsqueeze(2).to_broadcast([L, 32, 64]), op=ALU.mult),
                      reads=[KPS(2), KPS(3), kd], writes=["xdt"])
                dth = dth_all[:, (c0 - OWN0) // 128, :, :]
                def stage1(g):
                    ab = g % 2
                    for r in range(8):
                        h = 8 * g + r
                        outp = bank(4 + r // 4)[:, 128 * (r % 4):128 * (r % 4 + 1)]
                        for hl in range(2):
                            S.add("pe", lambda e, h=h, hl=hl, outp=outp: e.matmul(outp, lhsT=dth[:, hl, h:h + 1].to_broadcast([128, 128]), rhs=triLb[:],
                                                                                 start=(hl == 0), stop=False),
                                  reads=[kd, "triLb"], writes=[KPS(4 + r // 4)])
                        S.add("pe", lambda e, outp=outp: e.matmul(outp, lhsT=identb[:], rhs=penb[:], start=False, stop=True),
                              reads=["identb", "penb"], writes=[KPS(4 + r // 4)])
                    for r in range(8):
                        h = 8 * g + r
                        S.add("act", lambda e, r=r, h=h, ab=ab: e.activation(out=decT[ab][:, r, :], in_=bank(4 + r // 4)[:, 128 * (r % 4):128 * (r % 4 + 1)],
                                                                             func=AF.Exp, bias=negacs[:, h:h + 1]),
                              reads=[KPS(4 + r // 4), "negacs"], writes=[("decT", ab, r // 4)])

                def stage2a(g):
                    ab = g % 2
                    S.add("dve", lambda e: e.tensor_tensor(out=MT[ab], in0=decT[ab],
                                                           in1=CBm[:, g, :].unsqueeze(1).to_broadcast([128, 8, 128]), op=ALU.mult),
                          reads=[("decT", ab, 0), ("decT", ab, 1), "CBm"], writes=[("MT", ab)])
                    for hh in range(2):
                        qq = 2 * g + hh
                        cbi = qq % 4
                        eb = 6 if qq % 2 == 0 else 2
                        for r4 in range(4):
                            h = 8 * g + 4 * hh + r4
                            S.add("pe", lambda e, r4=r4, h=h, eb=eb: e.matmul(bank(eb)[:, 128 * r4:128 * (r4 + 1)],
                                                                              lhsT=identb[0:32, h:h + 1].to_broadcast([32, 128]),
                                                                              rhs=eacsT[:], start=True, stop=True),
                                  reads=["identb", "eacsT"], writes=[KPS(eb)])
                        S.add("dve", lambda e, cbi=cbi, eb=eb: e.tensor_tensor(out=CTe[cbi], in0=bank(eb).rearrange("p (r l) -> p r l", r=4),
                                                                               in1=post[20 + g][:, c0:c0 + 128].unsqueeze(1).to_broadcast([128, 4, 128]),
                                                                               op=ALU.mult),
                              reads=[KPS(eb)] + ck("post", 20 + g, c0, c0 + 128), writes=[("CTe", cbi), "stg"])

                def stage2b(g):
                    ab = g % 2
                    for hh in range(2):
                        qq = 2 * g + hh
                        cbi = qq % 4
                        yb = 7 if qq % 2 == 0 else 3
                        for t2 in range(2):
                            tt = 2 * hh + t2
                            yps = bank(yb)[:, 128 * t2:128 * (t2 + 1)]
                            for hp in range(2):
                                h = 8 * g + 2 * tt + hp
                                r = 2 * tt + hp
                                r4 = r - 4 * hh
                                S.add("pe", lambda e, h=h, r=r, hp=hp, yps=yps: e.matmul(yps[64 * hp:64 * hp + 64, :], lhsT=xdt[:, 64 * h:64 * h + 64],
                                                                                        rhs=MT[ab][:, r, :], start=True, stop=False),
                                      reads=["xdt", ("MT", ab)], writes=[KPS(yb)])
                                S.add("pe", lambda e, h=h, r4=r4, hp=hp, yps=yps, cbi=cbi: e.matmul(yps[64 * hp:64 * hp + 64, :], lhsT=Sb[:, 64 * h:64 * h + 64],
                                                                                                   rhs=CTe[cbi][:, r4, :], start=False, stop=True),
                                      reads=["Sb", ("CTe", cbi)], writes=[KPS(yb)])
                    for hh in range(2):
                        qq = 2 * g + hh
                        yb = 7 if qq % 2 == 0 else 3
                        for t2 in range(2):
                            tt = 2 * hh + t2
                            ct = 4 * g + tt
                            yps = bank(yb)[:, 128 * t2:128 * (t2 + 1)]
                            pk = ck("post", ct, c0, c0 + 128)
                            S.add("dve", lambda e, ct=ct, yps=yps: e.scalar_tensor_tensor(out=post[ct][:, c0:c0 + 128], in0=post[ct][:, c0:c0 + 128],
                                                                                         scalar=cvec[:, 200 + ct:201 + ct], in1=yps,
                                                                                         op0=ALU.mult, op1=ALU.add),
                                  reads=pk + [KPS(yb), "cvec"], writes=pk)
                stage1(0)
                stage1(1)
                stage2a(0)
                stage1(2)
                stage2a(1)
                stage2b(0)
                stage1(3)
                stage2a(2)
                stage2b(1)
                stage2a(3)
                stage2b(2)
                stage2b(3)
            sb0 = 2 if with_y else 4
            for g in range(4):
                S.add("pe", lambda e, g=g: e.matmul(bank(sb0 + g), lhsT=Btok[0:L, 128 * g:128 * (g + 1)], rhs=xdd[0:L, 512 * g:512 * (g + 1)],
                                                    start=True, stop=True),
                      reads=["Btok", "xdd"], writes=[KPS(sb0 + g)])
            S.add("pool", lambda e: e.tensor_tensor(out=Sf[:].rearrange("p (h q) -> p h q", h=32), in0=Sf[:].rearrange("p (h q) -> p h q", h=32),
                                                    in1=cd.unsqueeze(2).to_broadcast([128, 32, 64]), op=ALU.mult),
                  reads=["Sf", ("sm", 2)], writes=["Sf"])
            for g in range(4):
                S.add("dve", lambda e, g=g: e.tensor_tensor(out=Sf[:, 512 * g:512 * (g + 1)], in0=Sf[:, 512 * g:512 * (g + 1)], in1=bank(sb0 + g), op=ALU.add),
                      reads=["Sf", KPS(sb0 + g)], writes=["Sf"])
            S.add("act", lambda e: e.copy(out=Sb[:], in_=Sf[:]), reads=["Sf"], writes=["Sb"])

        S.barrier()
        dkeys = [("dtP", i) for i in range(8)]
        for c in range(8):
            psm = bank(1)[:, 32 * c:32 * (c + 1)]
            S.add("pe", lambda e, c=c, psm=psm: e.matmul(psm, lhsT=triU, rhs=dtaP[:, c, :], start=True, stop=(c == 7)),
                  reads=["cm"] + dkeys, writes=[KPS(1)])
            for c2 in range(c + 1, 8):
                S.add("pe", lambda e, c2=c2, psm=psm: e.matmul(psm, lhsT=onesf, rhs=dtaP[:, c2, :], start=False, stop=(c2 == 7)),
                      reads=["cm"] + dkeys, writes=[KPS(1)])
        smf = sm[:].rearrange("p c h -> p (c h)")
        smk = [("sm", i) for i in range(8)]
        S.add("act", lambda e: e.activation(out=smf, in_=bank(1)[:, 0:256], func=AF.Exp), reads=[KPS(1)], writes=smk)
        S.add("dve", lambda e: e.tensor_tensor(out=smf, in0=smf, in1=dtP[:].rearrange("p c h -> p (c h)"), op=ALU.mult),
              reads=smk + dkeys, writes=smk)
        xddP = [xdd, xdt]
        BtokP = [Btok, gi[:, 9216:9728]]
        for c in range(8):
            b = c % 2
            c0 = 128 * c
            for ct in range(16):
                S.add("pe", lambda e, ct=ct, c0=c0: e.transpose(out=xtok_ps[:, ct, :], in_=post[ct][:, c0:c0 + 128], identity=identb[:]),
                      reads=ck("post", ct, c0, c0 + 128) + ["identb"], writes=[KPS(2), KPS(3)])
            for g in range(4):
                S.add("pe", lambda e, g=g, c0=c0: e.transpose(out=btok_ps[:, g, :], in_=post[16 + g][:, c0:c0 + 128], identity=identb[:]),
                      reads=ck("post", 16 + g, c0, c0 + 128) + ["identb"], writes=[KPS(0)])
            xt3 = xtok_ps.rearrange("p c t -> p (c t)").rearrange("p (h q) -> p h q", h=32)
            S.add("dve", lambda e, b=b, c=c, xt3=xt3: e.tensor_tensor(out=xddP[b].rearrange("p (h q) -> p h q", h=32), in0=xt3,
                                                                     in1=sm[:, c, :].unsqueeze(2).to_broadcast([128, 32, 64]), op=ALU.mult),
                  reads=[KPS(2), KPS(3), ("sm", c)], writes=[("xddP", b)])
            S.add("act", lambda e, b=b: e.copy(out=BtokP[b], in_=btok_ps.rearrange("p g t -> p (g t)")), reads=[KPS(0)], writes=[("BtokP", b)])
            for g in range(4):
                S.add("pe", lambda e, g=g, b=b, c=c: e.matmul(bank(4 + g), lhsT=BtokP[b][:, 128 * g:128 * (g + 1)], rhs=xddP[b][:, 512 * g:512 * (g + 1)],
                                                              start=(c == 0), stop=(c == 7)),
                      reads=[("BtokP", b), ("xddP", b)], writes=[KPS(4 + g)])
        for g in range(4):
            S.add("dve" if g % 2 == 0 else "act", (lambda e, g=g: e.tensor_copy(out=Sf[:, 512 * g:512 * (g + 1)], in_=bank(4 + g))) if g % 2 == 0 else
                  (lambda e, g=g: e.copy(out=Sf[:, 512 * g:512 * (g + 1)], in_=bank(4 + g))),
                  reads=[KPS(4 + g)], writes=[("Sfq", g)])
        S.add("act", lambda e: e.copy(out=Sb[:], in_=Sf[:]), reads=[("Sfq", g) for g in range(4)] + ["Sf"], writes=["Sb", "Sf"])
        S.barrier()

        SRK = lambda ct, i: ("sraw", ct, i)

        S.add("dve", lambda e: e.memset(sraw.rearrange("p a b c -> p (a b c)"), 0.0),
              writes=[SRK(ct, i) for ct in range(24) for i in range(4)] + [("hnb", 0), ("hnb", 1)])
        S.add("dve", lambda e: e.memset(svraw[:].rearrange("p a b c -> p (a b c)"), 0.0), writes=["svraw"])

        def p3_side():
            rounds = [("x", i, half) for i in range(3) for half in range(2)] + [("v", i, 0) for i in range(2)]

            def load(r):
                kind, i, half = rounds[r]
                b = r % 2
                if kind == "x":
                    S.dma("sp", xin[b][0:16, 0:1536], scv[:, i, 1536 * half:1536 * (half + 1)], writes=[("xin", b)])
                else:
                    S.dma("sp", xin[b][0:16, :], ssc[:, i, :], writes=[("xin", b)])
            load(0)
            yield 1
            for r, (kind, i, half) in enumerate(rounds):
                b = r % 2
                if r + 1 < len(rounds):
                    load(r + 1)
                n = 12 if kind == "x" else 16
                pv = bank(7)[:, 0:16 * n].rearrange("p (j t) -> p j t", j=n)
                for j in range(n):
                    S.add("pe", lambda e, j=j, pv=pv, b=b: e.transpose(out=pv[:, j, :], in_=xin[b][0:16, 128 * j:128 * (j + 1)],
                                                                      identity=identf[0:16, 0:16]),
                          reads=[("xin", b), "cm"], writes=[KPS(7)])
                if kind == "x":
                    S.add("act", lambda e, pv=pv, half=half, i=i: e.copy(out=sraw[:, 12 * half:12 * (half + 1), :, i], in_=pv),
                          reads=[KPS(7)], writes=[SRK(ct, i) for ct in range(12 * half, 12 * half + 12)])
                else:
                    S.add("act", lambda e, pv=pv, i=i: e.copy(out=svraw[:, :, :, i], in_=pv), reads=[KPS(7)], writes=["svraw"])
                yield 1
            S.dma("sp", cssd_smp[:, 0:2, :], scv[:, 1:3, :], writes=[("o", "cssd_smp01")])
            S.dma("sp", csc_smp[:, 0:1, :], ssc[:, 1:2, :], writes=[("o", "csc_smp0")])
            out_keys.extend([("o", "cssd_smp01"), ("o", "csc_smp0")])
            dt_tile(0, 16, dtS[:, 0, :], dtS[:, 1, :], None, "dtS")
            yield 1
            dt_tile(HALO0, 16, dtH[:, 0, :], dtH[:, 1, :], hvec[0:16, 72:73], "dtH")
            yield 1
            for i in range(8):
                dt_tile(OWN0 + 128 * i, 128, dtO[:, i, :], dtaO[:, i, :], None, ("dtO", i), hl=dth_all[:, i, :, :])
                yield 1

        def sconv(ct):
            rb = ct % 2
            sr = sraw[:, ct, :, :].rearrange("p b c -> p (b c)")
            sa = sacc[:, rb, :]
            srk = [SRK(ct, i) for i in range(4)]
            S.add("act", lambda e: e.activation(out=sa[:, 3:64], in_=sr[:, 3:64], func=AF.Identity, bias=cb_(ct), scale=cw(3, ct)),
                  reads=srk + ["cvec"], writes=[("sacc", rb)])
            for i in range(3):
                S.add("dve", lambda e, i=i: e.scalar_tensor_tensor(out=sa[:, 3:64], in0=sr[:, i:i + 61], scalar=cw(i, ct), in1=sa[:, 3:64],
                                                                   op0=ALU.mult, op1=ALU.add),
                      reads=srk + ["cvec", ("sacc", rb)], writes=[("sacc", rb)])
            S.add("act", lambda e: e.activation(out=sxp[:, ct, :], in_=sa.rearrange("p (b c) -> p b c", c=4)[:, :, 3], func=AF.Silu),
                  reads=[("sacc", rb)], writes=[("sxp", ct)])

        jobs = []
        for ct in range(24):
            rb = ct % 2

            def epi(c0, c1, pap, pk, rb=rb, ct=ct):
                lo = max(c0, HALO0)
                S.add("act", lambda e: e.copy(out=raw[rb][:, 3 + lo:3 + c1], in_=pap[:, lo - c0:c1 - c0]), reads=[pk], writes=ck("raw", rb, 3 + lo, 3 + c1))
                if c0 == 0:
                    S.add("act", lambda e: e.copy(out=sraw[:, ct, :, 3], in_=pap[:, 0:16]), reads=[pk], writes=[SRK(ct, 3)])
                    S.add("dve", lambda e: e.tensor_copy(out=raw[rb][:, HALO0:HALO0 + 3], in_=xlastP[:, ct, :]), reads=[("xlastP", ct)],
                          writes=ck("raw", rb, HALO0, HALO0 + 3))

            def fin(rb=rb, ct=ct):
                S.add("dve", lambda e: e.tensor_copy(out=xlastE[:, ct, :], in_=raw[rb][:, NE:NE + 3]),
                      reads=ck("raw", rb, NE, NE + 3), writes=[("xlastE", ct)])
                conv4(rb, ct, HALO0, NE, post[ct])
                if ct >= 4:
                    sconv(ct - 4)
            jobs.append((C_XBC + 128 * ct, BLKS, epi, fin))
        bg[0], bg[1] = p3_side(), 1
        run_jobs(jobs)
        while bg[0] is not None:
            if next(bg[0], "done") == "done":
                bg[0] = None
        for ct in range(20, 24):
            sconv(ct)

        trc = [0]

        def tok_rows(src_fn, ntile, R, dst_dram, okey, half_tiles=8):
            for h0 in range(0, ntile, half_tiles):
                n = min(half_tiles, ntile - h0)
                ab = trc[0] % 2
                trc[0] += 1
                b0 = 4 if ab == 0 else 0
                stg = acc[ab]
                sk = ck("acc", ab, 0, 1024)
                for j in range(n):
                    bi = b0 + j // 4
                    S.add("pe", lambda e, j=j, bi=bi, h0=h0: e.transpose(out=bank(bi)[0:R, 128 * (j % 4):128 * (j % 4 + 1)], in_=src_fn(h0 + j),
                                                                          identity=identf),
                          reads=["cm", okey + "_src"], writes=[KPS(bi)])
                nb = (n + 3) // 4
                for q in range(nb):
                    w = min(4, n - 4 * q) * 128
                    S.add("act", lambda e, q=q, w=w, stg=stg, b0=b0: e.copy(out=stg[0:R, 512 * q:512 * q + w], in_=bank(b0 + q)[0:R, 0:w]),
                          reads=[KPS(b0 + q)], writes=sk)
                S.dma("sp", dst_dram[:, 128 * h0:128 * (h0 + n)], stg[0:R, 0:128 * n], reads=sk, writes=[("o", okey, h0)])
                out_keys.append(("o", okey, h0))

        S.add("dve", lambda e: e.tensor_copy(out=junk[:, 0:1], in_=xlastE[:, 0, 0:1]), reads=[("xlastE", c) for c in range(24)], writes=["cssd_fin_src"])
        tok_rows(lambda ct: xlastE[:, ct, :], 24, 3, cssd_fin, "cssd_fin")
        S.add("dve", lambda e: e.tensor_copy(out=junk[:, 1:2], in_=sraw[:, 0, 0, 0:1]), reads=[SRK(ct, 3) for ct in range(24)], writes=["cssd_smp_src"])
        tok_rows(lambda ct: sraw[:, ct, :, 3], 24, 16, cssd_smp[:, 2, :], "cssd_smp")
        S.barrier()

        ssd_chunk(HALO0, 16, dtH[:, 0, :], dtH[:, 1, :], "dtH", False)
        for i in range(8):
            ssd_chunk(OWN0 + 128 * i, 128, dtO[:, i, :], dtaO[:, i, :], ("dtO", i), True)
        stg3 = stg_f.rearrange("p (j n) -> p j n", j=16)
        for j in range(16):
            bi = 4 + (j // 4) % 2
            S.add("pe", lambda e, j=j, bi=bi: e.transpose(out=bank(bi)[:, 128 * (j % 4):128 * (j % 4 + 1)], in_=Sf[:, 128 * j:128 * (j + 1)], identity=identf),
                  reads=["Sf", "cm"], writes=[KPS(bi)])
            if j % 4 == 3:
                q = j // 4
                S.add("act", lambda e, q=q, bi=bi: e.copy(out=stg_f[:, 512 * q:512 * (q + 1)], in_=bank(bi)), reads=[KPS(bi)],
                      writes=["stg"] + [("CTe", i) for i in range(4)])
        S.dma("sp", ssm_fin.rearrange("(j p) n -> p j n", p=128), stg3, reads=["stg"], writes=[("o", "ssm_fin")])
        out_keys.append(("o", "ssm_fin"))

        S.barrier()
        S.add("act", lambda e: e.activation(out=dtS[:, 1, :], in_=dtS[:, 1, :], func=AF.Exp), reads=["dtS"], writes=["dtS"])
        for q in range(2):
            src = dtS[:, 1, :] if q == 0 else dtS[:, 0, :]
            S.add("pe", lambda e, q=q, src=src: e.transpose(out=bank(7)[0:32, 16 * q:16 * (q + 1)], in_=src, identity=identf[0:16, 0:16]),
                  reads=["dtS", "cm"], writes=[KPS(7)])
        S.add("act", lambda e: e.copy(out=dAT[:].rearrange("h q b -> h (q b)"), in_=bank(7)[0:32, 0:32]), reads=[KPS(7)], writes=["dAT"])
        for q in range(2):
            pv = bank(4 + q)[:, 0:256].rearrange("p (j b) -> p j b", j=16)
            for j in range(16):
                for hp in range(2):
                    h = 2 * j + hp
                    S.add("pe", lambda e, q=q, j=j, hp=hp, h=h, pv=pv: e.matmul(pv[64 * hp:64 * hp + 64, j, :], lhsT=identf[0:32, h:h + 1].to_broadcast([32, 64]),
                                                                                rhs=dAT[:, q, :], start=True, stop=True),
                          reads=["cm", "dAT"], writes=[KPS(4 + q)])
            dst = decx if q == 0 else dtx
            S.add("act", lambda e, dst=dst, pv=pv: e.copy(out=dst[:], in_=pv), reads=[KPS(4 + q)], writes=["decx" if q == 0 else "dtx"])
        S.add("dve", lambda e: e.tensor_tensor(out=xdts[:], in0=sxp[:, 0:16, :], in1=dtx[:], op=ALU.mult),
              reads=["dtx"] + [("sxp", c) for c in range(16)], writes=["xdts"])
        S.add("dve", lambda e: e.memset(ysT[:].rearrange("p a b -> p (a b)"), 0.0), writes=["ysT"])
        bctok = Sb[:].bitcast(F32)
        for q in range(8):
            S.add("pe", lambda e, q=q: e.transpose(out=bank(6 + q // 4)[0:16, 128 * (q % 4):128 * (q % 4 + 1)], in_=sxp[:, 16 + q, :], identity=identf),
                  reads=[("sxp", 16 + q), "cm"], writes=[KPS(6 + q // 4)])
        for q in range(2):
            S.add("act", lambda e, q=q: e.copy(out=bctok[0:16, 512 * q:512 * (q + 1)], in_=bank(6 + q)[0:16, :]), reads=[KPS(6 + q)], writes=["Sb"])
        S.dma("sp", scr, bctok[0:16, :], reads=["Sb"], writes=["scr"])
        sssm_v = sssm.rearrange("b (j p) n -> b p j n", p=128)
        ssm_smp_v = ssm_smp.rearrange("b (j p) n -> b p j n", p=128)
        Bc = [Sf[:, 1024 * k:1024 * (k + 1)] for k in range(2)]

        def stkeys(k):
            return [kk for ct in range(16 + 4 * k, 20 + 4 * k) for kk in ck("post", ct, 0, NE)]

        def smp_load(b):
            k = b % 2
            S.dma("sp", St[k], sssm_v[b], writes=stkeys(k))
            S.dma("sp", Bc[k], scr[b:b + 1, :].to_broadcast([128, 1024]), reads=["scr"], writes=[("Bc", k), "Sf"])

        def sample_gen():
            smp_load(0)
            for b in range(16):
                k = b % 2
                stk = stkeys(k)
                if b + 1 < 16:
                    smp_load(b + 1)
                for j in range(16):
                    g = j // 4
                    S.add("act", lambda e, j=j, b=b, k=k: e.activation(out=St[k][:, j, :], in_=St[k][:, j, :], func=AF.Identity, scale=decx[:, j, b:b + 1]),
                          reads=stk + ["decx"], writes=stk)
                    S.add("dve", lambda e, j=j, b=b, k=k, g=g: e.scalar_tensor_tensor(out=St[k][:, j, :], in0=Bc[k][:, 128 * g:128 * (g + 1)],
                                                                                     scalar=xdts[:, j, b:b + 1], in1=St[k][:, j, :],
                                                                                     op0=ALU.mult, op1=ALU.add),
                          reads=stk + [("Bc", k), "xdts"], writes=stk)
                    S.add("dve", lambda e, j=j, b=b, k=k, g=g: e.scalar_tensor_tensor(out=junk[:], in0=St[k][:, j, :], scalar=1.0,
                                                                                     in1=Bc[k][:, 512 + 128 * g:512 + 128 * (g + 1)], op0=ALU.mult, op1=ALU.mult,
                                                                                     accum_out=ysT[:, j, b:b + 1]),
                          reads=stk + [("Bc", k)], writes=["junk", "ysT"])
                    yield 1
                S.dma("sp", ssm_smp_v[b], St[k], reads=stk, writes=[("o", "ssm_smp", b)])
                out_keys.append(("o", "ssm_smp", b))
        bg[0] = sample_gen()
        bg[1] = 1

        def sample_finish():
            while bg[0] is not None:
                if next(bg[0], "done") == "done":
                    bg[0] = None
            S.add("dve", lambda e: e.tensor_tensor(out=xdts[:], in0=sxp[:, 0:16, :], in1=cvec[:, 200:216].unsqueeze(2).to_broadcast([128, 16, 16]), op=ALU.mult),
                  reads=["xdts", "cvec", "ysT"] + [("sxp", c) for c in range(16)], writes=["xdts"])
            S.add("dve", lambda e: e.tensor_tensor(out=ysT[:], in0=ysT[:], in1=xdts[:], op=ALU.add), reads=["ysT", "xdts"], writes=["ysT"])

        def sample_norm(q, gcol0, kin):
            src = yg_s[:, q, :, :]
            sq = sq16[:]
            S.add("dve", lambda e: e.tensor_tensor(out=sq, in0=src, in1=src, op=ALU.mult), reads=[kin], writes=["sq16"])
            S.add("pe", lambda e: e.matmul(bank(7)[:, 0:256], lhsT=onesf, rhs=sq.rearrange("p a b -> p (a b)"), start=True, stop=True),
                  reads=["sq16", "cm"], writes=[KPS(7)])
            r = rs16[:, q, :]
            S.add("dve", lambda e: e.tensor_reduce(out=r, in_=bank(7)[:, 0:256].rearrange("p (j b) -> p b j", j=16), axis=AX.X, op=ALU.add),
                  reads=[KPS(7)], writes=[("rs16", q)])
            S.add("dve", lambda e: e.tensor_scalar(out=r, in0=r, scalar1=1.0 / D, scalar2=EPS, op0=ALU.mult, op1=ALU.add),
                  reads=[("rs16", q)], writes=[("rs16", q)])
            rsqrt_ip(r, [("rs16", q)])
            S.add("dve", lambda e: e.tensor_tensor(out=sq, in0=src, in1=cvec[:, gcol0:gcol0 + 16].unsqueeze(2).to_broadcast([128, 16, 16]), op=ALU.mult),
                  reads=[kin, "cvec", "sq16"], writes=["sq16"])
            S.add("dve", lambda e: e.tensor_tensor(out=ysn[:, 16 * q:16 * q + 16, :], in0=sq, in1=r.unsqueeze(1).to_broadcast([128, 16, 16]), op=ALU.mult),
                  reads=["sq16", ("rs16", q)], writes=[("ysn", q)])

        bank_pool[0] = [0, 1, 2, 3, 4]
        sqc = [0]

        def sumsq_block(src_ap, bidx, ncols, first, last, rkeys):
            bi = 5 + bidx
            q = sqc[0] % 2
            sqc[0] += 1
            S.add("act", lambda e: e.activation(out=sqb[q][:, 0:ncols], in_=src_ap, func=AF.Square), reads=rkeys, writes=[("sqb", q)])
            pe_deferred.append(lambda: S.add("pe", lambda e: e.matmul(bank(bi)[:, 0:ncols], lhsT=onesb[:], rhs=sqb[q][:, 0:ncols], start=first, stop=last),
                                             reads=[("sqb", q), "onesb"], writes=[KPS(bi)]))

        def finish_norm(dst_list, gcol0, lazy=False):
            rst = acc[1]
            rst_use, rkeys = rst, ck("acc", 1, OWN0, NE)
            for bidx, (c0, c1) in enumerate(BLKS):
                lo = max(c0, OWN0)
                S.add("dve", lambda e, bidx=bidx, lo=lo, c1=c1: e.tensor_scalar(out=rst[:, lo:c1], in0=bank(5 + bidx)[:, 0:c1 - lo], scalar1=1.0 / D, scalar2=EPS,
                                                                                op0=ALU.mult, op1=ALU.add),
                      reads=[KPS(5 + bidx)], writes=ck("acc", 1, lo, c1))
                rsqrt_ip(rst[:, lo:c1], ck("acc", 1, lo, c1))
            if lazy:
                rstP = Sb[:].bitcast(F32)
                S.add("act", lambda e: e.copy(out=rstP[:, 0:NE - OWN0], in_=rst[:, OWN0:NE]), reads=ck("acc", 1, OWN0, NE), writes=["Sb"])
                rst_use, rkeys = None, ["Sb"]

            def norm_ops():
                for j, (dst, dkeys) in enumerate(dst_list):
                    S.add("dve", lambda e, j=j, dst=dst: e.scalar_tensor_tensor(out=dst[:, OWN0:NE], in0=dst[:, OWN0:NE], scalar=cvec[:, gcol0 + j:gcol0 + j + 1],
                                                                                in1=(rstP[:, 0:NE - OWN0] if lazy else rst[:, OWN0:NE]), op0=ALU.mult, op1=ALU.mult),
                          reads=dkeys + rkeys + ["cvec"], writes=dkeys)
                    yield 1
            if lazy:
                return norm_ops()
            for _ in norm_ops():
                pass

        jobs = []
        for j in range(16):
            def epi(c0, c1, pap, pk, j=j):
                if c0 == 0:
                    S.add("act", lambda e: e.activation(out=szs[:, j, :], in_=pap[:, 0:16], func=AF.Silu), reads=[pk], writes=[("szs", j)])
                lo = max(c0, OWN0)
                bidx = [b[0] for b in BLKS].index(c0)
                ka = ck("acc", 0, lo, c1)
                S.add("act", lambda e: e.activation(out=acc[0][:, lo:c1], in_=pap[:, lo - c0:c1 - c0], func=AF.Silu), reads=[pk], writes=ka)
                pk2 = ck("post", j, lo, c1)
                S.add("dve", lambda e: e.tensor_tensor(out=post[j][:, lo:c1], in0=post[j][:, lo:c1], in1=acc[0][:, lo:c1], op=ALU.mult),
                      reads=pk2 + ka, writes=pk2)
                sumsq_block(post[j][:, lo:c1], bidx, c1 - lo, j == 0, j == 15, pk2)
            jobs.append((C_Z + 128 * j, BLKS, epi, None))
        run_jobs(jobs)
        fn_gen = finish_norm([(post[j], ck("post", j, OWN0, NE)) for j in range(16)], 120, lazy=True)
        def chain2(g1, g2):
            for x in g1:
                yield x
            if g2 is not None:
                for x in g2:
                    yield x
        bg[0] = chain2(fn_gen, bg[0])

        jobs = []
        for f in range(16):
            def cwsc(i, f=f):
                return cvec[:, 136 + 16 * i + f:137 + 16 * i + f]

            def epi_c(c0, c1, pap, pk):
                S.add("act", lambda e: e.copy(out=raw[0][:, 3 + c0:3 + c1], in_=pap), reads=[pk], writes=ck("raw", 0, 3 + c0, 3 + c1))

            def epi_h(c0, c1, pap, pk, f=f):
                S.add("dve", lambda e: e.tensor_tensor(out=raw[1][:, 3 + c0:3 + c1], in0=pap, in1=raw[0][:, 3 + c0:3 + c1], op=ALU.mult),
                      reads=[pk] + ck("raw", 0, 3 + c0, 3 + c1), writes=ck("raw", 1, 3 + c0, 3 + c1))
                if c0 == 0:
                    S.add("act", lambda e: e.copy(out=svraw[:, f, :, 2], in_=raw[1][:, 3:19]), reads=ck("raw", 1, 3, 19), writes=["svraw"])

            def fin_h(f=f, cwsc=cwsc):
                lo, hi = HALO0, NE
                ka = ck("acc", 0, lo, hi)
                S.add("act", lambda e: e.activation(out=acc[0][:, lo:hi], in_=raw[1][:, 3 + lo:3 + hi], func=AF.Identity, scale=cwsc(2)),
                      reads=ck("raw", 1, 3 + lo, 3 + hi) + ["cvec"], writes=ka)
                for i in (1, 0):
                    S.add("dve", lambda e, i=i: e.scalar_tensor_tensor(out=acc[0][:, lo:hi], in0=raw[1][:, 1 + i + lo:1 + i + hi], scalar=cwsc(i),
                                                                       in1=acc[0][:, lo:hi], op0=ALU.mult, op1=ALU.add),
                          reads=ck("raw", 1, 1 + i + lo, 1 + i + hi) + ["cvec"] + ka, writes=ka)
                S.add("dve", lambda e: e.tensor_copy(out=vlast[:, f, :], in_=raw[1][:, NE + 1:NE + 3]), reads=ck("raw", 1, NE + 1, NE + 3),
                      writes=[("vlast", f)])
                sv = svraw[:, f, :, :].rearrange("p b c -> p (b c)")
                S.add("act", lambda e: e.activation(out=svacc[:, 2:48], in_=sv[:, 2:48], func=AF.Identity, scale=cwsc(2)),
                      reads=["svraw", "cvec"], writes=["svacc"])
                for i in (1, 0):
                    S.add("dve", lambda e, i=i: e.scalar_tensor_tensor(out=svacc[:, 2:48], in0=sv[:, i:i + 46], scalar=cwsc(i), in1=svacc[:, 2:48],
                                                                       op0=ALU.mult, op1=ALU.add),
                          reads=["svraw", "cvec", "svacc"], writes=["svacc"])

            def epi_b(c0, c1, pap, pk, f=f):
                ka = ck("acc", 0, c0, c1)
                S.add("dve", lambda e: e.tensor_tensor(out=acc[0][:, c0:c1], in0=pap, in1=acc[0][:, c0:c1], op=ALU.mult), reads=[pk] + ka, writes=ka)
                if c0 == 0:
                    S.add("dve", lambda e: e.tensor_tensor(out=tsb[:, f, :], in0=pap[:, 0:16], in1=svacc[:].rearrange("p (b c) -> p b c", c=3)[:, :, 2],
                                                           op=ALU.mult),
                          reads=[pk, "svacc"], writes=[("tsbf", f)])

            def epi_z(c0, c1, pap, pk, f=f):
                ka1 = ck("acc", 1, c0, c1)
                S.add("act", lambda e: e.activation(out=acc[1][:, c0:c1], in_=pap, func=AF.Silu), reads=[pk], writes=ka1)
                yk = ck("ysc", f, c0, c1)
                S.add("dve", lambda e: e.tensor_tensor(out=ysc[f][:, c0:c1], in0=acc[0][:, c0:c1], in1=acc[1][:, c0:c1], op=ALU.mult),
                      reads=ck("acc", 0, c0, c1) + ka1, writes=yk)
                if c0 == 0:
                    S.add("dve", lambda e: e.tensor_tensor(out=yg_s[:, 1, f, :], in0=tsb[:, f, :], in1=acc[1][:, 0:16], op=ALU.mult),
                          reads=[("tsbf", f)] + ka1, writes=[("yg1", f)])
                lo = max(c0, OWN0)
                bidx = [b[0] for b in BLKS].index(c0)
                sumsq_block(ysc[f][:, lo:c1], bidx, c1 - lo, f == 0, f == 15, yk)
            blks = BLKS
            jobs.append((C_CSC + 128 * f, blks, epi_c, None))
            jobs.append((C_HSC + 128 * f, blks, epi_h, fin_h))
            jobs.append((C_BSC + 128 * f, blks, epi_b, None))
            jobs.append((C_ZSC + 128 * f, blks, epi_z, None))
        run_jobs(jobs)
        wo_pre = {}
        for dc in range(2):
            wk = [("W", 4 * dc + q) for q in range(4)]
            wo_pre[dc] = S.dma("pool", Wout[dc][:], w_out_v[:, :, 256 * dc:256 * (dc + 1)], writes=wk)
        while bg[0] is not None:
            if next(bg[0], "done") == "done":
                bg[0] = None
        def oacc_alias(ti):
            k = ck("hnT", 0, 0, NE) + ck("hnTa", 0, 0, NE)
            if ti == 4:
                k = k + ck("raw", 0, 0, 1064) + ck("raw", 1, 0, 1064)
            return k
        tiles0 = [("o", 128, xs[1040 + 128 * i:1040 + 128 * (i + 1), :], y_own[128 * i:128 * (i + 1), :], i) for i in range(4)] + \
                 [("s", 16, xsm, y_smp, None)]
        for ti, (kind, R, usrc, ydst, i) in enumerate(tiles0):
            S.dma("sp", out_acc[0:R, ti, :], usrc, writes=[("oacc", ti)] + oacc_alias(ti))
        finish_norm([(ysc[f], ck("ysc", f, OWN0, NE)) for f in range(16)], 184)
        def late_p7():
            S.add("dve", lambda e: e.tensor_copy(out=junk[:, 2:3], in_=yg_s[:, 1, 0, 0:1]), reads=[("yg1", f) for f in range(16)], writes=["yg1"])
            sample_norm(1, 184, "yg1")
            sample_finish()
            S.add("dve", lambda e: e.tensor_tensor(out=yg_s[:, 0, :, :], in0=ysT[:], in1=szs[:], op=ALU.mult),
                  reads=["ysT"] + [("szs", j) for j in range(16)], writes=["yg0"])
            sample_norm(0, 120, "yg0")
            S.add("dve", lambda e: e.tensor_copy(out=junk[:, 3:4], in_=vlast[:, 0, 0:1]), reads=[("vlast", f) for f in range(16)], writes=["csc_fin_src"])
            S.add("dve", lambda e: e.tensor_copy(out=junk[:, 4:5], in_=svraw[:, 0, 0, 0:1]), reads=["svraw"], writes=["csc_smp_src"])

            stg2 = Sb[:].bitcast(F32)
            for (src_fn, R, dst, okey) in ((lambda f: vlast[:, f, :], 2, csc_fin, "csc_fin"), (lambda f: svraw[:, f, :, 2], 16, csc_smp[:, 1, :], "csc_smp")):
                for h0 in (0, 8):
                    for j in range(8):
                        bi = 6 + j // 4
                        S.add("pe", lambda e, j=j, bi=bi, h0=h0, src_fn=src_fn, R=R: e.transpose(out=bank(bi)[0:R, 128 * (j % 4):128 * (j % 4 + 1)],
                                                                                                 in_=src_fn(h0 + j), identity=identf),
                              reads=["cm", okey + "_src"], writes=[KPS(bi)])
                    for q in range(2):
                        S.add("act", lambda e, q=q, R=R: e.copy(out=stg2[0:R, 512 * q:512 * (q + 1)], in_=bank(6 + q)[0:R, :]), reads=[KPS(6 + q)], writes=["stg2", "Sb"])
                    S.dma("sp", dst[:, 128 * h0:128 * (h0 + 8)], stg2[0:R, 0:1024], reads=["stg2"], writes=[("o", okey, h0)])
                    out_keys.append(("o", okey, h0))

        gfbc = Sf
        S.dma("sp", gfbc[:], fnormw_bc, writes=["Sf", ("Bc", 0), ("Bc", 1)])
        jrow = St[0].rearrange("p j n -> p (j n)")

        def junk_row(R):
            return jrow[0:R, :]
        wo_ctr = [0]

        def wout_load(hp, dc):
            sl = wo_ctr[0] % 2
            wo_ctr[0] += 1
            wk = [("W", 4 * sl + q) for q in range(4)]
            if not (hp == 0 and dc in wo_pre):
                S.dma("pool", Wout[sl][:], w_out_v[:, :, 256 * dc:256 * (dc + 1)], writes=wk)
            return sl, wk

        def mm_group(tile, ti, dc, sl, wk):
            kind, R, usrc, ydst, i = tile
            bi = bctr[0] % 6
            bctr[0] += 1
            pap = bank(bi)[0:R, 0:256]
            for kt in range(32):
                if kind == "s":
                    lhsT = ysn[:, kt, :]
                    rk = [("ysn", kt // 16)]
                else:
                    src = post[kt] if kt < 16 else ysc[kt - 16]
                    c0 = OWN0 + 128 * i
                    lhsT = src[:, c0:c0 + 128]
                    rk = ck("post", kt, c0, c0 + 128) if kt < 16 else ck("ysc", kt - 16, c0, c0 + 128)
                S.add("pe", lambda e, kt=kt, lhsT=lhsT, pap=pap, sl=sl: e.matmul(pap, lhsT=lhsT, rhs=Wout[sl][:, kt, :], start=(kt == 0), stop=(kt == 31)),
                      reads=wk + rk, writes=[KPS(bi)])
            oa = out_acc[0:R, ti, 256 * dc:256 * (dc + 1)]
            S.add("dve", lambda e, oa=oa, pap=pap: e.tensor_tensor(out=oa, in0=oa, in1=pap, op=ALU.add), reads=[KPS(bi), ("oacc", ti)], writes=[("oacc", ti)])

        def final_norm(tile, ti, hp):
            kind, R, usrc, ydst, i = tile
            b = ti % 2
            sscol = ss[0:R, 2 + b:3 + b]
            S.add("dve", lambda e: e.memset(sscol, 0.0), writes=[("ss", 2 + b)])
            S.add("act", lambda e: e.activation(out=junk_row(R), in_=out_acc[0:R, ti, :], func=AF.Square, accum_out=sscol),
                  reads=[("oacc", ti)], writes=[("ss", 2 + b), "jrow"] + stkeys(0))
            S.add("dve", lambda e: e.tensor_scalar(out=sscol, in0=sscol, scalar1=1.0 / D, scalar2=EPS, op0=ALU.mult, op1=ALU.add),
                  reads=[("ss", 2 + b)], writes=[("ss", 2 + b)])
            rsqrt_ip(sscol, [("ss", 2 + b)])
            S.add("dve", lambda e: e.scalar_tensor_tensor(out=out_acc[0:R, ti, :], in0=out_acc[0:R, ti, :], scalar=sscol,
                                                          in1=gfbc[0:R, :], op0=ALU.mult, op1=ALU.mult),
                  reads=[("oacc", ti), ("ss", 2 + b), "Sf"], writes=[("oacc", ti)])
            ok = ("o", "y", hp, ti)
            S.dma("sp", ydst, out_acc[0:R, ti, :], reads=[("oacc", ti)], writes=[ok])
            out_keys.append(ok)

        for hp in range(2):
            tiles = [("o", 128, xs[1040 + 128 * i:1040 + 128 * (i + 1), :], y_own[128 * i:128 * (i + 1), :], i) for i in range(4 * hp, 4 * hp + 4)] + \
                    ([("s", 16, xsm, y_smp, None)] if hp == 0 else [])
            for ti, (kind, R, usrc, ydst, i) in enumerate(tiles):
                if hp == 1:
                    S.dma("sp", out_acc[0:R, ti, :], usrc, writes=[("oacc", ti)] + oacc_alias(ti))
            for dc in range(6):
                sl, wk = wout_load(hp, dc)
                for ti, tile in enumerate(tiles):
                    if tile[0] == "s" and dc == 0:
                        late_p7()
                    mm_group(tile, ti, dc, sl, wk)
            w6 = wout_load(hp, 6)
            w7 = wout_load(hp, 7)
            for ti, tile in enumerate(tiles):
                mm_group(tile, ti, 6, *w6)
                mm_group(tile, ti, 7, *w7)
                final_norm(tile, ti, hp)
        S.flush(final_keys=out_keys)
    return nc, dbg_outs


_CACHE = {}


def _consts():
    ident = np.eye(128, dtype=np.float32)
    triL = np.triu(np.ones((128, 128), np.float32))
    triU = np.tril(np.ones((128, 128), np.float32), -1)
    ones = np.ones((128, 128), np.float32)
    return np.concatenate([ident, triL, triU, ones], axis=1)


def kernel(x_prompt, x_sample, state_ssm, state_ssd_conv, state_short_conv, meta_tokens, norm_w, w_in,
           conv_ssd_w, conv_ssd_b, dt_bias, a_log, d_skip, ssd_norm_w, conv_sc_w, sc_norm_w, w_out, final_norm_w,
           _debug=False):
    f = lambda a: np.ascontiguousarray(np.asarray(a, dtype=np.float32))
    x_prompt, x_sample, state_ssm, state_ssd_conv, state_short_conv = map(f, (x_prompt, x_sample, state_ssm, state_ssd_conv, state_short_conv))
    meta_tokens, norm_w, w_in, conv_ssd_w, conv_ssd_b = map(f, (meta_tokens, norm_w, w_in, conv_ssd_w, conv_ssd_b))
    dt_bias, a_log, d_skip, ssd_norm_w, conv_sc_w, sc_norm_w, w_out, final_norm_w = map(
        f, (dt_bias, a_log, d_skip, ssd_norm_w, conv_sc_w, sc_norm_w, w_out, final_norm_w))
    key = ("nc", bool(_debug))
    if key not in _CACHE:
        _CACHE[key] = build(debug=_debug)
    nc, dbg_outs = _CACHE[key]

    def colmajor(v):
        return v.reshape(-1, 128).T

    cvec = np.zeros((128, 216), np.float32)
    for i in range(4):
        cvec[:, 24 * i:24 * (i + 1)] = colmajor(conv_ssd_w[0, i])
    cvec[:, 96:120] = colmajor(conv_ssd_b[0])
    cvec[:, 120:136] = colmajor(ssd_norm_w[0])
    for i in range(3):
        cvec[:, 136 + 16 * i:152 + 16 * i] = colmajor(conv_sc_w[0, i])
    cvec[:, 184:200] = colmajor(sc_norm_w[0])
    cvec[:, 200:216] = colmajor(np.repeat(d_skip[0], 64))
    cmat = _consts()
    normw_bc = np.ascontiguousarray(np.broadcast_to(norm_w[0][None, :], (128, D)))
    fnormw_bc = np.ascontiguousarray(np.broadcast_to(final_norm_w[None, :], (128, D)))
    w_in0, w_out0 = w_in[0], w_out[0]
    in_maps = []
    for c in range(8):
        b, half = c // 2, c % 2
        if half == 0:
            stream = np.concatenate([np.zeros((1024, D), np.float32), meta_tokens, x_prompt[b, 0:1024]], axis=0)
        else:
            stream = np.concatenate([meta_tokens, x_prompt[b]], axis=0)
        hvec = np.zeros((128, 128), np.float32)
        hvec[:, 0:32] = dt_bias[0][None, :]
        hvec[:, 32:64] = a_log[0][None, :]
        hvec[:, 64:72] = float(half)
        hvec[:, 72] = 1.0
        sl = slice(16 * c, 16 * (c + 1))
        in_maps.append(dict(
            xs=np.ascontiguousarray(stream), xsm=np.ascontiguousarray(x_sample[sl, 0, :]),
            sssm=np.ascontiguousarray(state_ssm[0, sl].reshape(16, 2048, 128)),
            scv=np.ascontiguousarray(state_ssd_conv[0, sl]), ssc=np.ascontiguousarray(state_short_conv[0, sl]),
            w_in=w_in0, w_out=w_out0, normw_bc=normw_bc, fnormw_bc=fnormw_bc, cvec=cvec, hvec=hvec, cmat=cmat))
    res = run_bass_kernel_spmd(nc, in_maps, core_ids=list(range(8)))
    R = res.results
    y_prompt = np.empty((4, 2048, D), np.float32)
    y_sample = np.empty((128, 1, D), np.float32)
    ssm_p = np.empty((1, 4, 32, 64, 128), np.float32)
    cssd_p = np.empty((1, 4, 3, 3072), np.float32)
    csc_p = np.empty((1, 4, 2, 2048), np.float32)
    ssm_s = np.empty((1, 128, 32, 64, 128), np.float32)
    cssd_s = np.empty((1, 128, 3, 3072), np.float32)
    csc_s = np.empty((1, 128, 2, 2048), np.float32)
    for c in range(8):
        b, half = c // 2, c % 2
        r = R[c]
        y_prompt[b, 1024 * half:1024 * (half + 1)] = r["y_own"]
        sl = slice(16 * c, 16 * (c + 1))
        y_sample[sl, 0, :] = r["y_smp"]
        ssm_s[0, sl] = r["ssm_smp"].reshape(16, 32, 64, 128)
        cssd_s[0, sl] = r["cssd_smp"]
        csc_s[0, sl] = r["csc_smp"]
        if half == 1:
            ssm_p[0, b] = r["ssm_fin"].reshape(32, 64, 128)
            cssd_p[0, b] = r["cssd_fin"]
            csc_p[0, b] = r["csc_fin"]
    if _debug:
        kernel.last_debug = [{k: R[c]["dbg_" + k] for k in dbg_outs} for c in range(8)]
    return (y_prompt, y_sample, ssm_p, cssd_p, csc_p, ssm_s, cssd_s, csc_s)
```

```python
import numpy as np
from contextlib import ExitStack
import concourse.bass as bass
import concourse.mybir as mybir
from concourse.bass_utils import run_bass_kernel_spmd

F32 = mybir.dt.float32
BF16 = mybir.dt.bfloat16
AF = mybir.ActivationFunctionType
ALU = mybir.AluOpType
AX = mybir.AxisListType

D = 2048
NE = 1056
OWN0 = 32
HALO0 = 16
EPS = 1e-5
DIN = 13344
C_Z, C_XBC, C_DT, C_ZSC, C_BSC, C_CSC, C_HSC = 0, 2048, 5120, 5152, 7200, 9248, 11296
BLKS = [(0, 352), (352, 704), (704, 1056)]
ENGS = ("pe", "act", "dve", "pool", "sp")
N_DMA_SEMS = 24
NWS = 8


class _Op:
    __slots__ = ("eng", "emit", "deps", "dma", "signal", "ticket", "sem", "idx")

    def __init__(self, eng, emit, deps, dma, idx):
        self.eng, self.emit, self.deps, self.dma, self.idx = eng, emit, deps, dma, idx
        self.signal = False
        self.ticket = None
        self.sem = None


class Sched:
    def __init__(self, nc, sems):
        self.nc, self.sems = nc, sems
        self.ops = []
        self.last_writer = {}
        self.readers = {}
        self.dma_last = [None] * N_DMA_SEMS
        self.dma_rr = {"sp": 0, "pool": 0}
        self.bar = None
        self.since_bar = []

    def add(self, eng, emit, reads=(), writes=(), dma=False):
        idx = len(self.ops)
        deps = set()
        for k in reads:
            w = self.last_writer.get(k)
            if w is not None:
                deps.add(w)
        for k in writes:
            w = self.last_writer.get(k)
            if w is not None:
                deps.add(w)
            for r in self.readers.get(k, ()):
                deps.add(r)
        for k in writes:
            self.last_writer[k] = idx
            self.readers[k] = []
        for k in reads:
            self.readers.setdefault(k, []).append(idx)
        if self.bar is not None:
            deps.add(self.bar)
        deps.discard(idx)
        op = _Op(eng, emit, deps, dma, idx)
        if dma:
            r = self.dma_rr[eng]
            self.dma_rr[eng] = r + 1
            s = (r % 16) if eng == "sp" else 16 + (r % (N_DMA_SEMS - 16))
            op.sem = s
            if self.dma_last[s] is not None:
                op.deps.add(self.dma_last[s])
            self.dma_last[s] = idx
        self.ops.append(op)
        self.since_bar.append(idx)
        return idx

    def dma(self, eng, out, in_, reads=(), writes=(), **kw):
        return self.add(eng, lambda e: e.dma_start(out=out, in_=in_, **kw), reads, writes, dma=True)

    def barrier(self):
        idx = len(self.ops)
        last = {}
        for i in self.since_bar:
            o = self.ops[i]
            last[("dma", o.sem) if o.dma else ("eng", o.eng)] = i
        deps = set(last.values())
        if self.bar is not None:
            deps.add(self.bar)
        op = _Op("sp", lambda e: e.nop(), deps, False, idx)
        self.ops.append(op)
        self.bar = idx
        self.since_bar = []
        return idx

    def flush(self, final_keys=()):
        nc, ops = self.nc, self.ops
        for op in ops:
            if op.eng == "pe" and not op.dma:
                op.deps = {d for d in op.deps if not (ops[d].eng == "pe" and not ops[d].dma)}
        fin = set()
        for k in final_keys:
            w = self.last_writer.get(k)
            if w is not None:
                fin.add(w)
        for op in ops:
            for d in op.deps:
                ops[d].signal = True
        for d in fin:
            ops[d].signal = True
        eng_count = {e: 0 for e in ENGS}
        dma_count = [0] * N_DMA_SEMS
        for op in ops:
            if op.dma:
                dma_count[op.sem] += 16
                op.ticket = dma_count[op.sem]
            elif op.signal:
                eng_count[op.eng] += 1
                op.ticket = eng_count[op.eng]
        per_eng = {e: [] for e in ENGS}
        for op in ops:
            per_eng[op.eng].append(op)
        sems = self.sems

        def semkey(d):
            return ("dma", d.sem) if d.dma else ("eng", d.eng)

        def getsem(k):
            return sems["dma%d" % k[1]] if k[0] == "dma" else sems[k[1]]

        def run(engname, eng):
            seen = {}
            for op in per_eng[engname]:
                need = {}
                for di in op.deps:
                    d = ops[di]
                    k = semkey(d)
                    if d.ticket > need.get(k, 0):
                        need[k] = d.ticket
                for k, t in need.items():
                    if seen.get(k, 0) >= t:
                        continue
                    seen[k] = t
                    eng.wait_ge(getsem(k), t)
                ins = op.emit(eng)
                if op.dma:
                    ins.then_inc(sems["dma%d" % op.sem], 16)
                elif op.signal:
                    ins.then_inc(sems[op.eng], 1)
            if engname == "sp":
                need = {}
                for di in fin:
                    d = ops[di]
                    k = semkey(d)
                    need[k] = max(need.get(k, 0), d.ticket)
                for k, t in need.items():
                    eng.wait_ge(getsem(k), t)

        with nc.Block() as block:
            @block.tensor
            def _(e):
                run("pe", e)

            @block.scalar
            def _(e):
                run("act", e)

            @block.vector
            def _(e):
                run("dve", e)

            @block.gpsimd
            def _(e):
                run("pool", e)

            @block.sync
            def _(e):
                run("sp", e)


def ck(name, idx, c0, c1, gran=128):
    return [(name, idx, q) for q in range(c0 // gran, (c1 - 1) // gran + 1)]


def build(debug=False):
    nc = bass.Bass("TRN2", target_bir_lowering=False)

    def din(name, shape):
        return nc.dram_tensor(name, shape, F32, kind="ExternalInput").ap()

    def dout(name, shape):
        return nc.dram_tensor(name, shape, F32, kind="ExternalOutput").ap()

    xs = din("xs", [2064, D])
    xsm = din("xsm", [16, D])
    sssm = din("sssm", [16, 2048, 128])
    scv = din("scv", [16, 3, 3072])
    ssc = din("ssc", [16, 2, 2048])
    w_in = din("w_in", [D, DIN])
    w_out = din("w_out", [2 * D, D])
    normw_bc = din("normw_bc", [128, D])
    fnormw_bc = din("fnormw_bc", [128, D])
    cvec_d = din("cvec", [128, 216])
    hvec_d = din("hvec", [128, 128])
    cmat_d = din("cmat", [128, 512])

    y_own = dout("y_own", [1024, D])
    y_smp = dout("y_smp", [16, D])
    ssm_fin = dout("ssm_fin", [2048, 128])
    cssd_fin = dout("cssd_fin", [3, 3072])
    csc_fin = dout("csc_fin", [2, 2048])
    ssm_smp = dout("ssm_smp", [16, 2048, 128])
    cssd_smp = dout("cssd_smp", [16, 3, 3072])
    csc_smp = dout("csc_smp", [16, 2, 2048])
    scr = nc.dram_tensor("scr", [16, 1024], F32).ap()
    out_keys = []
    dbg_outs = {}

    with ExitStack() as es:
        E = es.enter_context
        sems = {n: E(nc.semaphore(n)) for n in ("pe", "act", "dve", "pool", "sp")}
        for i in range(N_DMA_SEMS):
            sems["dma%d" % i] = E(nc.semaphore("dma%d" % i))

        def sb(name, shape, dt):
            return E(nc.sbuf_tensor("s_" + name, shape, dt))

        big = sb("big", [128, 25408], BF16)
        hnT = big[:, 0:16896].rearrange("p (k t) -> p k t", k=16)
        stage_f = big[:, 16896:25408].bitcast(F32)
        raw = [stage_f[:, 0:1064], stage_f[:, 1064:2128]]
        acc = [stage_f[:, 2128:3192], stage_f[:, 3192:4256]]
        out_acc = big[:, 0:20480].bitcast(F32).rearrange("p (t d) -> p t d", t=5)
        Wp = sb("Wp", [128, 16384], BF16)
        Wslot = [Wp[:, 2048 * s:2048 * (s + 1)].rearrange("p (k c) -> p k c", k=16) for s in range(NWS)]
        Wout = [Wp[:, 8192 * s:8192 * (s + 1)].rearrange("p (k c) -> p k c", k=32) for s in range(2)]
        post_t = sb("post", [128, 24, NE], BF16)
        post = [post_t[:, ct, :] for ct in range(24)]
        gi = sb("gi", [128, 17152], BF16)
        gbc = gi[:, 0:4096].bitcast(F32)
        xin = [gi[:, 4096 + 4096 * b:8192 + 4096 * b].bitcast(F32) for b in range(2)]
        hnb = [gi[:, 12288 + 2048 * b:14336 + 2048 * b] for b in range(2)]
        A_ = [[gi[:, 2048 * b + 1024 * hl:2048 * b + 1024 * (hl + 1)].rearrange("p (r l) -> p r l", r=8) for hl in range(2)] for b in range(2)]
        decT = [gi[:, 4096 + 1024 * b:5120 + 1024 * b].rearrange("p (r l) -> p r l", r=8) for b in range(2)]
        MT = [gi[:, 6144 + 1024 * b:7168 + 1024 * b].rearrange("p (r l) -> p r l", r=8) for b in range(2)]
        CTe = [gi[:, 512 * b:512 * (b + 1)].rearrange("p (r l) -> p r l", r=4) for b in range(4)]
        CBm = gi[:, 9216:9728].rearrange("p (g l) -> p g l", g=4)
        xdt = gi[:, 9728:11776]
        xdd = gi[:, 11776:13824]
        Btok = gi[:, 13824:14336]
        stg_f = gi[:, 0:4096].bitcast(F32)
        ysc_t = gi[:, 0:16896].rearrange("p (f t) -> p f t", f=16)
        ysc = [ysc_t[:, f, :] for f in range(16)]
        Sf = sb("Sf", [128, 2048], F32)
        Sb = sb("Sb", [128, 2048], BF16)
        Bsb = [Sf[:, 512 * b:512 * (b + 1)] for b in range(2)]
        Csb = [Sf[:, 1024 + 512 * b:1536 + 512 * b] for b in range(2)]
        St = [post_t[:, 16 + 4 * k:20 + 4 * k, :].rearrange("p a b -> p (a b)").bitcast(F32)[:, 0:2048]
              .rearrange("p (j n) -> p j n", j=16) for k in range(2)]
        cm = sb("cm", [128, 512], F32)
        identf, triL, triU, onesf = cm[:, 0:128], cm[:, 128:256], cm[:, 256:384], cm[:, 384:512]
        identb = sb("identb", [128, 128], BF16)
        onesb = sb("onesb", [128, 128], BF16)
        triLb = sb("triLb", [128, 128], BF16)
        triUb = sb("triUb", [128, 128], BF16)
        dth_all = sb("dth_all", [128, 8, 2, 32], BF16)
        dtf = sb("dtf", [128, 32], F32)
        penb = sb("penb", [128, 128], BF16)
        negacs = sb("negacs", [128, 32], F32)
        sq16 = sb("sq16", [128, 16, 16], F32)
        cvec = sb("cvec", [128, 216], F32)
        hvec = sb("hvec", [128, 128], F32)
        dtb_bc, alog_bc = hvec[:, 0:32], hvec[:, 32:64]
        a_bc = sb("a_bc", [128, 32], F32)
        wdt = sb("wdt", [128, 16, 32], BF16)
        dtP = sb("dtP", [128, 8, 32], F32)
        dtaP = sb("dtaP", [128, 8, 32], F32)
        dtO, dtaO = dtP, dtaP
        dtH = sb("dtH", [16, 2, 32], F32)
        dtS = sb("dtS", [16, 2, 32], F32)
        sm = sb("sm", [128, 8, 32], F32)
        eacs = sb("eacs", [128, 32], BF16)
        eacsT = sb("eacsT", [32, 128], BF16)
        ss = sb("ss", [128, 8], F32)
        sraw = gi[:, 12288:15360].bitcast(F32).rearrange("p (a b c) -> p a b c", a=24, b=16)
        sacc = sb("sacc", [128, 2, 64], F32)
        sxp = sb("sxp", [128, 24, 16], F32)
        svraw = sb("svraw", [128, 16, 16, 3], F32)
        svacc = sb("svacc", [128, 48], F32)
        tsb = sb("tsb", [128, 16, 16], F32)
        xlastP = sb("xlastP", [128, 24, 3], F32)
        xlastE = sb("xlastE", [128, 24, 3], F32)
        vlast = sb("vlast", [128, 16, 2], F32)
        szs = sb("szs", [128, 16, 16], F32)
        sqb = [sb("sqb%d" % b, [128, 512], BF16) for b in range(2)]
        dAT = sb("dAT", [32, 2, 16], F32)
        decx = sb("decx", [128, 16, 16], F32)
        dtx = sb("dtx", [128, 16, 16], F32)
        xdts = sb("xdts", [128, 16, 16], F32)
        ysT = sb("ysT", [128, 16, 16], F32)
        yg_s = sb("yg_s", [128, 2, 16, 16], F32)
        ysn = sb("ysn", [128, 32, 16], BF16)
        junk = sb("junk", [128, 128], F32)
        rs16 = sb("rs16", [128, 2, 16], F32)
        ps = E(nc.psum_tensor("ps", [128, 4096], F32))

        def bank(i):
            return ps[:, 512 * i:512 * (i + 1)]

        def bank_bf(i):
            return ps[:, 512 * i:512 * (i + 1)].bitcast(BF16)

        S = Sched(nc, sems)
        KPS = lambda i: ("ps", i)

        def dbg(name, ap, shape, reads):
            if not debug:
                return
            d = dout("dbg_" + name, list(shape))
            dbg_outs[name] = shape
            S.dma("sp", d, ap, reads=reads, writes=[("dbg", name)])
            out_keys.append(("dbg", name))

        def rsqrt_ip(ap, keys):
            S.add("act", lambda e: e.activation(out=ap, in_=ap, func=AF.Sqrt), reads=keys, writes=keys)
            S.add("dve", lambda e: e.reciprocal(out=ap, in_=ap), reads=keys, writes=keys)

        S.dma("sp", cm[:], cmat_d, writes=["cm"])
        S.dma("sp", cvec[:], cvec_d, writes=["cvec"])
        S.dma("sp", hvec[:], hvec_d, writes=["hvec"])
        S.dma("sp", gbc, normw_bc, writes=["gbc"])
        S.add("act", lambda e: e.copy(out=identb[:], in_=identf), reads=["cm"], writes=["identb"])
        S.add("act", lambda e: e.copy(out=onesb[:], in_=onesf), reads=["cm"], writes=["onesb"])
        S.add("act", lambda e: e.copy(out=triLb[:], in_=triL), reads=["cm"], writes=["triLb"])
        S.add("act", lambda e: e.copy(out=triUb[:], in_=triU), reads=["cm"], writes=["triUb"])
        S.add("act", lambda e: e.activation(out=penb[:], in_=triU, func=AF.Identity, scale=-30000.0), reads=["cm"], writes=["penb"])
        S.add("act", lambda e: e.activation(out=a_bc[:], in_=alog_bc, func=AF.Exp), reads=["hvec"], writes=["a_bc0"])
        S.add("dve", lambda e: e.tensor_scalar(out=a_bc[:], in0=a_bc[:], scalar1=-1.0, scalar2=None, op0=ALU.mult),
              reads=["a_bc0"], writes=["a_bc"])
        S.dma("pool", wdt[:], w_in.rearrange("(k p) c -> p k c", p=128)[:, :, C_DT:C_DT + 32], writes=["wdt"])
        for b in range(2):
            S.add("dve", lambda e, b=b: e.memset(raw[b][:, 0:1064], 0.0), writes=ck("raw", b, 0, 1064))
        S.add("dve", lambda e: e.memset(ss[:], 0.0), writes=[("ss", k) for k in range(8)])
        w_in_v = w_in.rearrange("(k p) c -> p k c", p=128)
        w_out_v = w_out.rearrange("(k p) c -> p k c", p=128)

        tile_ctr = [0]

        def hn_tile(srcs, R, col0, single_bank=False):
            t = tile_ctr[0]
            tile_ctr[0] += 1
            b = t % 2
            kx, kh = ("xin", b), ("hnb", b)
            if len(srcs) > 1 or srcs[0][1] != R:
                S.add("dve", lambda e: e.memset(xin[b][0:R, :], 0.0), writes=[kx])
            for (r0, n, ap) in srcs:
                S.dma("sp", xin[b][r0:r0 + n, :], ap, writes=[kx])
            sscol = ss[0:R, b:b + 1]
            S.add("act", lambda e: e.activation(out=hnb[b][0:R, :], in_=xin[b][0:R, :], func=AF.Square, accum_out=sscol),
                  reads=[kx], writes=[kh, ("ss", b)])
            S.add("dve", lambda e: e.tensor_scalar(out=sscol, in0=sscol, scalar1=1.0 / D, scalar2=EPS, op0=ALU.mult, op1=ALU.add),
                  reads=[("ss", b)], writes=[("ss", b)])
            rsqrt_ip(sscol, [("ss", b)])
            S.add("dve", lambda e: e.scalar_tensor_tensor(out=hnb[b][0:R, :], in0=xin[b][0:R, :], scalar=sscol, in1=gbc[0:R, :],
                                                          op0=ALU.mult, op1=ALU.mult),
                  reads=[kx, ("ss", b), "gbc"], writes=[kh])
            S.add("dve", lambda e: e.memset(sscol, 0.0), writes=[("ss", b)])

            def stage_b():
                if single_bank:
                    ptr1 = bank_bf(0).rearrange("p (k t) -> p k t", k=8)
                    for half in range(2):
                        for k8 in range(8):
                            kt = 8 * half + k8
                            S.add("pe", lambda e, kt=kt, k8=k8: e.transpose(out=ptr1[:, k8, 0:R], in_=hnb[b][0:R, kt * 128:(kt + 1) * 128],
                                                                            identity=identb[0:R, 0:R]),
                                  reads=[kh, "identb"], writes=[KPS(0)])
                        if half == 0:
                            S.add("act", lambda e: e.copy(out=hnT[:, 0:8, col0:col0 + R], in_=ptr1[:, :, 0:R]),
                                  reads=[KPS(0)], writes=ck("hnTa", 0, col0, col0 + R))
                        else:
                            S.add("dve", lambda e: e.tensor_copy(out=hnT[:, 8:16, col0:col0 + R], in_=ptr1[:, :, 0:R]),
                                  reads=[KPS(0)], writes=ck("hnT", 0, col0, col0 + R))
                    return
                pb = 2 * b
                ptr = ps[:, 512 * pb:512 * (pb + 2)].bitcast(BF16).rearrange("p (k t) -> p k t", k=16)
                for kt in range(16):
                    S.add("pe", lambda e, kt=kt: e.transpose(out=ptr[:, kt, 0:R], in_=hnb[b][0:R, kt * 128:(kt + 1) * 128],
                                                             identity=identb[0:R, 0:R]),
                          reads=[kh, "identb"], writes=[KPS(pb + kt // 8)])
                S.add("act", lambda e: e.copy(out=hnT[:, 0:8, col0:col0 + R], in_=ptr[:, 0:8, 0:R]),
                      reads=[KPS(pb)], writes=ck("hnTa", 0, col0, col0 + R))
                S.add("dve", lambda e: e.tensor_copy(out=hnT[:, 8:16, col0:col0 + R], in_=ptr[:, 8:16, 0:R]),
                      reads=[KPS(pb + 1)], writes=ck("hnT", 0, col0, col0 + R))
            return stage_b

        def hn_tiles(specs):
            pend = None
            for sp in specs:
                nb = hn_tile(*sp)
                if pend is not None:
                    pend()
                pend = nb
            pend()

        wctr = [0]
        bank_pool = [list(range(7))]
        bctr = [0]

        pe_deferred = []
        bg = [None, 0]

        def run_jobs(jobs, PF=5):
            slots = {}

            def issue(j):
                s = wctr[0] % NWS
                wctr[0] += 1
                slots[j] = s
                c = jobs[j][0]
                S.dma("pool", Wslot[s][:], w_in_v[:, :, c:c + 128], writes=[("W", s)])
            for j in range(min(PF, len(jobs))):
                issue(j)
            for j, (wc, blocks, epi, fin) in enumerate(jobs):
                if j + PF < len(jobs):
                    issue(j + PF)
                s = slots[j]
                for (c0, c1) in blocks:
                    bp = bank_pool[0]
                    bi = bp[bctr[0] % len(bp)]
                    bctr[0] += 1
                    pap = bank(bi)[:, 0:c1 - c0]
                    for kt in range(16):
                        S.add("pe", lambda e, kt=kt, s=s, c0=c0, c1=c1, pap=pap:
                              e.matmul(pap, lhsT=Wslot[s][:, kt, :], rhs=hnT[:, kt, c0:c1], start=(kt == 0), stop=(kt == 15)),
                              reads=[("W", s)] + ck("hnT", 0, c0, c1) + ck("hnTa", 0, c0, c1), writes=[KPS(bi)])
                    while pe_deferred:
                        pe_deferred.pop(0)()
                    epi(c0, c1, pap, KPS(bi))
                    if bg[0] is not None:
                        for _ in range(bg[1]):
                            if next(bg[0], "done") == "done":
                                bg[0] = None
                                break
                if fin is not None:
                    fin()
                if bg[0] is not None and next(bg[0], "done") == "done":
                    bg[0] = None
            while pe_deferred:
                pe_deferred.pop(0)()

        def dt_tile(c0, R, dst_dt, dst_dta, mask_ap, kd, hl=None):
            pap = bank(7)[0:R, 0:32]
            for kt in range(16):
                S.add("pe", lambda e, kt=kt: e.matmul(pap, lhsT=hnT[:, kt, c0:c0 + R], rhs=wdt[:, kt, :], start=(kt == 0), stop=(kt == 15)),
                      reads=["wdt"] + ck("hnT", 0, c0, c0 + R) + ck("hnTa", 0, c0, c0 + R), writes=[KPS(7)])
            t1 = sm[0:R, 0, :]
            S.add("dve", lambda e: e.tensor_tensor(out=t1, in0=pap, in1=dtb_bc[0:R, :], op=ALU.add),
                  reads=[KPS(7), "hvec"], writes=[("sm", 0)])
            S.add("act", lambda e: e.activation(out=t1, in_=t1, func=AF.Exp), reads=[("sm", 0)], writes=[("sm", 0)])
            S.add("act", lambda e: e.activation(out=dst_dt, in_=t1, func=AF.Ln, bias=1.0), reads=[("sm", 0)], writes=[kd])
            if mask_ap is not None:
                S.add("dve", lambda e: e.tensor_scalar(out=dst_dt, in0=dst_dt, scalar1=mask_ap, scalar2=None, op0=ALU.mult),
                      reads=[kd, "hvec"], writes=[kd])
            S.add("dve", lambda e: e.tensor_tensor(out=dst_dta, in0=dst_dt, in1=a_bc[0:R, :], op=ALU.mult),
                  reads=[kd, "a_bc"], writes=[kd])
            if hl is not None:
                S.add("dve", lambda e: e.tensor_copy(out=hl[:, 0, :], in_=dst_dta), reads=[kd], writes=[kd])
                S.add("dve", lambda e: e.tensor_copy(out=dtf[:], in_=hl[:, 0, :]), reads=[kd], writes=["dtf"])
                S.add("dve", lambda e: e.tensor_tensor(out=hl[:, 1, :], in0=dst_dta, in1=dtf[:], op=ALU.subtract), reads=[kd, "dtf"], writes=[kd])

        def cw(i, ct):
            return cvec[:, 24 * i + ct:24 * i + ct + 1]

        def cb_(ct):
            return cvec[:, 96 + ct:97 + ct]

        def conv4(rb, ct, lo, hi, dst):
            n = hi - lo
            ka = ck("acc", rb, lo, hi)
            S.add("act", lambda e: e.activation(out=acc[rb][:, lo:hi], in_=raw[rb][:, 3 + lo:3 + hi], func=AF.Identity,
                                                bias=cb_(ct), scale=cw(3, ct)),
                  reads=ck("raw", rb, 3 + lo, 3 + hi) + ["cvec"], writes=ka)
            for i in range(3):
                S.add("dve",
                      lambda e, i=i: e.scalar_tensor_tensor(out=acc[rb][:, lo:hi], in0=raw[rb][:, i + lo:i + hi], scalar=cw(i, ct),
                                                            in1=acc[rb][:, lo:hi], op0=ALU.mult, op1=ALU.add),
                      reads=ck("raw", rb, i + lo, i + hi) + ["cvec"] + ka, writes=ka)
            S.add("act", lambda e: e.activation(out=dst[:, lo:hi], in_=acc[rb][:, lo:hi], func=AF.Silu),
                  reads=ka, writes=ck("post", ct, lo, hi))

        hn_tiles([([(0, 128, xs[128 * i:128 * (i + 1), :])], 128, 128 * i) for i in range(8)])
        for i in range(8):
            dt_tile(128 * i, 128, dtP[:, i, :], dtaP[:, i, :], hvec[:, 64 + i:65 + i], ("dtP", i))
        jobs = []
        for ct in range(20):
            rb = ct % 2

            def epi(c0, c1, pap, pk, rb=rb):
                S.add("act", lambda e: e.copy(out=raw[rb][:, 3 + c0:3 + c1], in_=pap), reads=[pk], writes=ck("raw", rb, 3 + c0, 3 + c1))

            def fin(rb=rb, ct=ct):
                S.add("dve", lambda e: e.tensor_copy(out=xlastP[:, ct, :], in_=raw[rb][:, 1024:1027]),
                      reads=ck("raw", rb, 1024, 1027), writes=[("xlastP", ct)])
                conv4(rb, ct, 0, 1024, post[ct])
            jobs.append((C_XBC + 128 * ct, [(0, 512), (512, 1024)], epi, fin))
        run_jobs(jobs)
        S.add("dve", lambda e: e.memset(xlastP[:, 20:24, :], 0.0), writes=[("xlastP", c) for c in range(20, 24)])

        S.add("dve", lambda e: e.memset(Sf[:], 0.0), writes=["Sf"])
        S.add("dve", lambda e: e.memset(Sb[:], 0.0), writes=["Sb"])
        b1bf = bank_bf(1)
        xtok_ps = ps[:, 1024:2048].bitcast(BF16).rearrange("p (c t) -> p c t", c=16)
        btok_ps = bank_bf(0)[:, 0:512].rearrange("p (g t) -> p g t", g=4)
        sm_ps = bank(1)[:, 128:224]
        gab = [0]

        def ssd_chunk(c0, L, dt_ap, dta_ap, kd, with_y):
            S.add("pe", lambda e: e.matmul(sm_ps[0:L, 0:32], lhsT=triU[0:L, 0:L], rhs=dta_ap, start=True, stop=True),
                  reads=["cm", kd], writes=[KPS(1)])
            S.add("pe", lambda e: e.matmul(sm_ps[:, 32:64], lhsT=onesf[0:L, :], rhs=dta_ap, start=True, stop=True),
                  reads=["cm", kd], writes=[KPS(1)])
            if with_y:
                S.add("pe", lambda e: e.matmul(sm_ps[0:L, 64:96], lhsT=triL[0:L, 0:L], rhs=dta_ap, start=True, stop=True),
                      reads=["cm", kd], writes=[KPS(1)])
                for g in range(4):
                    S.add("pe", lambda e, g=g: e.matmul(bank(0)[:, 128 * g:128 * (g + 1)], lhsT=post[16 + g][:, c0:c0 + 128],
                                                        rhs=post[20 + g][:, c0:c0 + 128], start=True, stop=True),
                          reads=ck("post", 16 + g, c0, c0 + 128) + ck("post", 20 + g, c0, c0 + 128), writes=[KPS(0)])
            for ct in range(16):
                S.add("pe", lambda e, ct=ct: e.transpose(out=xtok_ps[0:L, ct, :], in_=post[ct][:, c0:c0 + L], identity=identb[:]),
                      reads=ck("post", ct, c0, c0 + L) + ["identb"], writes=[KPS(2), KPS(3)])
            dend, cd, w2 = sm[0:L, 1, :], sm[:, 2, :], sm[0:L, 3, :]
            S.add("act", lambda e: e.activation(out=dend, in_=sm_ps[0:L, 0:32], func=AF.Exp), reads=[KPS(1)], writes=[("sm", 1)])
            S.add("act", lambda e: e.activation(out=cd, in_=sm_ps[:, 32:64], func=AF.Exp), reads=[KPS(1)], writes=[("sm", 2)])
            if with_y:
                S.add("act", lambda e: e.activation(out=eacs[:], in_=sm_ps[:, 64:96], func=AF.Exp), reads=[KPS(1)], writes=["eacs"])
                S.add("act", lambda e: e.activation(out=negacs[:], in_=sm_ps[:, 64:96], func=AF.Identity, scale=-1.0), reads=[KPS(1)], writes=["negacs"])
                S.add("dve", lambda e: e.tensor_tensor(out=CBm, in0=bank(0).rearrange("p (g l) -> p g l", g=4),
                                                       in1=triL.unsqueeze(1).to_broadcast([128, 4, 128]), op=ALU.mult),
                      reads=[KPS(0), "cm"], writes=["CBm"])
            S.add("dve", lambda e: e.tensor_tensor(out=w2, in0=dt_ap, in1=dend, op=ALU.mult), reads=[kd, ("sm", 1)], writes=[("sm", 3)])
            if with_y:
                S.add("pe", lambda e: e.transpose(out=b1bf[0:32, 0:128], in_=eacs[:], identity=identb[:]),
                      reads=["eacs", "identb"], writes=[KPS(1)])
                S.add("act", lambda e: e.copy(out=eacsT[:], in_=b1bf[0:32, 0:128]), reads=[KPS(1)], writes=["eacsT"])
            for g in range(4):
                S.add("pe", lambda e, g=g: e.transpose(out=btok_ps[0:L, g, :], in_=post[16 + g][:, c0:c0 + L], identity=identb[:]),
                      reads=ck("post", 16 + g, c0, c0 + L) + ["identb"], writes=[KPS(0)])
            xt3 = xtok_ps[0:L].rearrange("p c t -> p (c t)").rearrange("p (h q) -> p h q", h=32)
            S.add("dve", lambda e: e.tensor_tensor(out=xdd[0:L, :].rearrange("p (h q) -> p h q", h=32), in0=xt3,
                                                   in1=w2.unsqueeze(2).to_broadcast([L, 32, 64]), op=ALU.mult),
                  reads=[KPS(2), KPS(3), ("sm", 3)], writes=["xdd"])
            S.add("act", lambda e: e.copy(out=Btok[0:L, :], in_=btok_ps[0:L].rearrange("p g t -> p (g t)")), reads=[KPS(0)], writes=["Btok"])
            if with_y:
                S.add("dve", lambda e: e.tensor_tensor(out=xdt[0:L, :].rearrange("p (h q) -> p h q", h=32), in0=xt3,
                                                       in1=dt_ap.unsqueeze(2).to_broadcast([L, 32, 64]), op=ALU.mult),
                      reads=[KPS(2), KPS(3), kd], writes=["xdt"])
                dth = dth_all[:, (c0 - OWN0) // 128, :, :]
                def stage1(g):
                    ab = g % 2
                    for r in range(8):
                        h = 8 * g + r
                        outp = bank(4 + r // 4)[:, 128 * (r % 4):128 * (r % 4 + 1)]
                        for hl in range(2):
                            S.add("pe", lambda e, h=h, hl=hl, outp=outp: e.matmul(outp, lhsT=dth[:, hl, h:h + 1].to_broadcast([128, 128]), rhs=triLb[:],
                                                                                 start=(hl == 0), stop=False),
                                  reads=[kd, "triLb"], writes=[KPS(4 + r // 4)])
                        S.add("pe", lambda e, outp=outp: e.matmul(outp, lhsT=identb[:], rhs=penb[:], start=False, stop=True),
                              reads=["identb", "penb"], writes=[KPS(4 + r // 4)])
                    for r in range(8):
                        h = 8 * g + r
                        S.add("act", lambda e, r=r, h=h, ab=ab: e.activation(out=decT[ab][:, r, :], in_=bank(4 + r // 4)[:, 128 * (r % 4):128 * (r % 4 + 1)],
                                                                             func=AF.Exp, bias=negacs[:, h:h + 1]),
                              reads=[KPS(4 + r // 4), "negacs"], writes=[("decT", ab, r // 4)])

                def stage2a(g):
                    ab = g % 2
                    S.add("dve", lambda e: e.tensor_tensor(out=MT[ab], in0=decT[ab],
                                                           in1=CBm[:, g, :].unsqueeze(1).to_broadcast([128, 8, 128]), op=ALU.mult),
                          reads=[("decT", ab, 0), ("decT", ab, 1), "CBm"], writes=[("MT", ab)])
                    for hh in range(2):
                        qq = 2 * g + hh
                        cbi = qq % 4
                        eb = 6 if qq % 2 == 0 else 2
                        for r4 in range(4):
                            h = 8 * g + 4 * hh + r4
                            S.add("pe", lambda e, r4=r4, h=h, eb=eb: e.matmul(bank(eb)[:, 128 * r4:128 * (r4 + 1)],
                                                                              lhsT=identb[0:32, h:h + 1].to_broadcast([32, 128]),
                                                                              rhs=eacsT[:], start=True, stop=True),
                                  reads=["identb", "eacsT"], writes=[KPS(eb)])
                        S.add("dve", lambda e, cbi=cbi, eb=eb: e.tensor_tensor(out=CTe[cbi], in0=bank(eb).rearrange("p (r l) -> p r l", r=4),
                                                                               in1=post[20 + g][:, c0:c0 + 128].unsqueeze(1).to_broadcast([128, 4, 128]),
                                                                               op=ALU.mult),
                              reads=[KPS(eb)] + ck("post", 20 + g, c0, c0 + 128), writes=[("CTe", cbi), "stg"])

                def stage2b(g):
                    ab = g % 2
                    for hh in range(2):
                        qq = 2 * g + hh
                        cbi = qq % 4
                        yb = 7 if qq % 2 == 0 else 3
                        for t2 in range(2):
                            tt = 2 * hh + t2
                            yps = bank(yb)[:, 128 * t2:128 * (t2 + 1)]
                            for hp in range(2):
                                h = 8 * g + 2 * tt + hp
                                r = 2 * tt + hp
                                r4 = r - 4 * hh
                                S.add("pe", lambda e, h=h, r=r, hp=hp, yps=yps: e.matmul(yps[64 * hp:64 * hp + 64, :], lhsT=xdt[:, 64 * h:64 * h + 64],
                                                                                        rhs=MT[ab][:, r, :], start=True, stop=False),
                                      reads=["xdt", ("MT", ab)], writes=[KPS(yb)])
                                S.add("pe", lambda e, h=h, r4=r4, hp=hp, yps=yps, cbi=cbi: e.matmul(yps[64 * hp:64 * hp + 64, :], lhsT=Sb[:, 64 * h:64 * h + 64],
                                                                                                   rhs=CTe[cbi][:, r4, :], start=False, stop=True),
                                      reads=["Sb", ("CTe", cbi)], writes=[KPS(yb)])
                    for hh in range(2):
                        qq = 2 * g + hh
                        yb = 7 if qq % 2 == 0 else 3
                        for t2 in range(2):
                            tt = 2 * hh + t2
                            ct = 4 * g + tt
                            yps = bank(yb)[:, 128 * t2:128 * (t2 + 1)]
                            pk = ck("post", ct, c0, c0 + 128)
                            S.add("dve", lambda e, ct=ct, yps=yps: e.scalar_tensor_tensor(out=post[ct][:, c0:c0 + 128], in0=post[ct][:, c0:c0 + 128],
                                                                                         scalar=cvec[:, 200 + ct:201 + ct], in1=yps,
                                                                                         op0=ALU.mult, op1=ALU.add),
                                  reads=pk + [KPS(yb), "cvec"], writes=pk)
                stage1(0)
                stage1(1)
                stage2a(0)
                stage1(2)
                stage2a(1)
                stage2b(0)
                stage1(3)
                stage2a(2)
                stage2b(1)
                stage2a(3)
                stage2b(2)
                stage2b(3)
            sb0 = 2 if with_y else 4
            for g in range(4):
                S.add("pe", lambda e, g=g: e.matmul(bank(sb0 + g), lhsT=Btok[0:L, 128 * g:128 * (g + 1)], rhs=xdd[0:L, 512 * g:512 * (g + 1)],
                                                    start=True, stop=True),
                      reads=["Btok", "xdd"], writes=[KPS(sb0 + g)])
            S.add("pool", lambda e: e.tensor_tensor(out=Sf[:].rearrange("p (h q) -> p h q", h=32), in0=Sf[:].rearrange("p (h q) -> p h q", h=32),
                                                    in1=cd.unsqueeze(2).to_broadcast([128, 32, 64]), op=ALU.mult),
                  reads=["Sf", ("sm", 2)], writes=["Sf"])
            for g in range(4):
                S.add("dve", lambda e, g=g: e.tensor_tensor(out=Sf[:, 512 * g:512 * (g + 1)], in0=Sf[:, 512 * g:512 * (g + 1)], in1=bank(sb0 + g), op=ALU.add),
                      reads=["Sf", KPS(sb0 + g)], writes=["Sf"])
            S.add("act", lambda e: e.copy(out=Sb[:], in_=Sf[:]), reads=["Sf"], writes=["Sb"])

        dkeys = [("dtP", i) for i in range(8)]
        for c in range(8):
            psm = bank(1)[:, 32 * c:32 * (c + 1)]
            S.add("pe", lambda e, c=c, psm=psm: e.matmul(psm, lhsT=triU, rhs=dtaP[:, c, :], start=True, stop=(c == 7)),
                  reads=["cm"] + dkeys, writes=[KPS(1)])
            for c2 in range(c + 1, 8):
                S.add("pe", lambda e, c2=c2, psm=psm: e.matmul(psm, lhsT=onesf, rhs=dtaP[:, c2, :], start=False, stop=(c2 == 7)),
                      reads=["cm"] + dkeys, writes=[KPS(1)])
        smf = sm[:].rearrange("p c h -> p (c h)")
        smk = [("sm", i) for i in range(8)]
        S.add("act", lambda e: e.activation(out=smf, in_=bank(1)[:, 0:256], func=AF.Exp), reads=[KPS(1)], writes=smk)
        S.add("dve", lambda e: e.tensor_tensor(out=smf, in0=smf, in1=dtP[:].rearrange("p c h -> p (c h)"), op=ALU.mult),
              reads=smk + dkeys, writes=smk)
        stage_bf = big[:, 16896:25408]
        xddP = [stage_bf[:, 2048 * q:2048 * (q + 1)] for q in range(2)]
        BtokP = [stage_bf[:, 4256 + 512 * q:4256 + 512 * (q + 1)] for q in range(2)]
        rawk = ck("raw", 0, 0, 1064) + ck("raw", 1, 0, 1064)
        acck = ck("acc", 0, 0, 1064)
        btokP_ps = bank_bf(1)[:, 512:1024].rearrange("p (g t) -> p g t", g=4)

        def pre_chunk(c):
            b = c % 2
            c0 = 128 * c
            for ct in range(16):
                S.add("pe", lambda e, ct=ct: e.transpose(out=xtok_ps[:, ct, :], in_=post[ct][:, c0:c0 + 128], identity=identb[:]),
                      reads=ck("post", ct, c0, c0 + 128) + ["identb"], writes=[KPS(2), KPS(3)])
            for g in range(4):
                S.add("pe", lambda e, g=g: e.transpose(out=btokP_ps[:, g, :], in_=post[16 + g][:, c0:c0 + 128], identity=identb[:]),
                      reads=ck("post", 16 + g, c0, c0 + 128) + ["identb"], writes=[KPS(1)])
            xt3 = xtok_ps.rearrange("p c t -> p (c t)").rearrange("p (h q) -> p h q", h=32)
            S.add("dve", lambda e: e.tensor_tensor(out=xddP[b].rearrange("p (h q) -> p h q", h=32), in0=xt3,
                                                   in1=sm[:, c, :].unsqueeze(2).to_broadcast([128, 32, 64]), op=ALU.mult),
                  reads=[KPS(2), KPS(3), ("sm", c)], writes=[("xddP", b)] + rawk)
            S.add("act", lambda e: e.copy(out=BtokP[b], in_=btokP_ps.rearrange("p g t -> p (g t)")), reads=[KPS(1)], writes=[("BtokP", b)] + acck)
            for g in range(4):
                S.add("pe", lambda e, g=g: e.matmul(bank(4 + g), lhsT=BtokP[b][:, 128 * g:128 * (g + 1)], rhs=xddP[b][:, 512 * g:512 * (g + 1)],
                                                    start=(c == 0), stop=(c == 7)),
                      reads=[("BtokP", b), ("xddP", b)], writes=[KPS(4 + g)])

        hn_specs = [([(0, 16, xsm), (16, 16, xs[1024:1040, :])], 32, 0)] + \
                   [([(0, 128, xs[1040 + 128 * i:1040 + 128 * (i + 1), :])], 128, OWN0 + 128 * i) for i in range(8)]
        pend = hn_tile(*hn_specs[0], single_bank=True)
        for k in range(9):
            nxt = hn_tile(*hn_specs[k + 1], single_bank=True) if k + 1 < 9 else None
            pend()
            pend = nxt
            if k < 8:
                pre_chunk(k)
        for g in range(4):
            S.add("dve" if g % 2 == 0 else "act", (lambda e, g=g: e.tensor_copy(out=Sf[:, 512 * g:512 * (g + 1)], in_=bank(4 + g))) if g % 2 == 0 else
                  (lambda e, g=g: e.copy(out=Sf[:, 512 * g:512 * (g + 1)], in_=bank(4 + g))),
                  reads=[KPS(4 + g)], writes=[("Sfq", g)])
        S.add("act", lambda e: e.copy(out=Sb[:], in_=Sf[:]), reads=[("Sfq", g) for g in range(4)] + ["Sf"], writes=["Sb", "Sf"])
        S.barrier()

        SRK = lambda ct, i: ("sraw", ct, i)

        S.add("dve", lambda e: e.memset(sraw.rearrange("p a b c -> p (a b c)"), 0.0),
              writes=[SRK(ct, i) for ct in range(24) for i in range(4)] + [("hnb", 0), ("hnb", 1)])
        S.add("dve", lambda e: e.memset(svraw[:].rearrange("p a b c -> p (a b c)"), 0.0), writes=["svraw"])

        def p3_side():
            rounds = [("x", i, half) for i in range(3) for half in range(2)] + [("v", i, 0) for i in range(2)]

            def load(r):
                kind, i, half = rounds[r]
                b = r % 2
                if kind == "x":
                    S.dma("sp", xin[b][0:16, 0:1536], scv[:, i, 1536 * half:1536 * (half + 1)], writes=[("xin", b)])
                else:
                    S.dma("sp", xin[b][0:16, :], ssc[:, i, :], writes=[("xin", b)])
            load(0)
            yield 1
            for r, (kind, i, half) in enumerate(rounds):
                b = r % 2
                if r + 1 < len(rounds):
                    load(r + 1)
                n = 12 if kind == "x" else 16
                pv = bank(7)[:, 0:16 * n].rearrange("p (j t) -> p j t", j=n)
                for j in range(n):
                    S.add("pe", lambda e, j=j, pv=pv, b=b: e.transpose(out=pv[:, j, :], in_=xin[b][0:16, 128 * j:128 * (j + 1)],
                                                                      identity=identf[0:16, 0:16]),
                          reads=[("xin", b), "cm"], writes=[KPS(7)])
                if kind == "x":
                    S.add("act", lambda e, pv=pv, half=half, i=i: e.copy(out=sraw[:, 12 * half:12 * (half + 1), :, i], in_=pv),
                          reads=[KPS(7)], writes=[SRK(ct, i) for ct in range(12 * half, 12 * half + 12)])
                else:
                    S.add("act", lambda e, pv=pv, i=i: e.copy(out=svraw[:, :, :, i], in_=pv), reads=[KPS(7)], writes=["svraw"])
                yield 1
            S.dma("sp", cssd_smp[:, 0:2, :], scv[:, 1:3, :], writes=[("o", "cssd_smp01")])
            S.dma("sp", csc_smp[:, 0:1, :], ssc[:, 1:2, :], writes=[("o", "csc_smp0")])
            out_keys.extend([("o", "cssd_smp01"), ("o", "csc_smp0")])
            dt_tile(0, 16, dtS[:, 0, :], dtS[:, 1, :], None, "dtS")
            yield 1
            dt_tile(HALO0, 16, dtH[:, 0, :], dtH[:, 1, :], hvec[0:16, 72:73], "dtH")
            yield 1
            for i in range(8):
                dt_tile(OWN0 + 128 * i, 128, dtO[:, i, :], dtaO[:, i, :], None, ("dtO", i), hl=dth_all[:, i, :, :])
                yield 1

        def sconv(ct):
            rb = ct % 2
            sr = sraw[:, ct, :, :].rearrange("p b c -> p (b c)")
            sa = sacc[:, rb, :]
            srk = [SRK(ct, i) for i in range(4)]
            S.add("act", lambda e: e.activation(out=sa[:, 3:64], in_=sr[:, 3:64], func=AF.Identity, bias=cb_(ct), scale=cw(3, ct)),
                  reads=srk + ["cvec"], writes=[("sacc", rb)])
            for i in range(3):
                S.add("dve", lambda e, i=i: e.scalar_tensor_tensor(out=sa[:, 3:64], in0=sr[:, i:i + 61], scalar=cw(i, ct), in1=sa[:, 3:64],
                                                                   op0=ALU.mult, op1=ALU.add),
                      reads=srk + ["cvec", ("sacc", rb)], writes=[("sacc", rb)])
            S.add("act", lambda e: e.activation(out=sxp[:, ct, :], in_=sa.rearrange("p (b c) -> p b c", c=4)[:, :, 3], func=AF.Silu),
                  reads=[("sacc", rb)], writes=[("sxp", ct)])

        jobs = []
        for ct in range(24):
            rb = ct % 2

            def epi(c0, c1, pap, pk, rb=rb, ct=ct):
                lo = max(c0, HALO0)
                S.add("act", lambda e: e.copy(out=raw[rb][:, 3 + lo:3 + c1], in_=pap[:, lo - c0:c1 - c0]), reads=[pk], writes=ck("raw", rb, 3 + lo, 3 + c1))
                if c0 == 0:
                    S.add("act", lambda e: e.copy(out=sraw[:, ct, :, 3], in_=pap[:, 0:16]), reads=[pk], writes=[SRK(ct, 3)])
                    S.add("dve", lambda e: e.tensor_copy(out=raw[rb][:, HALO0:HALO0 + 3], in_=xlastP[:, ct, :]), reads=[("xlastP", ct)],
                          writes=ck("raw", rb, HALO0, HALO0 + 3))

            def fin(rb=rb, ct=ct):
                S.add("dve", lambda e: e.tensor_copy(out=xlastE[:, ct, :], in_=raw[rb][:, NE:NE + 3]),
                      reads=ck("raw", rb, NE, NE + 3), writes=[("xlastE", ct)])
                conv4(rb, ct, HALO0, NE, post[ct])
                if ct >= 4:
                    sconv(ct - 4)
            jobs.append((C_XBC + 128 * ct, BLKS, epi, fin))
        bg[0], bg[1] = p3_side(), 1
        run_jobs(jobs)
        while bg[0] is not None:
            if next(bg[0], "done") == "done":
                bg[0] = None
        for ct in range(20, 24):
            sconv(ct)

        trc = [0]

        def tok_rows(src_fn, ntile, R, dst_dram, okey, half_tiles=8):
            for h0 in range(0, ntile, half_tiles):
                n = min(half_tiles, ntile - h0)
                ab = trc[0] % 2
                trc[0] += 1
                b0 = 4 if ab == 0 else 0
                stg = acc[ab]
                sk = ck("acc", ab, 0, 1024)
                for j in range(n):
                    bi = b0 + j // 4
                    S.add("pe", lambda e, j=j, bi=bi, h0=h0: e.transpose(out=bank(bi)[0:R, 128 * (j % 4):128 * (j % 4 + 1)], in_=src_fn(h0 + j),
                                                                          identity=identf),
                          reads=["cm", okey + "_src"], writes=[KPS(bi)])
                nb = (n + 3) // 4
                for q in range(nb):
                    w = min(4, n - 4 * q) * 128
                    S.add("act", lambda e, q=q, w=w, stg=stg, b0=b0: e.copy(out=stg[0:R, 512 * q:512 * q + w], in_=bank(b0 + q)[0:R, 0:w]),
                          reads=[KPS(b0 + q)], writes=sk)
                S.dma("sp", dst_dram[:, 128 * h0:128 * (h0 + n)], stg[0:R, 0:128 * n], reads=sk, writes=[("o", okey, h0)])
                out_keys.append(("o", okey, h0))

        S.add("dve", lambda e: e.tensor_copy(out=junk[:, 0:1], in_=xlastE[:, 0, 0:1]), reads=[("xlastE", c) for c in range(24)], writes=["cssd_fin_src"])
        tok_rows(lambda ct: xlastE[:, ct, :], 24, 3, cssd_fin, "cssd_fin")
        S.add("dve", lambda e: e.tensor_copy(out=junk[:, 1:2], in_=sraw[:, 0, 0, 0:1]), reads=[SRK(ct, 3) for ct in range(24)], writes=["cssd_smp_src"])
        tok_rows(lambda ct: sraw[:, ct, :, 3], 24, 16, cssd_smp[:, 2, :], "cssd_smp")
        S.barrier()

        ssd_chunk(HALO0, 16, dtH[:, 0, :], dtH[:, 1, :], "dtH", False)
        for i in range(8):
            ssd_chunk(OWN0 + 128 * i, 128, dtO[:, i, :], dtaO[:, i, :], ("dtO", i), True)
        stg3 = stg_f.rearrange("p (j n) -> p j n", j=16)
        for j in range(16):
            bi = 4 + (j // 4) % 2
            S.add("pe", lambda e, j=j, bi=bi: e.transpose(out=bank(bi)[:, 128 * (j % 4):128 * (j % 4 + 1)], in_=Sf[:, 128 * j:128 * (j + 1)], identity=identf),
                  reads=["Sf", "cm"], writes=[KPS(bi)])
            if j % 4 == 3:
                q = j // 4
                S.add("act", lambda e, q=q, bi=bi: e.copy(out=stg_f[:, 512 * q:512 * (q + 1)], in_=bank(bi)), reads=[KPS(bi)],
                      writes=["stg"] + [("CTe", i) for i in range(4)])
        S.dma("sp", ssm_fin.rearrange("(j p) n -> p j n", p=128), stg3, reads=["stg"], writes=[("o", "ssm_fin")])
        out_keys.append(("o", "ssm_fin"))

        S.barrier()
        S.add("act", lambda e: e.activation(out=dtS[:, 1, :], in_=dtS[:, 1, :], func=AF.Exp), reads=["dtS"], writes=["dtS"])
        for q in range(2):
            src = dtS[:, 1, :] if q == 0 else dtS[:, 0, :]
            S.add("pe", lambda e, q=q, src=src: e.transpose(out=bank(7)[0:32, 16 * q:16 * (q + 1)], in_=src, identity=identf[0:16, 0:16]),
                  reads=["dtS", "cm"], writes=[KPS(7)])
        S.add("act", lambda e: e.copy(out=dAT[:].rearrange("h q b -> h (q b)"), in_=bank(7)[0:32, 0:32]), reads=[KPS(7)], writes=["dAT"])
        for q in range(2):
            pv = bank(4 + q)[:, 0:256].rearrange("p (j b) -> p j b", j=16)
            for j in range(16):
                for hp in range(2):
                    h = 2 * j + hp
                    S.add("pe", lambda e, q=q, j=j, hp=hp, h=h, pv=pv: e.matmul(pv[64 * hp:64 * hp + 64, j, :], lhsT=identf[0:32, h:h + 1].to_broadcast([32, 64]),
                                                                                rhs=dAT[:, q, :], start=True, stop=True),
                          reads=["cm", "dAT"], writes=[KPS(4 + q)])
            dst = decx if q == 0 else dtx
            S.add("act", lambda e, dst=dst, pv=pv: e.copy(out=dst[:], in_=pv), reads=[KPS(4 + q)], writes=["decx" if q == 0 else "dtx"])
        S.add("dve", lambda e: e.tensor_tensor(out=xdts[:], in0=sxp[:, 0:16, :], in1=dtx[:], op=ALU.mult),
              reads=["dtx"] + [("sxp", c) for c in range(16)], writes=["xdts"])
        S.add("dve", lambda e: e.memset(ysT[:].rearrange("p a b -> p (a b)"), 0.0), writes=["ysT"])
        bctok = Sb[:].bitcast(F32)
        for q in range(8):
            S.add("pe", lambda e, q=q: e.transpose(out=bank(6 + q // 4)[0:16, 128 * (q % 4):128 * (q % 4 + 1)], in_=sxp[:, 16 + q, :], identity=identf),
                  reads=[("sxp", 16 + q), "cm"], writes=[KPS(6 + q // 4)])
        for q in range(2):
            S.add("act", lambda e, q=q: e.copy(out=bctok[0:16, 512 * q:512 * (q + 1)], in_=bank(6 + q)[0:16, :]), reads=[KPS(6 + q)], writes=["Sb"])
        S.dma("sp", scr, bctok[0:16, :], reads=["Sb"], writes=["scr"])
        sssm_v = sssm.rearrange("b (j p) n -> b p j n", p=128)
        ssm_smp_v = ssm_smp.rearrange("b (j p) n -> b p j n", p=128)
        Bc = [Sf[:, 1024 * k:1024 * (k + 1)] for k in range(2)]

        def stkeys(k):
            return [kk for ct in range(16 + 4 * k, 20 + 4 * k) for kk in ck("post", ct, 0, NE)]

        def smp_load(b):
            k = b % 2
            S.dma("sp", St[k], sssm_v[b], writes=stkeys(k))
            S.dma("sp", Bc[k], scr[b:b + 1, :].to_broadcast([128, 1024]), reads=["scr"], writes=[("Bc", k), "Sf"])

        def sample_gen():
            smp_load(0)
            for b in range(16):
                k = b % 2
                stk = stkeys(k)
                if b + 1 < 16:
                    smp_load(b + 1)
                for j in range(16):
                    g = j // 4
                    S.add("act", lambda e, j=j, b=b, k=k: e.activation(out=St[k][:, j, :], in_=St[k][:, j, :], func=AF.Identity, scale=decx[:, j, b:b + 1]),
                          reads=stk + ["decx"], writes=stk)
                    S.add("dve", lambda e, j=j, b=b, k=k, g=g: e.scalar_tensor_tensor(out=St[k][:, j, :], in0=Bc[k][:, 128 * g:128 * (g + 1)],
                                                                                     scalar=xdts[:, j, b:b + 1], in1=St[k][:, j, :],
                                                                                     op0=ALU.mult, op1=ALU.add),
                          reads=stk + [("Bc", k), "xdts"], writes=stk)
                    S.add("dve", lambda e, j=j, b=b, k=k, g=g: e.scalar_tensor_tensor(out=junk[:], in0=St[k][:, j, :], scalar=1.0,
                                                                                     in1=Bc[k][:, 512 + 128 * g:512 + 128 * (g + 1)], op0=ALU.mult, op1=ALU.mult,
                                                                                     accum_out=ysT[:, j, b:b + 1]),
                          reads=stk + [("Bc", k)], writes=["junk", "ysT"])
                    yield 1
                S.dma("sp", ssm_smp_v[b], St[k], reads=stk, writes=[("o", "ssm_smp", b)])
                out_keys.append(("o", "ssm_smp", b))
        bg[0] = sample_gen()
        bg[1] = 1

        def sample_finish():
            while bg[0] is not None:
                if next(bg[0], "done") == "done":
                    bg[0] = None
            S.add("dve", lambda e: e.tensor_tensor(out=xdts[:], in0=sxp[:, 0:16, :], in1=cvec[:, 200:216].unsqueeze(2).to_broadcast([128, 16, 16]), op=ALU.mult),
                  reads=["xdts", "cvec", "ysT"] + [("sxp", c) for c in range(16)], writes=["xdts"])
            S.add("dve", lambda e: e.tensor_tensor(out=ysT[:], in0=ysT[:], in1=xdts[:], op=ALU.add), reads=["ysT", "xdts"], writes=["ysT"])

        def sample_norm(q, gcol0, kin):
            src = yg_s[:, q, :, :]
            sq = sq16[:]
            S.add("dve", lambda e: e.tensor_tensor(out=sq, in0=src, in1=src, op=ALU.mult), reads=[kin], writes=["sq16"])
            S.add("pe", lambda e: e.matmul(bank(7)[:, 0:256], lhsT=onesf, rhs=sq.rearrange("p a b -> p (a b)"), start=True, stop=True),
                  reads=["sq16", "cm"], writes=[KPS(7)])
            r = rs16[:, q, :]
            S.add("dve", lambda e: e.tensor_reduce(out=r, in_=bank(7)[:, 0:256].rearrange("p (j b) -> p b j", j=16), axis=AX.X, op=ALU.add),
                  reads=[KPS(7)], writes=[("rs16", q)])
            S.add("dve", lambda e: e.tensor_scalar(out=r, in0=r, scalar1=1.0 / D, scalar2=EPS, op0=ALU.mult, op1=ALU.add),
                  reads=[("rs16", q)], writes=[("rs16", q)])
            rsqrt_ip(r, [("rs16", q)])
            S.add("dve", lambda e: e.tensor_tensor(out=sq, in0=src, in1=cvec[:, gcol0:gcol0 + 16].unsqueeze(2).to_broadcast([128, 16, 16]), op=ALU.mult),
                  reads=[kin, "cvec", "sq16"], writes=["sq16"])
            S.add("dve", lambda e: e.tensor_tensor(out=ysn[:, 16 * q:16 * q + 16, :], in0=sq, in1=r.unsqueeze(1).to_broadcast([128, 16, 16]), op=ALU.mult),
                  reads=["sq16", ("rs16", q)], writes=[("ysn", q)])

        bank_pool[0] = [0, 1, 2, 3, 4]
        sqc = [0]

        def sumsq_block(src_ap, bidx, ncols, first, last, rkeys):
            bi = 5 + bidx
            q = sqc[0] % 2
            sqc[0] += 1
            S.add("act", lambda e: e.activation(out=sqb[q][:, 0:ncols], in_=src_ap, func=AF.Square), reads=rkeys, writes=[("sqb", q)])
            pe_deferred.append(lambda: S.add("pe", lambda e: e.matmul(bank(bi)[:, 0:ncols], lhsT=onesb[:], rhs=sqb[q][:, 0:ncols], start=first, stop=last),
                                             reads=[("sqb", q), "onesb"], writes=[KPS(bi)]))

        def finish_norm(dst_list, gcol0, lazy=False):
            rst = acc[1]
            rst_use, rkeys = rst, ck("acc", 1, OWN0, NE)
            for bidx, (c0, c1) in enumerate(BLKS):
                lo = max(c0, OWN0)
                S.add("dve", lambda e, bidx=bidx, lo=lo, c1=c1: e.tensor_scalar(out=rst[:, lo:c1], in0=bank(5 + bidx)[:, 0:c1 - lo], scalar1=1.0 / D, scalar2=EPS,
                                                                                op0=ALU.mult, op1=ALU.add),
                      reads=[KPS(5 + bidx)], writes=ck("acc", 1, lo, c1))
                rsqrt_ip(rst[:, lo:c1], ck("acc", 1, lo, c1))
            if lazy:
                rstP = Sb[:].bitcast(F32)
                S.add("act", lambda e: e.copy(out=rstP[:, 0:NE - OWN0], in_=rst[:, OWN0:NE]), reads=ck("acc", 1, OWN0, NE), writes=["Sb"])
                rst_use, rkeys = None, ["Sb"]

            def norm_ops():
                for j, (dst, dkeys) in enumerate(dst_list):
                    S.add("dve", lambda e, j=j, dst=dst: e.scalar_tensor_tensor(out=dst[:, OWN0:NE], in0=dst[:, OWN0:NE], scalar=cvec[:, gcol0 + j:gcol0 + j + 1],
                                                                                in1=(rstP[:, 0:NE - OWN0] if lazy else rst[:, OWN0:NE]), op0=ALU.mult, op1=ALU.mult),
                          reads=dkeys + rkeys + ["cvec"], writes=dkeys)
                    yield 1
            if lazy:
                return norm_ops()
            for _ in norm_ops():
                pass

        jobs = []
        for j in range(16):
            def epi(c0, c1, pap, pk, j=j):
                if c0 == 0:
                    S.add("act", lambda e: e.activation(out=szs[:, j, :], in_=pap[:, 0:16], func=AF.Silu), reads=[pk], writes=[("szs", j)])
                lo = max(c0, OWN0)
                bidx = [b[0] for b in BLKS].index(c0)
                ka = ck("acc", 0, lo, c1)
                S.add("act", lambda e: e.activation(out=acc[0][:, lo:c1], in_=pap[:, lo - c0:c1 - c0], func=AF.Silu), reads=[pk], writes=ka)
                pk2 = ck("post", j, lo, c1)
                S.add("dve", lambda e: e.tensor_tensor(out=post[j][:, lo:c1], in0=post[j][:, lo:c1], in1=acc[0][:, lo:c1], op=ALU.mult),
                      reads=pk2 + ka, writes=pk2)
                sumsq_block(post[j][:, lo:c1], bidx, c1 - lo, j == 0, j == 15, pk2)
            jobs.append((C_Z + 128 * j, BLKS, epi, None))
        run_jobs(jobs)
        fn_gen = finish_norm([(post[j], ck("post", j, OWN0, NE)) for j in range(16)], 120, lazy=True)
        def chain2(g1, g2):
            for x in g1:
                yield x
            if g2 is not None:
                for x in g2:
                    yield x
        bg[0] = chain2(fn_gen, bg[0])

        jobs = []
        for f in range(16):
            def cwsc(i, f=f):
                return cvec[:, 136 + 16 * i + f:137 + 16 * i + f]

            def epi_c(c0, c1, pap, pk):
                S.add("act", lambda e: e.copy(out=raw[0][:, 3 + c0:3 + c1], in_=pap), reads=[pk], writes=ck("raw", 0, 3 + c0, 3 + c1))

            def epi_h(c0, c1, pap, pk, f=f):
                S.add("dve", lambda e: e.tensor_tensor(out=raw[1][:, 3 + c0:3 + c1], in0=pap, in1=raw[0][:, 3 + c0:3 + c1], op=ALU.mult),
                      reads=[pk] + ck("raw", 0, 3 + c0, 3 + c1), writes=ck("raw", 1, 3 + c0, 3 + c1))
                if c0 == 0:
                    S.add("act", lambda e: e.copy(out=svraw[:, f, :, 2], in_=raw[1][:, 3:19]), reads=ck("raw", 1, 3, 19), writes=["svraw"])

            def fin_h(f=f, cwsc=cwsc):
                lo, hi = HALO0, NE
                ka = ck("acc", 0, lo, hi)
                S.add("act", lambda e: e.activation(out=acc[0][:, lo:hi], in_=raw[1][:, 3 + lo:3 + hi], func=AF.Identity, scale=cwsc(2)),
                      reads=ck("raw", 1, 3 + lo, 3 + hi) + ["cvec"], writes=ka)
                for i in (1, 0):
                    S.add("dve", lambda e, i=i: e.scalar_tensor_tensor(out=acc[0][:, lo:hi], in0=raw[1][:, 1 + i + lo:1 + i + hi], scalar=cwsc(i),
                                                                       in1=acc[0][:, lo:hi], op0=ALU.mult, op1=ALU.add),
                          reads=ck("raw", 1, 1 + i + lo, 1 + i + hi) + ["cvec"] + ka, writes=ka)
                S.add("dve", lambda e: e.tensor_copy(out=vlast[:, f, :], in_=raw[1][:, NE + 1:NE + 3]), reads=ck("raw", 1, NE + 1, NE + 3),
                      writes=[("vlast", f)])
                sv = svraw[:, f, :, :].rearrange("p b c -> p (b c)")
                S.add("act", lambda e: e.activation(out=svacc[:, 2:48], in_=sv[:, 2:48], func=AF.Identity, scale=cwsc(2)),
                      reads=["svraw", "cvec"], writes=["svacc"])
                for i in (1, 0):
                    S.add("dve", lambda e, i=i: e.scalar_tensor_tensor(out=svacc[:, 2:48], in0=sv[:, i:i + 46], scalar=cwsc(i), in1=svacc[:, 2:48],
                                                                       op0=ALU.mult, op1=ALU.add),
                          reads=["svraw", "cvec", "svacc"], writes=["svacc"])

            def epi_b(c0, c1, pap, pk, f=f):
                ka = ck("acc", 0, c0, c1)
                S.add("dve", lambda e: e.tensor_tensor(out=acc[0][:, c0:c1], in0=pap, in1=acc[0][:, c0:c1], op=ALU.mult), reads=[pk] + ka, writes=ka)
                if c0 == 0:
                    S.add("dve", lambda e: e.tensor_tensor(out=tsb[:, f, :], in0=pap[:, 0:16], in1=svacc[:].rearrange("p (b c) -> p b c", c=3)[:, :, 2],
                                                           op=ALU.mult),
                          reads=[pk, "svacc"], writes=[("tsbf", f)])

            def epi_z(c0, c1, pap, pk, f=f):
                ka1 = ck("acc", 1, c0, c1)
                S.add("act", lambda e: e.activation(out=acc[1][:, c0:c1], in_=pap, func=AF.Silu), reads=[pk], writes=ka1)
                yk = ck("ysc", f, c0, c1)
                S.add("dve", lambda e: e.tensor_tensor(out=ysc[f][:, c0:c1], in0=acc[0][:, c0:c1], in1=acc[1][:, c0:c1], op=ALU.mult),
                      reads=ck("acc", 0, c0, c1) + ka1, writes=yk)
                if c0 == 0:
                    S.add("dve", lambda e: e.tensor_tensor(out=yg_s[:, 1, f, :], in0=tsb[:, f, :], in1=acc[1][:, 0:16], op=ALU.mult),
                          reads=[("tsbf", f)] + ka1, writes=[("yg1", f)])
                lo = max(c0, OWN0)
                bidx = [b[0] for b in BLKS].index(c0)
                sumsq_block(ysc[f][:, lo:c1], bidx, c1 - lo, f == 0, f == 15, yk)
            blks = BLKS
            jobs.append((C_CSC + 128 * f, blks, epi_c, None))
            jobs.append((C_HSC + 128 * f, blks, epi_h, fin_h))
            jobs.append((C_BSC + 128 * f, blks, epi_b, None))
            jobs.append((C_ZSC + 128 * f, blks, epi_z, None))
        run_jobs(jobs)
        wo_pre = {}
        for dc in range(2):
            wk = [("W", 4 * dc + q) for q in range(4)]
            wo_pre[dc] = S.dma("pool", Wout[dc][:], w_out_v[:, :, 256 * dc:256 * (dc + 1)], writes=wk)
        while bg[0] is not None:
            if next(bg[0], "done") == "done":
                bg[0] = None
        def oacc_alias(ti):
            k = ck("hnT", 0, 0, NE) + ck("hnTa", 0, 0, NE)
            if ti == 4:
                k = k + ck("raw", 0, 0, 1064) + ck("raw", 1, 0, 1064)
            return k
        tiles0 = [("o", 128, xs[1040 + 128 * i:1040 + 128 * (i + 1), :], y_own[128 * i:128 * (i + 1), :], i) for i in range(4)] + \
                 [("s", 16, xsm, y_smp, None)]
        for ti, (kind, R, usrc, ydst, i) in enumerate(tiles0):
            S.dma("sp", out_acc[0:R, ti, :], usrc, writes=[("oacc", ti)] + oacc_alias(ti))
        finish_norm([(ysc[f], ck("ysc", f, OWN0, NE)) for f in range(16)], 184)
        def late_p7():
            S.add("dve", lambda e: e.tensor_copy(out=junk[:, 2:3], in_=yg_s[:, 1, 0, 0:1]), reads=[("yg1", f) for f in range(16)], writes=["yg1"])
            sample_norm(1, 184, "yg1")
            sample_finish()
            S.add("dve", lambda e: e.tensor_tensor(out=yg_s[:, 0, :, :], in0=ysT[:], in1=szs[:], op=ALU.mult),
                  reads=["ysT"] + [("szs", j) for j in range(16)], writes=["yg0"])
            sample_norm(0, 120, "yg0")
            S.add("dve", lambda e: e.tensor_copy(out=junk[:, 3:4], in_=vlast[:, 0, 0:1]), reads=[("vlast", f) for f in range(16)], writes=["csc_fin_src"])
            S.add("dve", lambda e: e.tensor_copy(out=junk[:, 4:5], in_=svraw[:, 0, 0, 0:1]), reads=["svraw"], writes=["csc_smp_src"])

            stg2 = Sb[:].bitcast(F32)
            for (src_fn, R, dst, okey) in ((lambda f: vlast[:, f, :], 2, csc_fin, "csc_fin"), (lambda f: svraw[:, f, :, 2], 16, csc_smp[:, 1, :], "csc_smp")):
                for h0 in (0, 8):
                    for j in range(8):
                        bi = 6 + j // 4
                        S.add("pe", lambda e, j=j, bi=bi, h0=h0, src_fn=src_fn, R=R: e.transpose(out=bank(bi)[0:R, 128 * (j % 4):128 * (j % 4 + 1)],
                                                                                                 in_=src_fn(h0 + j), identity=identf),
                              reads=["cm", okey + "_src"], writes=[KPS(bi)])
                    for q in range(2):
                        S.add("act", lambda e, q=q, R=R: e.copy(out=stg2[0:R, 512 * q:512 * (q + 1)], in_=bank(6 + q)[0:R, :]), reads=[KPS(6 + q)], writes=["stg2", "Sb"])
                    S.dma("sp", dst[:, 128 * h0:128 * (h0 + 8)], stg2[0:R, 0:1024], reads=["stg2"], writes=[("o", okey, h0)])
                    out_keys.append(("o", okey, h0))

        gfbc = Sf
        S.dma("sp", gfbc[:], fnormw_bc, writes=["Sf", ("Bc", 0), ("Bc", 1)])
        jrow = St[0].rearrange("p j n -> p (j n)")

        def junk_row(R):
            return jrow[0:R, :]
        wo_ctr = [0]
        for hp in range(2):
            tiles = [("o", 128, xs[1040 + 128 * i:1040 + 128 * (i + 1), :], y_own[128 * i:128 * (i + 1), :], i) for i in range(4 * hp, 4 * hp + 4)] + \
                    ([("s", 16, xsm, y_smp, None)] if hp == 0 else [])
            for ti, (kind, R, usrc, ydst, i) in enumerate(tiles):
                if hp == 1:
                    S.dma("sp", out_acc[0:R, ti, :], usrc, writes=[("oacc", ti)] + oacc_alias(ti))
            for dc in range(8):
                s = wo_ctr[0] % 2
                wo_ctr[0] += 1
                wk = [("W", 4 * s + q) for q in range(4)]
                if not (hp == 0 and dc in wo_pre):
                    S.dma("pool", Wout[s][:], w_out_v[:, :, 256 * dc:256 * (dc + 1)], writes=wk)
                for ti, (kind, R, usrc, ydst, i) in enumerate(tiles):
                    if kind == "s" and dc == 0:
                        late_p7()
                    bi = bctr[0] % 6
                    bctr[0] += 1
                    pap = bank(bi)[0:R, 0:256]
                    for kt in range(32):
                        if kind == "s":
                            lhsT = ysn[:, kt, :]
                            rk = [("ysn", kt // 16)]
                        else:
                            src = post[kt] if kt < 16 else ysc[kt - 16]
                            c0 = OWN0 + 128 * i
                            lhsT = src[:, c0:c0 + 128]
                            rk = ck("post", kt, c0, c0 + 128) if kt < 16 else ck("ysc", kt - 16, c0, c0 + 128)
                        S.add("pe", lambda e, kt=kt, lhsT=lhsT, pap=pap, s=s: e.matmul(pap, lhsT=lhsT, rhs=Wout[s][:, kt, :], start=(kt == 0), stop=(kt == 31)),
                              reads=wk + rk, writes=[KPS(bi)])
                    oa = out_acc[0:R, ti, 256 * dc:256 * (dc + 1)]
                    S.add("dve", lambda e, oa=oa, pap=pap: e.tensor_tensor(out=oa, in0=oa, in1=pap, op=ALU.add), reads=[KPS(bi), ("oacc", ti)], writes=[("oacc", ti)])
            for ti, (kind, R, usrc, ydst, i) in enumerate(tiles):
                b = ti % 2
                sscol = ss[0:R, 2 + b:3 + b]
                S.add("dve", lambda e, sscol=sscol: e.memset(sscol, 0.0), writes=[("ss", 2 + b)])
                S.add("act", lambda e, ti=ti, R=R, sscol=sscol: e.activation(out=junk_row(R), in_=out_acc[0:R, ti, :], func=AF.Square, accum_out=sscol),
                      reads=[("oacc", ti)], writes=[("ss", 2 + b), "jrow"] + stkeys(0))
                S.add("dve", lambda e, sscol=sscol: e.tensor_scalar(out=sscol, in0=sscol, scalar1=1.0 / D, scalar2=EPS, op0=ALU.mult, op1=ALU.add),
                      reads=[("ss", 2 + b)], writes=[("ss", 2 + b)])
                rsqrt_ip(sscol, [("ss", 2 + b)])
                S.add("dve", lambda e, ti=ti, R=R, sscol=sscol: e.scalar_tensor_tensor(out=out_acc[0:R, ti, :], in0=out_acc[0:R, ti, :], scalar=sscol,
                                                                                        in1=gfbc[0:R, :], op0=ALU.mult, op1=ALU.mult),
                      reads=[("oacc", ti), ("ss", 2 + b), "Sf"], writes=[("oacc", ti)])
                ok = ("o", "y", hp, ti)
                S.dma("sp", ydst, out_acc[0:R, ti, :], reads=[("oacc", ti)], writes=[ok])
                out_keys.append(ok)
        S.flush(final_keys=out_keys)
    return nc, dbg_outs


_CACHE = {}


def _consts():
    ident = np.eye(128, dtype=np.float32)
    triL = np.triu(np.ones((128, 128), np.float32))
    triU = np.tril(np.ones((128, 128), np.float32), -1)
    ones = np.ones((128, 128), np.float32)
    return np.concatenate([ident, triL, triU, ones], axis=1)


def kernel(x_prompt, x_sample, state_ssm, state_ssd_conv, state_short_conv, meta_tokens, norm_w, w_in,
           conv_ssd_w, conv_ssd_b, dt_bias, a_log, d_skip, ssd_norm_w, conv_sc_w, sc_norm_w, w_out, final_norm_w,
           _debug=False):
    f = lambda a: np.ascontiguousarray(np.asarray(a, dtype=np.float32))
    x_prompt, x_sample, state_ssm, state_ssd_conv, state_short_conv = map(f, (x_prompt, x_sample, state_ssm, state_ssd_conv, state_short_conv))
    meta_tokens, norm_w, w_in, conv_ssd_w, conv_ssd_b = map(f, (meta_tokens, norm_w, w_in, conv_ssd_w, conv_ssd_b))
    dt_bias, a_log, d_skip, ssd_norm_w, conv_sc_w, sc_norm_w, w_out, final_norm_w = map(
        f, (dt_bias, a_log, d_skip, ssd_norm_w, conv_sc_w, sc_norm_w, w_out, final_norm_w))
    key = ("nc", bool(_debug))
    if key not in _CACHE:
        _CACHE[key] = build(debug=_debug)
    nc, dbg_outs = _CACHE[key]

    def colmajor(v):
        return v.reshape(-1, 128).T

    cvec = np.zeros((128, 216), np.float32)
    for i in range(4):
        cvec[:, 24 * i:24 * (i + 1)] = colmajor(conv_ssd_w[0, i])
    cvec[:, 96:120] = colmajor(conv_ssd_b[0])
    cvec[:, 120:136] = colmajor(ssd_norm_w[0])
    for i in range(3):
        cvec[:, 136 + 16 * i:152 + 16 * i] = colmajor(conv_sc_w[0, i])
    cvec[:, 184:200] = colmajor(sc_norm_w[0])
    cvec[:, 200:216] = colmajor(np.repeat(d_skip[0], 64))
    cmat = _consts()
    normw_bc = np.ascontiguousarray(np.broadcast_to(norm_w[0][None, :], (128, D)))
    fnormw_bc = np.ascontiguousarray(np.broadcast_to(final_norm_w[None, :], (128, D)))
    w_in0, w_out0 = w_in[0], w_out[0]
    in_maps = []
    for c in range(8):
        b, half = c // 2, c % 2
        if half == 0:
            stream = np.concatenate([np.zeros((1024, D), np.float32), meta_tokens, x_prompt[b, 0:1024]], axis=0)
        else:
            stream = np.concatenate([meta_tokens, x_prompt[b]], axis=0)
        hvec = np.zeros((128, 128), np.float32)
        hvec[:, 0:32] = dt_bias[0][None, :]
        hvec[:, 32:64] = a_log[0][None, :]
        hvec[:, 64:72] = float(half)
        hvec[:, 72] = 1.0
        sl = slice(16 * c, 16 * (c + 1))
        in_maps.append(dict(
            xs=np.ascontiguousarray(stream), xsm=np.ascontiguousarray(x_sample[sl, 0, :]),
            sssm=np.ascontiguousarray(state_ssm[0, sl].reshape(16, 2048, 128)),
            scv=np.ascontiguousarray(state_ssd_conv[0, sl]), ssc=np.ascontiguousarray(state_short_conv[0, sl]),
            w_in=w_in0, w_out=w_out0, normw_bc=normw_bc, fnormw_bc=fnormw_bc, cvec=cvec, hvec=hvec, cmat=cmat))
    res = run_bass_kernel_spmd(nc, in_maps, core_ids=list(range(8)))
    R = res.results
    y_prompt = np.empty((4, 2048, D), np.float32)
    y_sample = np.empty((128, 1, D), np.float32)
    ssm_p = np.empty((1, 4, 32, 64, 128), np.float32)
    cssd_p = np.empty((1, 4, 3, 3072), np.float32)
    csc_p = np.empty((1, 4, 2, 2048), np.float32)
    ssm_s = np.empty((1, 128, 32, 64, 128), np.float32)
    cssd_s = np.empty((1, 128, 3, 3072), np.float32)
    csc_s = np.empty((1, 128, 2, 2048), np.float32)
    for c in range(8):
        b, half = c // 2, c % 2
        r = R[c]
        y_prompt[b, 1024 * half:1024 * (half + 1)] = r["y_own"]
        sl = slice(16 * c, 16 * (c + 1))
        y_sample[sl, 0, :] = r["y_smp"]
        ssm_s[0, sl] = r["ssm_smp"].reshape(16, 32, 64, 128)
        cssd_s[0, sl] = r["cssd_smp"]
        csc_s[0, sl] = r["csc_smp"]
        if half == 1:
            ssm_p[0, b] = r["ssm_fin"].reshape(32, 64, 128)
            cssd_p[0, b] = r["cssd_fin"]
            csc_p[0, b] = r["csc_fin"]
    if _debug:
        kernel.last_debug = [{k: R[c]["dbg_" + k] for k in dbg_outs} for c in range(8)]
    return (y_prompt, y_sample, ssm_p, cssd_p, csc_p, ssm_s, cssd_s, csc_s)
```
